# Optimizing a Trainium2 kernel written in Bass

```python
import math
import jax, jax.numpy as jnp
from jax import lax
import numpy as np

D_MODEL = 1024
BATCH = 8
SEQ = 2048
DEPTH = 1

MIX_WIDTH = D_MODEL
DIFF_HEADS = 4
DIFF_QK_DIM = 64
DIFF_V_DIM = 2 * DIFF_QK_DIM
DIFF_WIDTH = DIFF_HEADS * DIFF_V_DIM
Q_BLOCK = 128
GLA_HEADS = 4
GLA_WIDTH = MIX_WIDTH - DIFF_WIDTH
GLA_V_DIM = GLA_WIDTH // GLA_HEADS
GLA_K_DIM = GLA_V_DIM // 2
GLA_GATE_RANK = 16
GLA_TAU = 16.0
GLA_CHUNK = 64
D_FF = 2816
CONV_WIDTH = 3
EPS = 1e-6

IN_SPLIT_SIZES = (
    DIFF_HEADS * 2 * DIFF_QK_DIM,
    DIFF_HEADS * 2 * DIFF_QK_DIM,
    DIFF_WIDTH,
    GLA_HEADS * GLA_K_DIM,
    GLA_HEADS * GLA_K_DIM,
    GLA_WIDTH,
    GLA_GATE_RANK,
    GLA_WIDTH,
)
IN_COLS = 3088

kernel_name = "hymba_diffattn_gla_convffn"


def rmsnorm(x, w):
    xf = x.astype(jnp.float32)
    y = xf * lax.rsqrt(jnp.mean(xf * xf, axis=-1, keepdims=True) + EPS)
    return (y * w.astype(jnp.float32)).astype(x.dtype)


def diff_attention(q, k, v, lam, subln_w, lambda_init):
    B, S, H, _, dq = q.shape
    dv = v.shape[-1]
    nb = S // Q_BLOCK
    scale = dq ** -0.5
    q_blocks = q.reshape(B, nb, Q_BLOCK, H, 2, dq).transpose(1, 0, 2, 3, 4, 5)
    kpos = jnp.arange(S)

    def one_block(args):
        q_blk, i = args
        s = jnp.einsum('bqhcd,bkhcd->bhcqk', q_blk, k).astype(jnp.float32) * scale
        qpos = i * Q_BLOCK + jnp.arange(Q_BLOCK)
        mask = kpos[None, :] <= qpos[:, None]
        s = jnp.where(mask, s, -jnp.inf)
        p = jax.nn.softmax(s, axis=-1)
        a = p[:, :, 0] - lam * p[:, :, 1]
        return jnp.einsum('bhqk,bkhe->bqhe', a.astype(v.dtype), v)

    o = lax.map(one_block, (q_blocks, jnp.arange(nb)))
    o = o.transpose(1, 0, 2, 3, 4).reshape(B, S, H, dv)
    o = rmsnorm(o, subln_w) * (1.0 - lambda_init)
    return o.reshape(B, S, H * dv)


def gla_chunked(q, k, v, log_g):
    B, S, H, dk = q.shape
    dv = v.shape[-1]
    C = GLA_CHUNK
    n = S // C

    def to_chunks(t):
        return t.reshape(B, n, C, H, t.shape[-1]).transpose(0, 3, 1, 2, 4)

    q, k, v, log_g = to_chunks(q), to_chunks(k), to_chunks(v), to_chunks(log_g)
    b = jnp.cumsum(log_g, axis=3)
    b_last = b[:, :, :, -1:, :]
    q_dec = q * jnp.exp(b)
    k_dec = k * jnp.exp(-b)
    k_to_end = k * jnp.exp(b_last - b)

    causal = jnp.tril(jnp.ones((C, C), dtype=bool))
    a = jnp.einsum('bhnik,bhnjk->bhnij', q_dec, k_dec)
    a = jnp.where(causal, a, 0.0)
    o_intra = jnp.einsum('bhnij,bhnjv->bhniv', a, v)

    chunk_state = jnp.einsum('bhnjk,bhnjv->bhnkv', k_to_end, v)
    decay = jnp.exp(b_last[:, :, :, 0, :])

    def step(state, inp):
        q_d, dec, cs = inp
        o = jnp.einsum('bhik,bhkv->bhiv', q_d, state)
        state = dec[..., None] * state + cs
        return state, o

    xs = (q_dec.transpose(2, 0, 1, 3, 4), decay.transpose(2, 0, 1, 3),
          chunk_state.transpose(2, 0, 1, 3, 4))
    _, o_inter = lax.scan(step, jnp.zeros((B, H, dk, dv), jnp.float32), xs)
    o = o_intra + o_inter.transpose(1, 2, 0, 3, 4)
    return o.transpose(0, 2, 3, 1, 4).reshape(B, S, H, dv)


def causal_depthwise_conv(u, w, b):
    C = u.shape[-1]
    y = lax.conv_general_dilated(
        u, w[:, None, :].astype(u.dtype), window_strides=(1,),
        padding=[(CONV_WIDTH - 1, 0)], dimension_numbers=('NWC', 'WIO', 'NWC'),
        feature_group_count=C)
    return y + b.astype(u.dtype)


def setup_inputs(seed: int = 0) -> dict:
    key = jax.random.key(seed)
    ks = jax.random.split(key, 20)
    f32 = jnp.float32
    nrm = lambda k, shape, s: jax.random.normal(k, shape, f32) * s
    L, D = DEPTH, D_MODEL
    return {
        "x": jax.random.normal(ks[0], (BATCH, SEQ, D), f32),
        "ln1_w": 1.0 + nrm(ks[1], (L, D), 0.02),
        "w_in": nrm(ks[2], (L, D, IN_COLS), D ** -0.5),
        "diff_lq1": nrm(ks[3], (L, DIFF_QK_DIM), 0.1),
        "diff_lk1": nrm(ks[4], (L, DIFF_QK_DIM), 0.1),
        "diff_lq2": nrm(ks[5], (L, DIFF_QK_DIM), 0.1),
        "diff_lk2": nrm(ks[6], (L, DIFF_QK_DIM), 0.1),
        "diff_subln_w": 1.0 + nrm(ks[7], (L, DIFF_V_DIM), 0.02),
        "gla_wg2": nrm(ks[8], (L, GLA_GATE_RANK, GLA_HEADS * GLA_K_DIM), GLA_GATE_RANK ** -0.5),
        "gla_bg": nrm(ks[9], (L, GLA_HEADS * GLA_K_DIM), 0.1),
        "gla_norm_w": 1.0 + nrm(ks[10], (L, GLA_V_DIM), 0.02),
        "w_out": nrm(ks[11], (L, MIX_WIDTH, D), MIX_WIDTH ** -0.5),
        "ln2_w": 1.0 + nrm(ks[12], (L, D), 0.02),
        "w_up": nrm(ks[13], (L, D, 2 * D_FF), D ** -0.5),
        "conv_w": nrm(ks[14], (L, CONV_WIDTH, 2 * D_FF), CONV_WIDTH ** -0.5),
        "conv_b": nrm(ks[15], (L, 2 * D_FF), 0.02),
        "w_down": nrm(ks[16], (L, D_FF, D), D_FF ** -0.5),
        "lnf_w": 1.0 + nrm(ks[17], (D,), 0.02),
    }


def reference(x, ln1_w, w_in, diff_lq1, diff_lk1, diff_lq2, diff_lk2, diff_subln_w,
              gla_wg2, gla_bg, gla_norm_w, w_out, ln2_w, w_up, conv_w, conv_b,
              w_down, lnf_w):
    B, S, _ = x.shape
    split_idx = [int(i) for i in np.cumsum(IN_SPLIT_SIZES)[:-1]]
    for l in range(DEPTH):
        lambda_init = 0.8 - 0.6 * math.exp(-0.3 * l)
        h = rmsnorm(x, ln1_w[l])
        proj = h @ w_in[l]
        d_q, d_k, d_v, g_q, g_k, g_v, g_r, g_o = jnp.split(proj, split_idx, axis=-1)

        lam = (jnp.exp(jnp.sum(diff_lq1[l].astype(jnp.float32) * diff_lk1[l].astype(jnp.float32)))
               - jnp.exp(jnp.sum(diff_lq2[l].astype(jnp.float32) * diff_lk2[l].astype(jnp.float32)))
               + lambda_init)
        o_diff = diff_attention(
            d_q.reshape(B, S, DIFF_HEADS, 2, DIFF_QK_DIM),
            d_k.reshape(B, S, DIFF_HEADS, 2, DIFF_QK_DIM),
            d_v.reshape(B, S, DIFF_HEADS, DIFF_V_DIM),
            lam, diff_subln_w[l], lambda_init)

        f32 = jnp.float32
        gq = g_q.astype(f32).reshape(B, S, GLA_HEADS, GLA_K_DIM) * GLA_K_DIM ** -0.5
        gk = g_k.astype(f32).reshape(B, S, GLA_HEADS, GLA_K_DIM)
        gv = g_v.astype(f32).reshape(B, S, GLA_HEADS, GLA_V_DIM)
        gate_logits = g_r.astype(f32) @ gla_wg2[l].astype(f32) + gla_bg[l].astype(f32)
        log_g = (jax.nn.log_sigmoid(gate_logits) / GLA_TAU).reshape(B, S, GLA_HEADS, GLA_K_DIM)
        o_gla = gla_chunked(gq, gk, gv, log_g).astype(x.dtype)
        o_gla = rmsnorm(o_gla, gla_norm_w[l]).reshape(B, S, GLA_WIDTH) * jax.nn.silu(g_o)

        mix = jnp.concatenate([o_diff, o_gla], axis=-1) @ w_out[l]
        x = x + mix

        h = rmsnorm(x, ln2_w[l])
        u = causal_depthwise_conv(h @ w_up[l], conv_w[l], conv_b[l])
        gate, val = jnp.split(u, 2, axis=-1)
        x = x + (jax.nn.silu(gate) * val) @ w_down[l]
    return rmsnorm(x, lnf_w)
```

```python
import numpy as np
import concourse.bass as bass
import concourse.mybir as mybir
from concourse.bass_utils import run_bass_kernel_spmd

F32 = mybir.dt.float32
BF16 = mybir.dt.bfloat16
AF = mybir.ActivationFunctionType
ALU = mybir.AluOpType

S = 2048
D = 1024
KC = 8
DFF = 2816
NFF = 22
EPS = 1e-6
SAME_ENGINE_SYNC = True

C_LN1, C_LN2, C_LNF, C_BG, C_SUB, C_GNW, C_CB, C_CW = 0, 8, 16, 24, 26, 27, 28, 72
C_LQ1, C_LK1, C_LQ2, C_LK2 = 204, 268, 332, 396
NCST = 460


class Op:
    __slots__ = ("id", "eng", "fn", "deps", "dma", "stream", "signal", "tick", "waits")

    def __init__(self, id, eng, fn, dma):
        self.id, self.eng, self.fn, self.dma = id, eng, fn, dma
        self.stream = ("dma:" + dma) if dma else eng
        self.deps = set()
        self.signal = False
        self.tick = 0
        self.waits = {}


class Prog:
    ENGS = ("pe", "act", "dve", "pool", "sp")

    def __init__(self):
        self.ops = []
        self.state = {}
        self.desc = {}
        self.last = {}
        self.barrier_op = None

    def _related(self, t):
        out = []
        for i in range(1, len(t) + 1):
            pre = t[:i]
            if pre in self.state:
                out.append(pre)
        for d in self.desc.get(t, ()):
            if d in self.state:
                out.append(d)
        return out

    def _ensure(self, t):
        if t not in self.state:
            self.state[t] = [None, {}]
            for i in range(1, len(t)):
                self.desc.setdefault(t[:i], set()).add(t)

    def op(self, eng, fn, reads=(), writes=(), dma=None):
        o = Op(len(self.ops), eng, fn, dma)
        deps = o.deps
        for t in reads:
            for r in self._related(t):
                w = self.state[r][0]
                if w is not None:
                    deps.add(w)
        for t in writes:
            for r in self._related(t):
                st = self.state[r]
                if st[0] is not None:
                    deps.add(st[0])
                deps.update(st[1].values())
        if self.barrier_op is not None:
            deps.add(self.barrier_op)
        for t in reads:
            self._ensure(t)
            self.state[t][1][o.stream] = o.id
        for t in writes:
            for d in list(self.desc.get(t, ())):
                self.state.pop(d, None)
            self.desc.pop(t, None)
            self._ensure(t)
            self.state[t] = [o.id, {}]
        deps.discard(o.id)
        self.ops.append(o)
        self.last[o.stream] = o.id
        return o

    def barrier(self, exempt=()):
        o = self.op("pool", lambda e: e.nop())
        for s, i in self.last.items():
            if s not in exempt and i != o.id:
                o.deps.add(i)
        self.barrier_op = o.id
        return o

    def finalize(self):
        ops = self.ops
        for o in ops:
            for d in list(o.deps):
                do = ops[d]
                if do.stream == o.stream and not do.dma:
                    if o.eng == "pe" or o.eng == "sp" or not SAME_ENGINE_SYNC:
                        o.deps.discard(d)
                        continue
                do.signal = True
        cnt = {}
        for o in ops:
            if o.dma:
                o.signal = True
            if o.signal:
                inc = 16 if o.dma else 1
                cnt[o.stream] = cnt.get(o.stream, 0) + inc
                o.tick = cnt[o.stream]
        for s, v in cnt.items():
            assert v < 30000, (s, v)
        self.streams = sorted(cnt.keys())
        clock = {e: {} for e in self.ENGS}
        snap = {}
        for o in ops:
            need = {}
            for d in o.deps:
                do = ops[d]
                if need.get(do.stream, 0) < do.tick:
                    need[do.stream] = do.tick
            ck = clock[o.eng]
            for d in o.deps:
                do = ops[d]
                if ck.get(do.stream, 0) >= do.tick:
                    for k, v in snap[d].items():
                        if ck.get(k, 0) < v:
                            ck[k] = v
            waits = {}
            for k, v in need.items():
                if ck.get(k, 0) < v:
                    waits[k] = v
            for k, v in waits.items():
                ck[k] = v
            for d in o.deps:
                for k, v in snap[d].items():
                    if ck.get(k, 0) < v:
                        ck[k] = v
            o.waits = waits
            if o.signal:
                sn = dict(ck)
                sn[o.stream] = o.tick
                snap[o.id] = sn

    def emit(self, nc, block, sems):
        by_eng = {e: [o for o in self.ops if o.eng == e] for e in self.ENGS}

        def run(eng_name, e):
            for o in by_eng[eng_name]:
                for k, v in o.waits.items():
                    e.wait_ge(sems[k], v)
                inst = o.fn(e)
                if o.signal:
                    inst.then_inc(sems[o.stream], 16 if o.dma else 1)

        @block.tensor
        def _(e):
            run("pe", e)

        @block.scalar
        def _(e):
            run("act", e)

        @block.vector
        def _(e):
            run("dve", e)

        @block.gpsimd
        def _(e):
            run("pool", e)

        @block.sync
        def _(e):
            run("sp", e)


class Arena:
    def __init__(self, big, nbytes):
        self.big = big
        self.free = [(0, nbytes)]
        self.used = {}

    def alloc(self, name, nbytes, top=False):
        nbytes = (nbytes + 63) // 64 * 64
        if top:
            for i in range(len(self.free) - 1, -1, -1):
                o, n = self.free[i]
                if n >= nbytes:
                    self.free[i] = (o, n - nbytes)
                    self.used[name] = (o + n - nbytes, nbytes)
                    return o + n - nbytes
        for i, (o, n) in enumerate(self.free):
            if n >= nbytes:
                self.free[i] = (o + nbytes, n - nbytes)
                self.used[name] = (o, nbytes)
                return o
        raise RuntimeError(f"arena out of space for {name} {nbytes}: {self.free}")

    def release(self, *names):
        for name in names:
            o, n = self.used.pop(name)
            self.free.append((o, n))
        self.free.sort()
        m = []
        for o, n in self.free:
            if n == 0:
                continue
            if m and m[-1][0] + m[-1][1] == o:
                m[-1] = (m[-1][0], m[-1][1] + n)
            else:
                m.append((o, n))
        self.free = m

    def view(self, name, dtype, shape, parts=128, top=False):
        esz = 4 if dtype == F32 else 2
        n = int(np.prod(shape))
        o = self.alloc(name, n * esz, top=top)
        ap = self.big[0:parts, o // 4:(o + n * esz) // 4]
        if dtype != F32:
            ap = ap.bitcast(dtype)
        if len(shape) == 2:
            ap = ap.rearrange("p (a b) -> p a b", a=shape[0])
        elif len(shape) == 3:
            ap = ap.rearrange("p (a b c) -> p a b c", a=shape[0], b=shape[1])
        return ap


def build_nc(stop_after=None, debug=()):
    nc = bass.Bass("TRN2", target_bir_lowering=False)
    xT = nc.dram_tensor("xT", [128, KC, S], F32, kind="ExternalInput").ap()
    w_in = nc.dram_tensor("w_in", [D, 3088], F32, kind="ExternalInput").ap()
    w_out = nc.dram_tensor("w_out", [D, D], F32, kind="ExternalInput").ap()
    w_up = nc.dram_tensor("w_up", [D, 2 * DFF], F32, kind="ExternalInput").ap()
    w_down = nc.dram_tensor("w_down", [DFF, D], F32, kind="ExternalInput").ap()
    wg2 = nc.dram_tensor("wg2", [16, 256], F32, kind="ExternalInput").ap()
    cst_d = nc.dram_tensor("cst", [128, NCST], F32, kind="ExternalInput").ap()
    outT = nc.dram_tensor("outT", [128, KC, S], F32, kind="ExternalOutput").ap()
    dbg_out = {}
    for name, shape, dt in debug:
        dbg_out[name] = nc.dram_tensor("dbg_" + name, list(shape), dt, kind="ExternalOutput").ap()

    p = Prog()
    ARENA_BYTES = 207 * 1024
    import contextlib
    es = contextlib.ExitStack()
    with es:
        big = es.enter_context(nc.sbuf_tensor("big", [128, ARENA_BYTES // 4], F32))
        ps = es.enter_context(nc.psum_tensor("ps", [128, 8, 512], F32))
        A = Arena(big, ARENA_BYTES)

        def bank(b):
            return ps[:, b, :]

        PS = lambda b: ("ps", b)

        cst = A.view("cst", F32, [NCST])
        ones_bf = A.view("ones", BF16, [128])
        ident = A.view("ident", BF16, [128])
        maskT = A.view("maskT", BF16, [128])
        mask2 = A.view("mask2", BF16, [128])
        misc = A.view("misc", F32, [16])
        wg2_sb = A.view("wg2", F32, [256])
        wslot_bytes = 8192
        wsl = [A.alloc(f"wslot{i}", wslot_bytes) for i in range(3)]
        wbase = wsl[0]
        assert wsl[1] == wbase + wslot_bytes and wsl[2] == wbase + 2 * wslot_bytes

        def wview(byte_off, shape, dtype=BF16):
            n = int(np.prod(shape))
            esz = 2
            o = wbase + byte_off
            ap = big[:, o // 4:(o + n * esz) // 4].bitcast(dtype)
            if len(shape) == 2:
                ap = ap.rearrange("p (a b) -> p a b", a=shape[0])
            elif len(shape) == 3:
                ap = ap.rearrange("p (a b c) -> p a b c", a=shape[0], b=shape[1])
            return ap

        p.op("sp", lambda e: e.dma_start(out=cst, in_=cst_d), writes=[("cst",)], dma="cst")
        p.op("sp", lambda e: e.dma_start(out=wg2_sb[0:16, :], in_=wg2), writes=[("wg2",)], dma="wg2")
        p.op("pool", lambda e: e.memset(ones_bf, 1.0), writes=[("ones",)])
        p.op("pool", lambda e: e.affine_select(out=maskT, in_=ones_bf, pattern=[[1, 128]], compare_op=ALU.is_ge,
                                               fill=0.0, base=0, channel_multiplier=-1),
             reads=[("ones",)], writes=[("maskT",)])
        p.op("pool", lambda e: e.affine_select(out=ident, in_=ones_bf, pattern=[[1, 128]], compare_op=ALU.is_equal,
                                               fill=0.0, base=0, channel_multiplier=-1),
             reads=[("ones",)], writes=[("ident",)])
        p.op("pool", lambda e: e.tensor_copy(out=mask2, in_=maskT), reads=[("maskT",)], writes=[("mask2",)])
        p.op("pool", lambda e: e.memset(mask2[0:64, 64:128], 0.0), writes=[("mask2",)])
        neg_lam = misc[:, 0:1]
        w8 = misc[:, 1:2]
        lt = A.view("lamtmp", F32, [2, 64])
        p.op("dve", lambda e: e.tensor_tensor(out=lt[:, 0, :], in0=cst[:, C_LQ1:C_LQ1 + 64], in1=cst[:, C_LK1:C_LK1 + 64], op=ALU.mult),
             reads=[("cst",)], writes=[("lt", 0)])
        p.op("dve", lambda e: e.tensor_tensor(out=lt[:, 1, :], in0=cst[:, C_LQ2:C_LQ2 + 64], in1=cst[:, C_LK2:C_LK2 + 64], op=ALU.mult),
             reads=[("cst",)], writes=[("lt", 1)])
        p.op("dve", lambda e: e.reduce_sum(out=misc[:, 2:4], in_=lt, axis=mybir.AxisListType.X), reads=[("lt",)], writes=[("misc", "s")])
        p.op("act", lambda e: e.activation(out=misc[:, 4:6], in_=misc[:, 2:4], func=AF.Exp), reads=[("misc", "s")], writes=[("misc", "e")])
        p.op("dve", lambda e: e.tensor_tensor(out=misc[:, 6:7], in0=misc[:, 5:6], in1=misc[:, 4:5], op=ALU.subtract),
             reads=[("misc", "e")], writes=[("misc", "d")])
        p.op("dve", lambda e: e.tensor_scalar(out=neg_lam, in0=misc[:, 6:7], scalar1=-0.2, scalar2=None, op0=ALU.add),
             reads=[("misc", "d")], writes=[("misc", "nl")])
        p.op("dve", lambda e: e.tensor_scalar(out=w8, in0=cst[:, C_SUB:C_SUB + 1], scalar1=0.8, scalar2=None, op0=ALU.mult),
             reads=[("cst",)], writes=[("misc", "w8")])

        def wdma(dst, src, tok, key):
            toks = [("w", 0), ("w", 1)] if tok is None else [tok]
            p.op("pool", lambda e: e.dma_start(out=dst, in_=src), writes=toks, dma=key)

        def w_in_cols(c0, c1):
            return w_in[:, c0:c1].rearrange("(kc p) n -> p kc n", p=128)

        def rstd_from_ss(ss_ps_ap, out_ap, n, reads, writes):
            p.op("act", lambda e: e.activation(out=out_ap, in_=ss_ps_ap, func=AF.Ln, scale=1.0 / n, bias=eps_ap),
                 reads=reads + [("misc", "eps")], writes=writes)
            p.op("act", lambda e: e.activation(out=out_ap, in_=out_ap, func=AF.Exp, scale=-0.5),
                 reads=writes, writes=writes)

        eps_ap = misc[:, 8:9]
        one_ap = misc[:, 9:10]
        p.op("pool", lambda e: e.memset(eps_ap, EPS), writes=[("misc", "eps")])
        p.op("pool", lambda e: e.memset(one_ap, 1.0), writes=[("misc", "one")])

        def dbg(name, ap, reads):
            if name in dbg_out:
                p.op("sp", lambda e: e.dma_start(out=dbg_out[name], in_=ap), reads=reads, writes=[("dbg", name)], dma="dbg_" + name)

        wA = [wview(i * wslot_bytes, [KC, 512]) for i in range(3)]
        w_r = A.view("w_r", BF16, [KC, 16])
        wdma(w_r, w_in_cols(2560, 2576), ("w_r",), "wr")
        wdma(wA[1], w_in_cols(2048, 2560), ("w", 1), "w1")
        wdma(wA[2], w_in_cols(1024, 1536), ("w", 2), "w2")
        wdma(wA[0], w_in_cols(1536, 2048), ("w", 0), "w0")
        gv_tok = A.view("gv_tok", BF16, [16, 512])
        grT = A.view("grT", F32, [S])
        v_tok = A.view("v_tok", BF16, [16, 512], top=True)
        psrot = [0]

        def tm_group(tkb, wtile, tok_w, dst, dst_name, evac, bank_base):
            b = bank_base + psrot[0] % 4
            psrot[0] += 1
            tsl = slice(tkb * 128, (tkb + 1) * 128)
            for kc in range(KC):
                p.op("pe", lambda e, kc=kc: e.matmul(bank(b), lhsT=hT[:, kc, tsl], rhs=wtile[:, kc, :], start=(kc == 0), stop=(kc == KC - 1)),
                     reads=[tok_w, ("hT", tkb // 4)], writes=[PS(b)])
            if evac == "act":
                p.op("act", lambda e: e.activation(out=dst[:, tkb, :], in_=bank(b), func=AF.Copy), reads=[PS(b)], writes=[(dst_name, tkb)])
            else:
                p.op("dve", lambda e: e.tensor_copy(out=dst[:, tkb, :], in_=bank(b)), reads=[PS(b)], writes=[(dst_name, tkb)])
        hT = A.view("hT", BF16, [KC, S])
        xs = [A.view(f"xs{i}", F32, [KC, 512]) for i in range(2)]
        sq1 = A.view("sq1", BF16, [KC, 512])
        rs1 = A.view("rs1", F32, [512])
        def a1_proj(tb):
            tsl = slice(tb * 512, (tb + 1) * 512)
            for kc in range(KC):
                p.op("pe", lambda e, kc=kc: e.matmul(bank(2)[0:16, :], lhsT=w_r[:, kc, 0:16], rhs=hT[:, kc, tsl], start=(kc == 0), stop=(kc == KC - 1)),
                     reads=[("w_r",), ("hT", tb)], writes=[PS(2)])
            p.op("act", lambda e: e.activation(out=grT[0:16, tb * 512:(tb + 1) * 512], in_=bank(2)[0:16, :], func=AF.Copy),
                 reads=[PS(2)], writes=[("grT", tb)])
            for i in range(4):
                tm_group(4 * tb + i, wA[1], ("w", 1), gv_tok, "gv_tok", "dve", 4)

        for tb in range(4):
            sl = tb % 2
            tsl = slice(tb * 512, (tb + 1) * 512)
            p.op("sp", lambda e, sl=sl, tsl=tsl: e.dma_start(out=xs[sl], in_=xT[:, :, tsl]), writes=[("xs", sl)], dma=f"xs{sl}")
            p.op("act", lambda e, sl=sl: e.activation(out=sq1, in_=xs[sl], func=AF.Square), reads=[("xs", sl)], writes=[("sq1",)])
            b = tb % 2
            for kc in range(KC):
                p.op("pe", lambda e, kc=kc, b=b: e.matmul(bank(b), lhsT=ones_bf, rhs=sq1[:, kc, :], start=(kc == 0), stop=(kc == KC - 1)),
                     reads=[("ones",), ("sq1",)], writes=[PS(b)])
            rstd_from_ss(bank(b), rs1, D, [PS(b)], [("rs1",)])
            for kc in range(KC):
                eng = "dve" if kc % 2 == 0 else "dve"
                p.op(eng, lambda e, kc=kc, sl=sl, tsl=tsl: e.scalar_tensor_tensor(
                    out=hT[:, kc, tsl], in0=xs[sl][:, kc, :], scalar=cst[:, C_LN1 + kc:C_LN1 + kc + 1], in1=rs1,
                    op0=ALU.mult, op1=ALU.mult),
                    reads=[("xs", sl), ("rs1",), ("cst",)], writes=[("hT", tb, kc)])
            if tb >= 1:
                a1_proj(tb - 1)
        a1_proj(3)
        dbg("hT", hT, [("hT",)])
        p.barrier()
        A.release("xs0", "xs1", "sq1", "rs1", "lamtmp")
        if stop_after == "A1":
            return finish(nc, p, es, None)


        def proj_fm(wtile, c0, ncols, tok_w, consume):
            for tb in range(4):
                b = psrot[0] % 4
                psrot[0] += 1
                tsl = slice(tb * 512, (tb + 1) * 512)
                for kc in range(KC):
                    p.op("pe", lambda e, kc=kc, b=b, tsl=tsl: e.matmul(bank(b)[0:ncols, :], lhsT=wtile[:, kc, c0:c0 + ncols], rhs=hT[:, kc, tsl],
                                                                      start=(kc == 0), stop=(kc == KC - 1)),
                         reads=[tok_w, ("hT", tb)], writes=[PS(b)])
                consume(tb, bank(b)[0:ncols, :], b)

        def proj_tm(wtile, tok_w, consume):
            for tkb in range(16):
                b = psrot[0] % 4
                psrot[0] += 1
                tsl = slice(tkb * 128, (tkb + 1) * 128)
                for kc in range(KC):
                    p.op("pe", lambda e, kc=kc, b=b, tsl=tsl: e.matmul(bank(b), lhsT=hT[:, kc, tsl], rhs=wtile[:, kc, :],
                                                                      start=(kc == 0), stop=(kc == KC - 1)),
                         reads=[tok_w, ("hT", tkb // 4)], writes=[PS(b)])
                consume(tkb, bank(b), b)


        q_decT = A.view("q_decT", BF16, [2, S])
        k_decT = A.view("k_decT", BF16, [2, S])
        kteT = A.view("kteT", BF16, [2, S])
        kte_tok = A.view("kte_tok", BF16, [16, 256])
        sgo = A.view("sgo", BF16, [4, S])
        resetm = A.view("resetm", F32, [S])
        T0 = A.view("T0", F32, [S])
        T1 = A.view("T1", F32, [S])
        T2 = A.view("T2", F32, [S])
        ebT = A.view("ebT", F32, [S])
        enbT = A.view("enbT", F32, [S])
        ek2T = A.view("ek2T", F32, [S])
        ebl = A.view("ebl", F32, [2, 32])

        p.op("pool", lambda e: e.memset(resetm, 1.0), writes=[("resetm",)])
        p.op("pool", lambda e: e.memset(resetm[:, 0:S:64], 0.0), writes=[("resetm",)])


        def go_groups(c):
            out = []
            for c4 in (2 * c, 2 * c + 1):
                for tb in range(4):
                    def g(c4=c4, tb=tb):
                        b = psrot[0] % 4
                        psrot[0] += 1
                        tsl = slice(tb * 512, (tb + 1) * 512)
                        for kc in range(KC):
                            p.op("pe", lambda e, kc=kc: e.matmul(bank(b), lhsT=wA[2][:, kc, c4 * 128:(c4 + 1) * 128], rhs=hT[:, kc, tsl],
                                                                 start=(kc == 0), stop=(kc == KC - 1)),
                                 reads=[("w", 2), ("hT", tb)], writes=[PS(b)])
                        p.op("act", lambda e: e.activation(out=sgo[:, c4, tsl], in_=bank(b), func=AF.Copy), reads=[PS(b)], writes=[("sgo", c4, tb)])
                    out.append(g)
            return out

        fill = []

        def FILL(n=2):
            for _ in range(n):
                if fill:
                    fill.pop(0)()

        for c in range(2):
            if c == 0:
                fill.extend([(lambda tkb=tkb: tm_group(tkb, wA[2], ("w", 2), v_tok, "v_tok", "act", 0)) for tkb in range(16)])
            else:
                fill.extend(go_groups(0) + go_groups(1))
            for tb in range(4):
                p.op("pe", lambda e, tb=tb, c=c: e.matmul(bank(4 + tb), lhsT=wg2_sb[0:16, c * 128:(c + 1) * 128],
                                                         rhs=grT[0:16, tb * 512:(tb + 1) * 512], start=True, stop=True),
                     reads=[("wg2",), ("grT", tb)], writes=[PS(4 + tb)])
            zps = ps[:, 4:8, :]
            T0v = T0.rearrange("p (a b) -> p a b", a=4)
            rd4 = [PS(4), PS(5), PS(6), PS(7)]
            p.op("act", lambda e, c=c: e.activation(out=T0v, in_=zps, func=AF.Identity, bias=cst[:, C_BG + c:C_BG + c + 1], scale=1.0),
                 reads=rd4 + [("cst",)], writes=[("T0",)])
            FILL()
            p.op("act", lambda e: e.activation(out=T1, in_=T0, func=AF.Abs), reads=[("T0",)], writes=[("T1",)])
            FILL()
            p.op("act", lambda e: e.activation(out=T1, in_=T1, func=AF.Exp, scale=-1.0), reads=[("T1",)], writes=[("T1",)])
            FILL()
            p.op("act", lambda e: e.activation(out=T1, in_=T1, func=AF.Ln, bias=one_ap, scale=1.0), reads=[("T1",), ("misc", "one")], writes=[("T1",)])
            FILL()
            p.op("dve", lambda e: e.tensor_scalar(out=T2, in0=T0, scalar1=0.0, scalar2=1.0 / 16.0, op0=ALU.min, op1=ALU.mult),
                 reads=[("T0",)], writes=[("T2",)])
            p.op("dve", lambda e: e.scalar_tensor_tensor(out=T1, in0=T1, scalar=-1.0 / 16.0, in1=T2, op0=ALU.mult, op1=ALU.add),
                 reads=[("T1",), ("T2",)], writes=[("T1",)])
            FILL()
            p.op("dve", lambda e: e.tensor_tensor_scan(out=T0, data0=resetm, data1=T1, initial=0.0, op0=ALU.mult, op1=ALU.add),
                 reads=[("T1",), ("resetm",)], writes=[("T0",)])
            FILL()
            p.op("act", lambda e: e.activation(out=ebT, in_=T0, func=AF.Exp), reads=[("T0",)], writes=[("ebT",)])
            p.op("act", lambda e: e.activation(out=enbT, in_=T0, func=AF.Exp, scale=-1.0), reads=[("T0",)], writes=[("enbT",)])
            FILL()
            bl = T0[:, 63:S:64].unsqueeze(2).broadcast_to([128, 32, 64])
            p.op("dve", lambda e, bl=bl: e.tensor_tensor(out=T2.rearrange("p (a b) -> p a b", a=32), in0=bl,
                                                        in1=T0.rearrange("p (a b) -> p a b", a=32), op=ALU.subtract),
                 reads=[("T0",)], writes=[("T2",)])
            p.op("act", lambda e: e.activation(out=ek2T, in_=T2, func=AF.Exp), reads=[("T2",)], writes=[("ek2T",)])
            p.op("dve", lambda e, c=c: e.tensor_copy(out=ebl[:, c, :], in_=ebT[:, 63:S:64]), reads=[("ebT",)], writes=[("ebl", c)])
            while fill:
                FILL()
            if c == 0:
                wdma(wA[2], w_in_cols(2576, 3088), ("w", 2), "w2")
            if c == 0:
                dbg("bT", T0, [("T0",)])
                dbg("ebT", ebT, [("ebT",)])

            def cons_q(tb, pap, b, c=c):
                tsl = slice(tb * 512, (tb + 1) * 512)
                p.op("dve", lambda e: e.scalar_tensor_tensor(out=q_decT[:, c, tsl], in0=pap, scalar=0.125, in1=ebT[:, tsl],
                                                             op0=ALU.mult, op1=ALU.mult),
                     reads=[PS(b), ("ebT",)], writes=[("q_decT", c, tb)])
            proj_fm(wA[0], c * 128, 128, ("w", 0), cons_q)

            def cons_k(tb, pap, b, c=c):
                tsl = slice(tb * 512, (tb + 1) * 512)
                p.op("dve", lambda e: e.tensor_tensor(out=k_decT[:, c, tsl], in0=pap, in1=enbT[:, tsl], op=ALU.mult),
                     reads=[PS(b), ("enbT",)], writes=[("k_decT", c, tb)])
                p.op("dve", lambda e: e.tensor_tensor(out=kteT[:, c, tsl], in0=pap, in1=ek2T[:, tsl], op=ALU.mult),
                     reads=[PS(b), ("ek2T",)], writes=[("kteT", c, tb)])
            proj_fm(wA[0], 256 + c * 128, 128, ("w", 0), cons_k)

        for c in range(2):
            for g in range(4):
                b = psrot[0] % 4
                psrot[0] += 1
                pst = bank(b)[:, 0:256].bitcast(BF16).rearrange("p (a b) -> p a b", a=4)
                for i in range(4):
                    tkb = g * 4 + i
                    p.op("pe", lambda e, i=i, tkb=tkb, c=c, pst=pst: e.transpose(out=pst[:, i, :], in_=kteT[:, c, tkb * 128:(tkb + 1) * 128], identity=ident),
                         reads=[("kteT", c, tkb // 4), ("ident",)], writes=[PS(b)])
                p.op("act", lambda e, g=g, c=c, pst=pst: e.activation(out=kte_tok[:, g * 4:(g + 1) * 4, c * 128:(c + 1) * 128], in_=pst, func=AF.Copy),
                     reads=[PS(b)], writes=[("kte_tok", c, g)])


        for c4 in range(4):
            p.op("act", lambda e, c4=c4: e.activation(out=sgo[:, c4, :], in_=sgo[:, c4, :], func=AF.Silu), reads=[("sgo", c4)], writes=[("sgo", c4)])
        wdma(wA[0], w_in_cols(0, 512), ("w", 0), "w0")
        wdma(wA[1], w_in_cols(512, 1024), ("w", 1), "w1")

        dbg("q_decT", q_decT, [("q_decT",)])
        dbg("k_decT", k_decT, [("k_decT",)])
        dbg("kte_tok", kte_tok, [("kte_tok",)])
        dbg("gv_tok", gv_tok, [("gv_tok",)])
        dbg("sgo", sgo, [("sgo",)])
        dbg("ebl", ebl, [("ebl",)])
        p.barrier(exempt=("dma:w0", "dma:w1", "dma:w2"))
        A.release("kteT", "grT", "resetm", "T0", "T1", "T2", "ebT", "enbT", "ek2T")
        if stop_after == "A2":
            return finish(nc, p, es, None)

        cc = A.view("cc", BF16, [KC, S], top=True)
        Sbf = A.view("Sbf", BF16, [33, 2, 128])
        aT_halves = [A.view("aT_lo", BF16, [8, 4, 128]), A.view("aT_hi", BF16, [8, 4, 128])]

        def aTv(tkb):
            return aT_halves[tkb // 8][:, tkb % 8]
        Sst2 = A.view("Sst", F32, [2, 2, 128])
        e_sq = A.view("e_sq", BF16, [512], top=True)
        e_rs = A.view("e_rs", F32, [512], top=True)
        e_sq1 = A.view("e_t", BF16, [512], top=True)
        e_rs1 = A.view("e_rs1", F32, [512], top=True)

        for tkb in range(16):
            pair = (tkb % 2) * 2
            tsl = slice(tkb * 128, (tkb + 1) * 128)
            for h in range(4):
                c, hh = h // 2, h % 2
                rows = slice(hh * 64, (hh + 1) * 64)
                b = pair + hh
                p.op("pe", lambda e, b=b, h=h, c=c, rows=rows, tsl=tsl: e.matmul(bank(b)[:, c * 128:(c + 1) * 128], lhsT=k_decT[rows, c, tsl],
                                                                                  rhs=q_decT[rows, c, tsl], start=True, stop=True),
                     reads=[("k_decT", c, tkb // 4), ("q_decT", c, tkb // 4)], writes=[PS(b)])
            for hh in range(2):
                b = pair + hh
                p.op("dve", lambda e, b=b, tkb=tkb, hh=hh: e.tensor_tensor(out=aTv(tkb)[:, hh:4:2, :], in0=bank(b)[:, 0:256].rearrange("p (a b) -> p a b", a=2),
                                                                       in1=mask2.unsqueeze(1).broadcast_to([128, 2, 128]), op=ALU.mult),
                     reads=[PS(b), ("mask2",)], writes=[("aT", tkb, hh)])

        p.op("pool", lambda e: e.memset(Sbf[:, 0, :, :], 0.0), writes=[("Sbf", 0)])
        p.op("pool", lambda e: e.memset(Sst2[:, 0, :, :], 0.0), writes=[("Sst", 0)])
        for tkb in range(16):
            bp = 4 + 2 * (tkb % 2)
            for c in range(2):
                for hh in range(2):
                    h = 2 * c + hh
                    orow = slice(hh * 64, (hh + 1) * 64)
                    for s in range(2):
                        trow = slice(s * 64, (s + 1) * 64)
                        p.op("pe", lambda e, c=c, hh=hh, h=h, trow=trow, orow=orow, s=s, bp=bp, tkb=tkb: e.matmul(
                            bank(bp + s)[orow, c * 128:(c + 1) * 128], lhsT=kte_tok[trow, tkb, c * 128 + hh * 64:c * 128 + hh * 64 + 64],
                            rhs=gv_tok[trow, tkb, h * 128:(h + 1) * 128], start=True, stop=True),
                            reads=[("kte_tok", c, tkb // 4), ("gv_tok", tkb)], writes=[PS(bp + s)])
            for s in range(2):
                n = 2 * tkb + s
                so, sn = n % 2, (n + 1) % 2
                for c in range(2):
                    p.op("dve", lambda e, c=c, n=n, so=so, sn=sn, s=s, bp=bp: e.scalar_tensor_tensor(out=Sst2[:, sn, c, :], in0=Sst2[:, so, c, :], scalar=ebl[:, c, n:n + 1],
                                                                                            in1=bank(bp + s)[:, c * 128:(c + 1) * 128], op0=ALU.mult, op1=ALU.add),
                         reads=[PS(bp + s), ("Sst", so, c), ("ebl", c)], writes=[("Sst", sn, c)])
                p.op("act", lambda e, n=n, sn=sn: e.activation(out=Sbf[:, n + 1, :, :], in_=Sst2[:, sn, :, :], func=AF.Copy), reads=[("Sst", sn)], writes=[("Sbf", n + 1)])

        it = 0
        epi_prev = []

        def a3_epi(tb, h):
            b = h
            tsl = slice(tb * 512, (tb + 1) * 512)
            k = it_box[0] % 2
            it_box[0] += 1
            sq_, rs_ = (e_sq, e_rs) if k == 0 else (e_sq1, e_rs1)
            tq, tr = ("e_sqA", k), ("e_rsA", k)
            bs = 6 + k

            def part1():
                p.op("act", lambda e: e.activation(out=sq_, in_=bank(b), func=AF.Square), reads=[PS(b)], writes=[tq])

            def part2():
                p.op("pe", lambda e: e.matmul(bank(bs), lhsT=ones_bf, rhs=sq_, start=True, stop=True), reads=[tq, ("ones",)], writes=[PS(bs)])
                rstd_from_ss(bank(bs), rs_, 128, [PS(bs)], [tr])
                p.op("dve", lambda e: e.scalar_tensor_tensor(out=rs_, in0=bank(b), scalar=cst[:, C_GNW:C_GNW + 1], in1=rs_, op0=ALU.mult, op1=ALU.mult),
                     reads=[PS(b), tr, ("cst",)], writes=[tr])
                p.op("dve", lambda e: e.tensor_tensor(out=cc[:, 4 + h, tsl], in0=rs_, in1=sgo[:, h, tsl], op=ALU.mult),
                     reads=[tr, ("sgo", h, tb)], writes=[("cc", 4 + h, tb)])
            return part1, part2

        it_box = [0]
        carry = None
        for tb in range(4):
            for c in range(2):
                hs = (2 * c, 2 * c + 1)
                for i in range(4):
                    tkb = tb * 4 + i
                    osl = slice(i * 128, (i + 1) * 128)
                    for h in hs:
                        p.op("pe", lambda e, h=h, osl=osl, tkb=tkb: e.matmul(bank(h)[:, osl], lhsT=gv_tok[:, tkb, h * 128:(h + 1) * 128], rhs=aTv(tkb)[:, h, :], start=True, stop=False),
                             reads=[("gv_tok", tkb), ("aT", tkb)], writes=[PS(h)])
                    for s in range(2):
                        n = 2 * tkb + s
                        o2 = slice(i * 128 + s * 64, i * 128 + s * 64 + 64)
                        t2 = slice(tkb * 128 + s * 64, tkb * 128 + s * 64 + 64)
                        for h in hs:
                            rows = slice((h % 2) * 64, (h % 2) * 64 + 64)
                            p.op("pe", lambda e, h=h, rows=rows, s=s, n=n, o2=o2, t2=t2, c=c: e.matmul(bank(h)[:, o2], lhsT=Sbf[rows, n, c, :], rhs=q_decT[rows, c, t2],
                                                                                              start=False, stop=(s == 1)),
                                 reads=[("Sbf", n), ("q_decT", c, tkb // 4)], writes=[PS(h)])
                for h in hs:
                    p1, p2 = a3_epi(tb, h)
                    p1()
                    if carry is not None:
                        carry()
                    carry = p2
        if carry is not None:
            carry()
        p.barrier(exempt=("dma:w0", "dma:w1", "dma:w2"))
        A.release("aT_lo", "aT_hi", "Sbf", "Sst", "q_decT", "k_decT", "kte_tok", "gv_tok", "sgo", "ebl")
        if stop_after == "A3":
            return finish(nc, p, es, (cc, "cc"), dbg_out)

        qT = A.view("qT", BF16, [4, S])
        kT = A.view("kT", BF16, [4, S])
        flip = [0]

        def cons_qk(dst, name, h):
            def cons(tb, pap, b):
                flip[0] ^= 1
                if flip[0]:
                    p.op("act", lambda e: e.activation(out=dst[:, h, tb * 512:(tb + 1) * 512], in_=pap, func=AF.Copy), reads=[PS(b)], writes=[(name, h, tb)])
                else:
                    p.op("dve", lambda e: e.tensor_copy(out=dst[:, h, tb * 512:(tb + 1) * 512], in_=pap), reads=[PS(b)], writes=[(name, h, tb)])
            return cons

        def cons_dv(tkb, pap, b):
            flip[0] ^= 1
            if flip[0]:
                p.op("act", lambda e: e.activation(out=v_tok[:, tkb, :], in_=pap, func=AF.Copy), reads=[PS(b)], writes=[("v_tok", tkb)])
            else:
                p.op("dve", lambda e: e.tensor_copy(out=v_tok[:, tkb, :], in_=pap), reads=[PS(b)], writes=[("v_tok", tkb)])
        for h in range(4):
            proj_fm(wA[0], h * 128, 128, ("w", 0), cons_qk(qT, "qT", h))
            proj_fm(wA[1], h * 128, 128, ("w", 1), cons_qk(kT, "kT", h))
        w_o = wview(0, [KC, D])
        for half in range(2):
            wdma(w_o[:, :, half * 512:(half + 1) * 512], w_out[:, half * 512:(half + 1) * 512].rearrange("(kc p) n -> p kc n", p=128),
                 None, f"w{half}")

        def filler_ops(hn):
            out = []
            for wi, dst, name in ((0, qT, "qT"), (1, kT, "kT")):
                for tb in range(4):
                    tsl = slice(tb * 512, (tb + 1) * 512)
                    for kc in range(KC):
                        def mo(wi=wi, dst=dst, name=name, tb=tb, tsl=tsl, kc=kc):
                            p.op("pe", lambda e: e.matmul(bank(7), lhsT=wA[wi][:, kc, hn * 128:(hn + 1) * 128], rhs=hT[:, kc, tsl],
                                                          start=(kc == 0), stop=(kc == KC - 1)),
                                 reads=[("w", wi), ("hT", tb)], writes=[PS(7)])
                            if kc == KC - 1:
                                p.op("dve", lambda e: e.tensor_copy(out=dst[:, hn, tsl], in_=bank(7)), reads=[PS(7)], writes=[(name, hn, tb)])
                        out.append(mo)
            return out

        NP = 4
        p_sb = A.view("p_sb", BF16, [NP, 2, 512])
        a_r = A.view("a_r", F32, [2, 512])
        a_o = A.view("a_o", F32, [512])
        xsB = [A.view(f"xs{i}", F32, [KC, 512], top=True) for i in range(2)]
        o_s = A.view("o_s", F32, [2, 512])
        item = [0]
        pending = []

        def flush_one():
            if pending:
                pending.pop(0)()

        blocks = [(h, qb) for h in range(4) for qb in range(4)]
        steps = [(bi, kb) for bi, (h, qb) in enumerate(blocks) for kb in range(4 * (qb + 1))]

        def emit_S(step):
            bi, kb = step
            h, qb = blocks[bi]
            qs = qb * 512
            j = kb - 4 * qb
            q0 = 128 * j if j > 0 else 0
            sl_ = item[0] % NP
            sbk = 2 * (item[0] % 2)
            item[0] += 1
            for c in range(2):
                rows = slice(c * 64, (c + 1) * 64)
                p.op("pe", lambda e, c=c, rows=rows: e.matmul(bank(sbk + c)[:, q0:512], lhsT=kT[rows, h, kb * 128:(kb + 1) * 128],
                                                              rhs=qT[rows, h, qs + q0:qs + 512], start=True, stop=True),
                     reads=[("kT", h, kb // 4), ("qT", h, qb)], writes=[PS(sbk + c)])
            for c in range(2):
                p.op("act", lambda e, c=c: e.activation(out=p_sb[:, sl_, c, q0:512], in_=bank(sbk + c)[:, q0:512], func=AF.Exp, scale=0.125),
                     reads=[PS(sbk + c)], writes=[("p_sb", sl_, c)])
                if j >= 0:
                    d0 = 128 * j
                    p.op("pool", lambda e, c=c, d0=d0: e.tensor_tensor(out=p_sb[:, sl_, c, d0:d0 + 128], in0=p_sb[:, sl_, c, d0:d0 + 128], in1=maskT, op=ALU.mult),
                         reads=[("p_sb", sl_, c), ("maskT",)], writes=[("p_sb", sl_, c)])
            return (sl_, q0)

        def emit_AV(step, res):
            bi, kb = step
            h, qb = blocks[bi]
            nkb = 4 * (qb + 1)
            sl_, q0 = res
            st, sp_ = (kb == 0), (kb == nkb - 1)
            for c in range(2):
                p.op("pe", lambda e, c=c: e.matmul(bank(4 + c)[:, q0:512], lhsT=v_tok[:, kb, h * 128:(h + 1) * 128], rhs=p_sb[:, sl_, c, q0:512], start=st, stop=sp_),
                     reads=[("v_tok", kb), ("p_sb", sl_, c)], writes=[PS(4 + c)])
                p.op("pe", lambda e, c=c: e.matmul(bank(6 + c)[:, q0:512], lhsT=ones_bf, rhs=p_sb[:, sl_, c, q0:512], start=st, stop=sp_),
                     reads=[("ones",), ("p_sb", sl_, c)], writes=[PS(6 + c)])

        def make_epilogue(bi):
            h, qb = blocks[bi]
            tsl = slice(qb * 512, qb * 512 + 512)

            def s2():
                p.op("act", lambda e: e.activation(out=a_r, in_=a_r, func=AF.Exp, scale=-1.0), reads=[("a_r",)], writes=[("a_r",)])

            def s3():
                p.op("dve", lambda e: e.tensor_tensor(out=o_s, in0=o_s, in1=a_r, op=ALU.mult), reads=[("o_s",), ("a_r",)], writes=[("o_s",)])

            def s4():
                p.op("dve", lambda e: e.scalar_tensor_tensor(out=a_o, in0=o_s[:, 1, :], scalar=neg_lam, in1=o_s[:, 0, :], op0=ALU.mult, op1=ALU.add),
                     reads=[("o_s",), ("misc", "nl")], writes=[("a_o",)])
                p.op("dve", lambda e: e.tensor_tensor(out=e_sq, in0=a_o, in1=a_o, op=ALU.mult), reads=[("a_o",)], writes=[("e_sq",)])

            def s5():
                b_ = 2 * ((item[0] + 1) % 2)
                p.op("pe", lambda e: e.matmul(bank(b_), lhsT=ones_bf, rhs=e_sq, start=True, stop=True), reads=[("e_sq",), ("ones",)], writes=[PS(b_)])
                rstd_from_ss(bank(b_), e_rs, 128, [PS(b_)], [("e_rs",)])

            def s7():
                p.op("dve", lambda e: e.scalar_tensor_tensor(out=cc[:, h, tsl], in0=a_o, scalar=w8, in1=e_rs, op0=ALU.mult, op1=ALU.mult),
                     reads=[("a_o",), ("e_rs",), ("misc", "w8")], writes=[("cc", h, qb)])
            return [s2, s3, s4, s5, s7]

        prev = emit_S(steps[0])
        for si, step in enumerate(steps):
            bi, kb = step
            nkb = 4 * (blocks[bi][1] + 1)
            nxt = emit_S(steps[si + 1]) if si + 1 < len(steps) else None
            emit_AV(step, prev)
            prev = nxt
            for _ in range(2 if nkb == 4 else 1):
                flush_one()
            if kb == nkb - 1:
                while pending:
                    flush_one()
                for c in range(2):
                    p.op("dve", lambda e, c=c: e.tensor_copy(out=o_s[:, c, :], in_=bank(4 + c)), reads=[PS(4 + c)], writes=[("o_s", c)])
                    p.op("act", lambda e, c=c: e.activation(out=a_r[:, c, :], in_=bank(6 + c), func=AF.Ln), reads=[PS(6 + c)], writes=[("a_r", c)])
                pending = make_epilogue(bi)
        while pending:
            flush_one()
        for tb_ in range(2):
            p.op("sp", lambda e, tb_=tb_: e.dma_start(out=xsB[tb_], in_=xT[:, :, tb_ * 512:(tb_ + 1) * 512]), writes=[("xs", tb_)], dma=f"xs{tb_}")
        p.barrier(exempt=("dma:w0", "dma:w1", "dma:w2", "dma:xs0", "dma:xs1"))
        A.release("qT", "kT", "v_tok", "p_sb", "a_r", "a_o", "e_t", "e_rs1", "hT", "o_s")
        if stop_after == "A5":
            return finish(nc, p, es, (cc, "cc"), dbg_out)

        dbg("w_o", w_o, [("w",)])
        dbg("cc", cc, [("cc",)])
        x1T = A.view("x1T", F32, [KC, S])
        rstd2 = A.view("rstd2", F32, [S])
        sqB = A.view("sq1", BF16, [KC, 512])
        h2 = A.view("h2", BF16, [KC, 1024])
        wu = [wview(16384, [KC, 2, 128]), wview(20480, [KC, 2, 128]), wview(0, [KC, 2, 128])]
        wd = [wview(4096 + i * 5632, [NFF, 128]) for i in range(2)]

        def load_wu(ffc, slot):
            for g in range(2):
                c0 = g * DFF + ffc * 128
                wdma(wu[slot][:, :, g, :], w_up[:, c0:c0 + 128].rearrange("(kc p) n -> p kc n", p=128), ("wu", slot, g), f"wu{slot}{g}")

        def load_wd(dmc, slot):
            wdma(wd[slot], w_down[:, dmc * 128:(dmc + 1) * 128].rearrange("(f p) n -> p f n", p=128), ("wd", slot), f"wd{slot}")

        load_wu(0, 0)
        load_wu(1, 1)

        def h2_compute(th):
            t0 = th * 1024
            for kc in range(KC):
                for tb2 in range(2):
                    tsl = slice(t0 + tb2 * 512, t0 + (tb2 + 1) * 512)
                    p.op("dve", lambda e, kc=kc, tb2=tb2, tsl=tsl: e.scalar_tensor_tensor(
                        out=h2[:, kc, tb2 * 512:(tb2 + 1) * 512], in0=x1T[:, kc, tsl], scalar=cst[:, C_LN2 + kc:C_LN2 + kc + 1], in1=rstd2[:, tsl],
                        op0=ALU.mult, op1=ALU.mult),
                        reads=[("x1T", 2 * th + tb2, kc), ("rstd2", 2 * th + tb2), ("cst",)], writes=[("h2", kc, tb2)])

        def b_proj(tb):
            sl = tb % 2
            tsl = slice(tb * 512, (tb + 1) * 512)
            if tb >= 2:
                p.op("sp", lambda e: e.dma_start(out=xsB[sl], in_=xT[:, :, tsl]), writes=[("xs", sl)], dma=f"xs{sl}")
            for dmc in range(KC):
                b = psrot[0] % 4
                psrot[0] += 1
                for kc in range(KC):
                    p.op("pe", lambda e, b=b, kc=kc, dmc=dmc: e.matmul(bank(b), lhsT=w_o[:, kc, dmc * 128:(dmc + 1) * 128], rhs=cc[:, kc, tsl],
                                                                     start=(kc == 0), stop=(kc == KC - 1)),
                         reads=[("w", dmc // 4), ("cc",)], writes=[PS(b)])
                p.op("dve", lambda e, b=b, dmc=dmc: e.tensor_tensor(out=x1T[:, dmc, tsl], in0=bank(b), in1=xsB[sl][:, dmc, :], op=ALU.add),
                     reads=[PS(b), ("xs", sl)], writes=[("x1T", tb, dmc)])

        def b_stats(tb):
            tsl = slice(tb * 512, (tb + 1) * 512)
            p.op("act", lambda e: e.activation(out=sqB, in_=x1T[:, :, tsl], func=AF.Square), reads=[("x1T", tb)], writes=[("sq1",)])
            bs = 4 + tb % 2
            for kc in range(KC):
                p.op("pe", lambda e, kc=kc: e.matmul(bank(bs), lhsT=ones_bf, rhs=sqB[:, kc, :], start=(kc == 0), stop=(kc == KC - 1)),
                     reads=[("ones",), ("sq1",)], writes=[PS(bs)])
            rstd_from_ss(bank(bs), rstd2[:, tsl], D, [PS(bs)], [("rstd2", tb)])

        b_proj(0)
        for tb in range(1, 4):
            b_proj(tb)
            b_stats(tb - 1)
            if tb == 2:
                h2_compute(0)
        b_stats(3)
        p.barrier()
        A.release("cc", "xs0", "xs1", "e_sq", "e_rs")
        if stop_after == "B":
            return finish(nc, p, es, (x1T, "x1T"), dbg_out)

        actT = A.view("actT", BF16, [NFF, 1024])
        uh = A.view("uh", F32, [2 * NFF, 2])
        NCB = 2
        u_sb = [[A.view(f"u_sb{i}{g}", F32, [1026]) for g in range(2)] for i in range(NCB)]
        y_sb = [[A.view(f"y_sb{i}{g}", F32, [1024]) for g in range(2)] for i in range(NCB)]
        for i_ in range(NCB):
            for g_ in range(2):
                p.op("pool", lambda e, us=u_sb[i_][g_]: e.memset(us[:, 0:2], 0.0), writes=[("u_sb", i_, g_, "h")])

        rsf = A.view("rsf", F32, [512])
        fin_pending = []

        def final_stages(tb):
            tsl = slice(tb * 512, (tb + 1) * 512)

            def f1():
                p.op("act", lambda e: e.activation(out=sqB, in_=x1T[:, :, tsl], func=AF.Square), reads=[("x1T", tb)], writes=[("sq1",)])

            def f2():
                bs = psrot[0] % 8
                psrot[0] += 1
                for kc in range(KC):
                    p.op("pe", lambda e, kc=kc: e.matmul(bank(bs), lhsT=ones_bf, rhs=sqB[:, kc, :], start=(kc == 0), stop=(kc == KC - 1)),
                         reads=[("ones",), ("sq1",)], writes=[PS(bs)])
                rstd_from_ss(bank(bs), rsf, D, [PS(bs)], [("rsf",)])

            def f3():
                for kc in range(KC):
                    p.op("dve", lambda e, kc=kc: e.scalar_tensor_tensor(out=x1T[:, kc, tsl], in0=x1T[:, kc, tsl], scalar=cst[:, C_LNF + kc:C_LNF + kc + 1],
                                                                        in1=rsf, op0=ALU.mult, op1=ALU.mult),
                         reads=[("x1T", tb, kc), ("rsf",), ("cst",)], writes=[("x1T", tb, kc)])
                p.op("sp", lambda e: e.dma_start(out=outT[:, :, tsl], in_=x1T[:, :, tsl]), reads=[("x1T", tb)], writes=[("out", tb)], dma=f"out{tb % 2}")
            return [f1, f2, f3]

        pair_it = 0
        for th in range(2):
            t0 = th * 1024
            for ffc in range(NFF):
                slot = pair_it % 3
                cb = pair_it % NCB
                pair_it += 1
                if ffc + 2 < NFF:
                    load_wu(ffc + 2, (slot + 2) % 3)
                if ffc == NFF - 4:
                    load_wd(0, 0)
                    load_wd(1, 1)
                pb = 4 * (ffc % 2)
                for g in range(2):
                    for tb2 in range(2):
                        b = pb + 2 * g + tb2
                        for kc in range(KC):
                            p.op("pe", lambda e, b=b, kc=kc, g=g, tb2=tb2, slot=slot: e.matmul(bank(b), lhsT=wu[slot][:, kc, g, :], rhs=h2[:, kc, tb2 * 512:(tb2 + 1) * 512],
                                                                                             start=(kc == 0), stop=(kc == KC - 1)),
                                 reads=[("wu", slot, g), ("h2",)], writes=[PS(b)])
                for g in range(2):
                    ch = g * NFF + ffc
                    b0 = pb + 2 * g
                    pss = ps[:, b0:b0 + 2, :]
                    rd = [PS(b0), PS(b0 + 1)]
                    us = u_sb[cb][g]
                    ys = y_sb[cb][g]
                    cw = lambda i, ch=ch: cst[:, C_CW + ch * 3 + i:C_CW + ch * 3 + i + 1]
                    p.op("act", lambda e, us=us, pss=pss: e.activation(out=us[:, 2:1026].rearrange("p (a b) -> p a b", a=2), in_=pss, func=AF.Copy),
                         reads=rd, writes=[("u_sb", cb, g, "m")])
                    if th == 0:
                        p.op("act", lambda e, ch=ch, b0=b0: e.activation(out=uh[:, ch, :], in_=bank(b0 + 1)[:, 510:512], func=AF.Copy),
                             reads=[PS(b0 + 1)], writes=[("uh", ch)])
                    else:
                        p.op("dve", lambda e, us=us, ch=ch: e.tensor_copy(out=us[:, 0:2], in_=uh[:, ch, :]), reads=[("uh", ch)], writes=[("u_sb", cb, g, "h")])
                    p.op("act", lambda e, ys=ys, pss=pss, ch=ch, cw=cw: e.activation(out=ys.rearrange("p (a b) -> p a b", a=2), in_=pss, func=AF.Identity,
                                                                                    scale=cw(2), bias=cst[:, C_CB + ch:C_CB + ch + 1]),
                         reads=rd + [("cst",)], writes=[("y_sb", cb, g)])
                    p.op("dve", lambda e, ys=ys, us=us, cw=cw: e.scalar_tensor_tensor(out=ys, in0=us[:, 1:1025], scalar=cw(1), in1=ys, op0=ALU.mult, op1=ALU.add),
                         reads=[("u_sb", cb, g), ("y_sb", cb, g), ("cst",)], writes=[("y_sb", cb, g)])
                    p.op("dve", lambda e, ys=ys, us=us, cw=cw: e.scalar_tensor_tensor(out=ys, in0=us[:, 0:1024], scalar=cw(0), in1=ys, op0=ALU.mult, op1=ALU.add),
                         reads=[("u_sb", cb, g), ("y_sb", cb, g), ("cst",)], writes=[("y_sb", cb, g)])
                yg, yv = y_sb[cb][0], y_sb[cb][1]
                p.op("act", lambda e, yg=yg: e.activation(out=yg, in_=yg, func=AF.Silu), reads=[("y_sb", cb, 0)], writes=[("y_sb", cb, 0)])
                p.op("dve", lambda e, yg=yg, yv=yv, ffc=ffc: e.tensor_tensor(out=actT[:, ffc, :], in0=yg, in1=yv, op=ALU.mult),
                     reads=[("y_sb", cb, 0), ("y_sb", cb, 1)], writes=[("actT", ffc)])
            if th == 0:
                h2_compute(1)
                load_wu(0, (pair_it) % 3)
                load_wu(1, (pair_it + 1) % 3)
            def dn_mm(b, f, slot, tb2):
                p.op("pe", lambda e: e.matmul(bank(b), lhsT=wd[slot][:, f, :], rhs=actT[:, f, tb2 * 512:(tb2 + 1) * 512],
                                              start=(f == 0), stop=(f == NFF - 1)),
                     reads=[("wd", slot), ("actT", f)], writes=[PS(b)])

            def dn_evac(b, dmc, tb2):
                tsl = slice(t0 + tb2 * 512, t0 + (tb2 + 1) * 512)
                p.op("dve", lambda e: e.tensor_tensor(out=x1T[:, dmc, tsl], in0=bank(b), in1=x1T[:, dmc, tsl], op=ALU.add),
                     reads=[PS(b), ("x1T", 2 * th + tb2, dmc)], writes=[("x1T", 2 * th + tb2, dmc)])

            NL = 4
            first = []
            for dmc in (0, 1):
                for tb2 in (0, 1):
                    b = psrot[0] % 8
                    psrot[0] += 1
                    first.append((b, dmc, tb2))
                    for f in range(NFF - NL):
                        dn_mm(b, f, dmc % 2, tb2)
            for (b, dmc, tb2) in first:
                for f in range(NFF - NL, NFF):
                    dn_mm(b, f, dmc % 2, tb2)
                dn_evac(b, dmc, tb2)
            load_wd(2, 0)
            load_wd(3, 1)
            for dmc in range(2, KC):
                slot = dmc % 2
                for tb2 in (0, 1):
                    b = psrot[0] % 8
                    psrot[0] += 1
                    for f in range(NFF):
                        dn_mm(b, f, slot, tb2)
                    dn_evac(b, dmc, tb2)
                if dmc + 2 < KC:
                    load_wd(dmc + 2, slot)
                if fin_pending:
                    fin_pending.pop(0)()
            if th == 0:
                fin_pending.extend(final_stages(0) + final_stages(1))
            else:
                a2, a3 = final_stages(2), final_stages(3)
                for f in (a2[0], a2[1], a3[0], a2[2], a3[1], a3[2]):
                    f()
        return finish(nc, p, es, None, dbg_out)


def finish(nc, p, es, dump, dbg_out=None):
    if dump is not None and dbg_out:
        ap, name = dump
        if name in dbg_out:
            p.op("sp", lambda e: e.dma_start(out=dbg_out[name], in_=ap), reads=[(name,)], writes=[("dbg", name)], dma="dbg_" + name)
    fin = p.op("sp", lambda e: e.nop())
    for s, i in p.last.items():
        if s.startswith("dma:") and i != fin.id:
            fin.deps.add(i)
    p.finalize()
    sems = {}
    for s in p.streams:
        sems[s] = es.enter_context(nc.semaphore("s_" + s.replace(":", "_")))
    block = es.enter_context(nc.Block())
    p.emit(nc, block, sems)
    return nc


def make_inputs(x, ln1_w, w_in, diff_lq1, diff_lk1, diff_lq2, diff_lk2, diff_subln_w, gla_wg2, gla_bg,
                gla_norm_w, w_out, ln2_w, w_up, conv_w, conv_b, w_down, lnf_w):
    f = lambda a: np.ascontiguousarray(np.asarray(a, dtype=np.float32))
    cst = np.zeros((128, NCST), np.float32)
    cst[:, C_LN1:C_LN1 + 8] = f(ln1_w)[0].reshape(8, 128).T
    cst[:, C_LN2:C_LN2 + 8] = f(ln2_w)[0].reshape(8, 128).T
    cst[:, C_LNF:C_LNF + 8] = f(lnf_w).reshape(8, 128).T
    cst[:, C_BG:C_BG + 2] = f(gla_bg)[0].reshape(2, 128).T
    cst[:, C_SUB] = f(diff_subln_w)[0]
    cst[:, C_GNW] = f(gla_norm_w)[0]
    cst[:, C_CB:C_CB + 44] = f(conv_b)[0].reshape(44, 128).T
    cst[:, C_CW:C_CW + 132] = f(conv_w)[0].T.reshape(44, 128, 3).transpose(1, 0, 2).reshape(128, 132)
    cst[:, C_LQ1:C_LQ1 + 64] = f(diff_lq1)[0][None, :]
    cst[:, C_LK1:C_LK1 + 64] = f(diff_lk1)[0][None, :]
    cst[:, C_LQ2:C_LQ2 + 64] = f(diff_lq2)[0][None, :]
    cst[:, C_LK2:C_LK2 + 64] = f(diff_lk2)[0][None, :]
    shared = {"w_in": f(w_in)[0], "w_out": f(w_out)[0], "w_up": f(w_up)[0], "w_down": f(w_down)[0],
              "wg2": f(gla_wg2)[0], "cst": cst}
    xf = f(x)
    maps = []
    for b in range(xf.shape[0]):
        xt = np.ascontiguousarray(xf[b].T.reshape(KC, 128, S).transpose(1, 0, 2))
        m = dict(shared)
        m["xT"] = xt
        maps.append(m)
    return maps


_NC_CACHE = {}


def kernel(**inputs):
    maps = make_inputs(**inputs)
    if "nc" not in _NC_CACHE:
        _NC_CACHE["nc"] = build_nc()
    nc = _NC_CACHE["nc"]
    n = len(maps)
    res = run_bass_kernel_spmd(nc, maps, core_ids=list(range(n)))
    out = np.empty((n, S, D), np.float32)
    for b in range(n):
        o = res.results[b]["outT"]
        out[b] = o.transpose(1, 0, 2).reshape(D, S).T
    return out
```

```python
import numpy as np
import concourse.bass as bass
import concourse.mybir as mybir
from concourse.bass_utils import run_bass_kernel_spmd

F32 = mybir.dt.float32
BF16 = mybir.dt.bfloat16
AF = mybir.ActivationFunctionType
ALU = mybir.AluOpType

S = 2048
D = 1024
KC = 8
DFF = 2816
NFF = 22
EPS = 1e-6
SAME_ENGINE_SYNC = True

C_LN1, C_LN2, C_LNF, C_BG, C_SUB, C_GNW, C_CB, C_CW = 0, 8, 16, 24, 26, 27, 28, 72
C_LQ1, C_LK1, C_LQ2, C_LK2 = 204, 268, 332, 396
NCST = 460


class Op:
    __slots__ = ("id", "eng", "fn", "deps", "dma", "stream", "signal", "tick", "waits")

    def __init__(self, id, eng, fn, dma):
        self.id, self.eng, self.fn, self.dma = id, eng, fn, dma
        self.stream = ("dma:" + dma) if dma else eng
        self.deps = set()
        self.signal = False
        self.tick = 0
        self.waits = {}


class Prog:
    ENGS = ("pe", "act", "dve", "pool", "sp")

    def __init__(self):
        self.ops = []
        self.state = {}
        self.desc = {}
        self.last = {}
        self.barrier_op = None

    def _related(self, t):
        out = []
        for i in range(1, len(t) + 1):
            pre = t[:i]
            if pre in self.state:
                out.append(pre)
        for d in self.desc.get(t, ()):
            if d in self.state:
                out.append(d)
        return out

    def _ensure(self, t):
        if t not in self.state:
            self.state[t] = [None, {}]
            for i in range(1, len(t)):
                self.desc.setdefault(t[:i], set()).add(t)

    def op(self, eng, fn, reads=(), writes=(), dma=None):
        o = Op(len(self.ops), eng, fn, dma)
        deps = o.deps
        for t in reads:
            for r in self._related(t):
                w = self.state[r][0]
                if w is not None:
                    deps.add(w)
        for t in writes:
            for r in self._related(t):
                st = self.state[r]
                if st[0] is not None:
                    deps.add(st[0])
                deps.update(st[1].values())
        if self.barrier_op is not None:
            deps.add(self.barrier_op)
        for t in reads:
            self._ensure(t)
            self.state[t][1][o.stream] = o.id
        for t in writes:
            for d in list(self.desc.get(t, ())):
                self.state.pop(d, None)
            self.desc.pop(t, None)
            self._ensure(t)
            self.state[t] = [o.id, {}]
        deps.discard(o.id)
        self.ops.append(o)
        self.last[o.stream] = o.id
        return o

    def barrier(self, exempt=()):
        o = self.op("pool", lambda e: e.nop())
        for s, i in self.last.items():
            if s not in exempt and i != o.id:
                o.deps.add(i)
        self.barrier_op = o.id
        return o

    def finalize(self):
        ops = self.ops
        for o in ops:
            for d in list(o.deps):
                do = ops[d]
                if do.stream == o.stream and not do.dma:
                    if o.eng == "pe" or o.eng == "sp" or not SAME_ENGINE_SYNC:
                        o.deps.discard(d)
                        continue
                do.signal = True
        cnt = {}
        for o in ops:
            if o.dma:
                o.signal = True
            if o.signal:
                inc = 16 if o.dma else 1
                cnt[o.stream] = cnt.get(o.stream, 0) + inc
                o.tick = cnt[o.stream]
        for s, v in cnt.items():
            assert v < 30000, (s, v)
        self.streams = sorted(cnt.keys())
        clock = {e: {} for e in self.ENGS}
        snap = {}
        for o in ops:
            need = {}
            for d in o.deps:
                do = ops[d]
                if need.get(do.stream, 0) < do.tick:
                    need[do.stream] = do.tick
            ck = clock[o.eng]
            for d in o.deps:
                do = ops[d]
                if ck.get(do.stream, 0) >= do.tick:
                    for k, v in snap[d].items():
                        if ck.get(k, 0) < v:
                            ck[k] = v
            waits = {}
            for k, v in need.items():
                if ck.get(k, 0) < v:
                    waits[k] = v
            for k, v in waits.items():
                ck[k] = v
            for d in o.deps:
                for k, v in snap[d].items():
                    if ck.get(k, 0) < v:
                        ck[k] = v
            o.waits = waits
            if o.signal:
                sn = dict(ck)
                sn[o.stream] = o.tick
                snap[o.id] = sn

    def emit(self, nc, block, sems):
        by_eng = {e: [o for o in self.ops if o.eng == e] for e in self.ENGS}

        def run(eng_name, e):
            for o in by_eng[eng_name]:
                for k, v in o.waits.items():
                    e.wait_ge(sems[k], v)
                inst = o.fn(e)
                if o.signal:
                    inst.then_inc(sems[o.stream], 16 if o.dma else 1)

        @block.tensor
        def _(e):
            run("pe", e)

        @block.scalar
        def _(e):
            run("act", e)

        @block.vector
        def _(e):
            run("dve", e)

        @block.gpsimd
        def _(e):
            run("pool", e)

        @block.sync
        def _(e):
            run("sp", e)


class Arena:
    def __init__(self, big, nbytes):
        self.big = big
        self.free = [(0, nbytes)]
        self.used = {}

    def alloc(self, name, nbytes, top=False):
        nbytes = (nbytes + 63) // 64 * 64
        if top:
            for i in range(len(self.free) - 1, -1, -1):
                o, n = self.free[i]
                if n >= nbytes:
                    self.free[i] = (o, n - nbytes)
                    self.used[name] = (o + n - nbytes, nbytes)
                    return o + n - nbytes
        for i, (o, n) in enumerate(self.free):
            if n >= nbytes:
                self.free[i] = (o + nbytes, n - nbytes)
                self.used[name] = (o, nbytes)
                return o
        raise RuntimeError(f"arena out of space for {name} {nbytes}: {self.free}")

    def release(self, *names):
        for name in names:
            o, n = self.used.pop(name)
            self.free.append((o, n))
        self.free.sort()
        m = []
        for o, n in self.free:
            if n == 0:
                continue
            if m and m[-1][0] + m[-1][1] == o:
                m[-1] = (m[-1][0], m[-1][1] + n)
            else:
                m.append((o, n))
        self.free = m

    def view(self, name, dtype, shape, parts=128, top=False):
        esz = 4 if dtype == F32 else 2
        n = int(np.prod(shape))
        o = self.alloc(name, n * esz, top=top)
        ap = self.big[0:parts, o // 4:(o + n * esz) // 4]
        if dtype != F32:
            ap = ap.bitcast(dtype)
        if len(shape) == 2:
            ap = ap.rearrange("p (a b) -> p a b", a=shape[0])
        elif len(shape) == 3:
            ap = ap.rearrange("p (a b c) -> p a b c", a=shape[0], b=shape[1])
        return ap


def build_nc(stop_after=None, debug=()):
    nc = bass.Bass("TRN2", target_bir_lowering=False)
    xT = nc.dram_tensor("xT", [128, KC, S], F32, kind="ExternalInput").ap()
    w_in = nc.dram_tensor("w_in", [D, 3088], F32, kind="ExternalInput").ap()
    w_out = nc.dram_tensor("w_out", [D, D], F32, kind="ExternalInput").ap()
    w_up = nc.dram_tensor("w_up", [D, 2 * DFF], F32, kind="ExternalInput").ap()
    w_down = nc.dram_tensor("w_down", [DFF, D], F32, kind="ExternalInput").ap()
    wg2 = nc.dram_tensor("wg2", [16, 256], F32, kind="ExternalInput").ap()
    cst_d = nc.dram_tensor("cst", [128, NCST], F32, kind="ExternalInput").ap()
    outT = nc.dram_tensor("outT", [128, KC, S], F32, kind="ExternalOutput").ap()
    dbg_out = {}
    for name, shape, dt in debug:
        dbg_out[name] = nc.dram_tensor("dbg_" + name, list(shape), dt, kind="ExternalOutput").ap()

    p = Prog()
    ARENA_BYTES = 207 * 1024
    import contextlib
    es = contextlib.ExitStack()
    with es:
        big = es.enter_context(nc.sbuf_tensor("big", [128, ARENA_BYTES // 4], F32))
        ps = es.enter_context(nc.psum_tensor("ps", [128, 8, 512], F32))
        A = Arena(big, ARENA_BYTES)

        def bank(b):
            return ps[:, b, :]

        PS = lambda b: ("ps", b)

        cst = A.view("cst", F32, [NCST])
        ones_bf = A.view("ones", BF16, [128])
        ident = A.view("ident", BF16, [128])
        maskT = A.view("maskT", BF16, [128])
        mask2 = A.view("mask2", BF16, [128])
        misc = A.view("misc", F32, [16])
        wg2_sb = A.view("wg2", F32, [256])
        wslot_bytes = 8192
        wsl = [A.alloc(f"wslot{i}", wslot_bytes) for i in range(3)]
        wbase = wsl[0]
        assert wsl[1] == wbase + wslot_bytes and wsl[2] == wbase + 2 * wslot_bytes

        def wview(byte_off, shape, dtype=BF16):
            n = int(np.prod(shape))
            esz = 2
            o = wbase + byte_off
            ap = big[:, o // 4:(o + n * esz) // 4].bitcast(dtype)
            if len(shape) == 2:
                ap = ap.rearrange("p (a b) -> p a b", a=shape[0])
            elif len(shape) == 3:
                ap = ap.rearrange("p (a b c) -> p a b c", a=shape[0], b=shape[1])
            return ap

        p.op("sp", lambda e: e.dma_start(out=cst, in_=cst_d), writes=[("cst",)], dma="cst")
        p.op("sp", lambda e: e.dma_start(out=wg2_sb[0:16, :], in_=wg2), writes=[("wg2",)], dma="wg2")
        p.op("pool", lambda e: e.memset(ones_bf, 1.0), writes=[("ones",)])
        p.op("pool", lambda e: e.affine_select(out=maskT, in_=ones_bf, pattern=[[1, 128]], compare_op=ALU.is_ge,
                                               fill=0.0, base=0, channel_multiplier=-1),
             reads=[("ones",)], writes=[("maskT",)])
        p.op("pool", lambda e: e.affine_select(out=ident, in_=ones_bf, pattern=[[1, 128]], compare_op=ALU.is_equal,
                                               fill=0.0, base=0, channel_multiplier=-1),
             reads=[("ones",)], writes=[("ident",)])
        p.op("pool", lambda e: e.tensor_copy(out=mask2, in_=maskT), reads=[("maskT",)], writes=[("mask2",)])
        p.op("pool", lambda e: e.memset(mask2[0:64, 64:128], 0.0), writes=[("mask2",)])
        neg_lam = misc[:, 0:1]
        w8 = misc[:, 1:2]
        lt = A.view("lamtmp", F32, [2, 64])
        p.op("dve", lambda e: e.tensor_tensor(out=lt[:, 0, :], in0=cst[:, C_LQ1:C_LQ1 + 64], in1=cst[:, C_LK1:C_LK1 + 64], op=ALU.mult),
             reads=[("cst",)], writes=[("lt", 0)])
        p.op("dve", lambda e: e.tensor_tensor(out=lt[:, 1, :], in0=cst[:, C_LQ2:C_LQ2 + 64], in1=cst[:, C_LK2:C_LK2 + 64], op=ALU.mult),
             reads=[("cst",)], writes=[("lt", 1)])
        p.op("dve", lambda e: e.reduce_sum(out=misc[:, 2:4], in_=lt, axis=mybir.AxisListType.X), reads=[("lt",)], writes=[("misc", "s")])
        p.op("act", lambda e: e.activation(out=misc[:, 4:6], in_=misc[:, 2:4], func=AF.Exp), reads=[("misc", "s")], writes=[("misc", "e")])
        p.op("dve", lambda e: e.tensor_tensor(out=misc[:, 6:7], in0=misc[:, 5:6], in1=misc[:, 4:5], op=ALU.subtract),
             reads=[("misc", "e")], writes=[("misc", "d")])
        p.op("dve", lambda e: e.tensor_scalar(out=neg_lam, in0=misc[:, 6:7], scalar1=-0.2, scalar2=None, op0=ALU.add),
             reads=[("misc", "d")], writes=[("misc", "nl")])
        p.op("dve", lambda e: e.tensor_scalar(out=w8, in0=cst[:, C_SUB:C_SUB + 1], scalar1=0.8, scalar2=None, op0=ALU.mult),
             reads=[("cst",)], writes=[("misc", "w8")])

        def wdma(dst, src, tok, key):
            toks = [("w", 0), ("w", 1)] if tok is None else [tok]
            p.op("pool", lambda e: e.dma_start(out=dst, in_=src), writes=toks, dma=key)

        def w_in_cols(c0, c1):
            return w_in[:, c0:c1].rearrange("(kc p) n -> p kc n", p=128)

        def rstd_from_ss(ss_ps_ap, out_ap, n, reads, writes):
            p.op("act", lambda e: e.activation(out=out_ap, in_=ss_ps_ap, func=AF.Ln, scale=1.0 / n, bias=eps_ap),
                 reads=reads + [("misc", "eps")], writes=writes)
            p.op("act", lambda e: e.activation(out=out_ap, in_=out_ap, func=AF.Exp, scale=-0.5),
                 reads=writes, writes=writes)

        eps_ap = misc[:, 8:9]
        one_ap = misc[:, 9:10]
        p.op("pool", lambda e: e.memset(eps_ap, EPS), writes=[("misc", "eps")])
        p.op("pool", lambda e: e.memset(one_ap, 1.0), writes=[("misc", "one")])

        def dbg(name, ap, reads):
            if name in dbg_out:
                p.op("sp", lambda e: e.dma_start(out=dbg_out[name], in_=ap), reads=reads, writes=[("dbg", name)], dma="dbg_" + name)

        wA = [wview(i * wslot_bytes, [KC, 512]) for i in range(3)]
        w_r = A.view("w_r", BF16, [KC, 16])
        wdma(w_r, w_in_cols(2560, 2576), ("w_r",), "wr")
        wdma(wA[1], w_in_cols(2048, 2560), ("w", 1), "w1")
        wdma(wA[2], w_in_cols(1024, 1536), ("w", 2), "w2")
        wdma(wA[0], w_in_cols(1536, 2048), ("w", 0), "w0")
        gv_tok = A.view("gv_tok", BF16, [16, 512])
        grT = A.view("grT", F32, [S])
        v_tok = A.view("v_tok", BF16, [16, 512], top=True)
        psrot = [0]

        def tm_group(tkb, wtile, tok_w, dst, dst_name, evac, bank_base):
            b = bank_base + psrot[0] % 4
            psrot[0] += 1
            tsl = slice(tkb * 128, (tkb + 1) * 128)
            for kc in range(KC):
                p.op("pe", lambda e, kc=kc: e.matmul(bank(b), lhsT=hT[:, kc, tsl], rhs=wtile[:, kc, :], start=(kc == 0), stop=(kc == KC - 1)),
                     reads=[tok_w, ("hT", tkb // 4)], writes=[PS(b)])
            if evac == "act":
                p.op("act", lambda e: e.activation(out=dst[:, tkb, :], in_=bank(b), func=AF.Copy), reads=[PS(b)], writes=[(dst_name, tkb)])
            else:
                p.op("dve", lambda e: e.tensor_copy(out=dst[:, tkb, :], in_=bank(b)), reads=[PS(b)], writes=[(dst_name, tkb)])
        hT = A.view("hT", BF16, [KC, S])
        xs = [A.view(f"xs{i}", F32, [KC, 512]) for i in range(2)]
        sq1 = A.view("sq1", BF16, [KC, 512])
        rs1 = A.view("rs1", F32, [512])
        def a1_proj(tb):
            tsl = slice(tb * 512, (tb + 1) * 512)
            for kc in range(KC):
                p.op("pe", lambda e, kc=kc: e.matmul(bank(2)[0:16, :], lhsT=w_r[:, kc, 0:16], rhs=hT[:, kc, tsl], start=(kc == 0), stop=(kc == KC - 1)),
                     reads=[("w_r",), ("hT", tb)], writes=[PS(2)])
            p.op("act", lambda e: e.activation(out=grT[0:16, tb * 512:(tb + 1) * 512], in_=bank(2)[0:16, :], func=AF.Copy),
                 reads=[PS(2)], writes=[("grT", tb)])
            for i in range(4):
                tm_group(4 * tb + i, wA[1], ("w", 1), gv_tok, "gv_tok", "dve", 4)

        for tb in range(4):
            sl = tb % 2
            tsl = slice(tb * 512, (tb + 1) * 512)
            p.op("sp", lambda e, sl=sl, tsl=tsl: e.dma_start(out=xs[sl], in_=xT[:, :, tsl]), writes=[("xs", sl)], dma=f"xs{sl}")
            p.op("act", lambda e, sl=sl: e.activation(out=sq1, in_=xs[sl], func=AF.Square), reads=[("xs", sl)], writes=[("sq1",)])
            b = tb % 2
            for kc in range(KC):
                p.op("pe", lambda e, kc=kc, b=b: e.matmul(bank(b), lhsT=ones_bf, rhs=sq1[:, kc, :], start=(kc == 0), stop=(kc == KC - 1)),
                     reads=[("ones",), ("sq1",)], writes=[PS(b)])
            rstd_from_ss(bank(b), rs1, D, [PS(b)], [("rs1",)])
            for kc in range(KC):
                eng = "dve" if kc % 2 == 0 else "dve"
                p.op(eng, lambda e, kc=kc, sl=sl, tsl=tsl: e.scalar_tensor_tensor(
                    out=hT[:, kc, tsl], in0=xs[sl][:, kc, :], scalar=cst[:, C_LN1 + kc:C_LN1 + kc + 1], in1=rs1,
                    op0=ALU.mult, op1=ALU.mult),
                    reads=[("xs", sl), ("rs1",), ("cst",)], writes=[("hT", tb, kc)])
            if tb >= 1:
                a1_proj(tb - 1)
        a1_proj(3)
        dbg("hT", hT, [("hT",)])
        p.barrier()
        A.release("xs0", "xs1", "sq1", "rs1", "lamtmp")
        if stop_after == "A1":
            return finish(nc, p, es, None)


        def proj_fm(wtile, c0, ncols, tok_w, consume):
            for tb in range(4):
                b = psrot[0] % 4
                psrot[0] += 1
                tsl = slice(tb * 512, (tb + 1) * 512)
                for kc in range(KC):
                    p.op("pe", lambda e, kc=kc, b=b, tsl=tsl: e.matmul(bank(b)[0:ncols, :], lhsT=wtile[:, kc, c0:c0 + ncols], rhs=hT[:, kc, tsl],
                                                                      start=(kc == 0), stop=(kc == KC - 1)),
                         reads=[tok_w, ("hT", tb)], writes=[PS(b)])
                consume(tb, bank(b)[0:ncols, :], b)

        def proj_tm(wtile, tok_w, consume):
            for tkb in range(16):
                b = psrot[0] % 4
                psrot[0] += 1
                tsl = slice(tkb * 128, (tkb + 1) * 128)
                for kc in range(KC):
                    p.op("pe", lambda e, kc=kc, b=b, tsl=tsl: e.matmul(bank(b), lhsT=hT[:, kc, tsl], rhs=wtile[:, kc, :],
                                                                      start=(kc == 0), stop=(kc == KC - 1)),
                         reads=[tok_w, ("hT", tkb // 4)], writes=[PS(b)])
                consume(tkb, bank(b), b)


        q_decT = A.view("q_decT", BF16, [2, S])
        k_decT = A.view("k_decT", BF16, [2, S])
        kteT = A.view("kteT", BF16, [2, S])
        kte_tok = A.view("kte_tok", BF16, [16, 256])
        sgo = A.view("sgo", BF16, [4, S])
        resetm = A.view("resetm", F32, [S])
        T0 = A.view("T0", F32, [S])
        T1 = A.view("T1", F32, [S])
        T2 = A.view("T2", F32, [S])
        ebT = A.view("ebT", F32, [S])
        enbT = A.view("enbT", F32, [S])
        ek2T = A.view("ek2T", F32, [S])
        ebl = A.view("ebl", F32, [2, 32])

        p.op("pool", lambda e: e.memset(resetm, 1.0), writes=[("resetm",)])
        p.op("pool", lambda e: e.memset(resetm[:, 0:S:64], 0.0), writes=[("resetm",)])


        def go_groups(c):
            out = []
            for c4 in (2 * c, 2 * c + 1):
                for tb in range(4):
                    def g(c4=c4, tb=tb):
                        b = psrot[0] % 4
                        psrot[0] += 1
                        tsl = slice(tb * 512, (tb + 1) * 512)
                        for kc in range(KC):
                            p.op("pe", lambda e, kc=kc: e.matmul(bank(b), lhsT=wA[2][:, kc, c4 * 128:(c4 + 1) * 128], rhs=hT[:, kc, tsl],
                                                                 start=(kc == 0), stop=(kc == KC - 1)),
                                 reads=[("w", 2), ("hT", tb)], writes=[PS(b)])
                        p.op("act", lambda e: e.activation(out=sgo[:, c4, tsl], in_=bank(b), func=AF.Copy), reads=[PS(b)], writes=[("sgo", c4, tb)])
                    out.append(g)
            return out

        fill = []

        def FILL(n=2):
            for _ in range(n):
                if fill:
                    fill.pop(0)()

        for c in range(2):
            if c == 0:
                fill.extend([(lambda tkb=tkb: tm_group(tkb, wA[2], ("w", 2), v_tok, "v_tok", "act", 0)) for tkb in range(16)])
            else:
                fill.extend(go_groups(0) + go_groups(1))
            for tb in range(4):
                p.op("pe", lambda e, tb=tb, c=c: e.matmul(bank(4 + tb), lhsT=wg2_sb[0:16, c * 128:(c + 1) * 128],
                                                         rhs=grT[0:16, tb * 512:(tb + 1) * 512], start=True, stop=True),
                     reads=[("wg2",), ("grT", tb)], writes=[PS(4 + tb)])
            zps = ps[:, 4:8, :]
            T0v = T0.rearrange("p (a b) -> p a b", a=4)
            rd4 = [PS(4), PS(5), PS(6), PS(7)]
            p.op("act", lambda e, c=c: e.activation(out=T0v, in_=zps, func=AF.Identity, bias=cst[:, C_BG + c:C_BG + c + 1], scale=1.0),
                 reads=rd4 + [("cst",)], writes=[("T0",)])
            FILL()
            p.op("act", lambda e: e.activation(out=T1, in_=T0, func=AF.Abs), reads=[("T0",)], writes=[("T1",)])
            FILL()
            p.op("act", lambda e: e.activation(out=T1, in_=T1, func=AF.Exp, scale=-1.0), reads=[("T1",)], writes=[("T1",)])
            FILL()
            p.op("act", lambda e: e.activation(out=T1, in_=T1, func=AF.Ln, bias=one_ap, scale=1.0), reads=[("T1",), ("misc", "one")], writes=[("T1",)])
            FILL()
            p.op("dve", lambda e: e.tensor_scalar(out=T2, in0=T0, scalar1=0.0, scalar2=1.0 / 16.0, op0=ALU.min, op1=ALU.mult),
                 reads=[("T0",)], writes=[("T2",)])
            p.op("dve", lambda e: e.scalar_tensor_tensor(out=T1, in0=T1, scalar=-1.0 / 16.0, in1=T2, op0=ALU.mult, op1=ALU.add),
                 reads=[("T1",), ("T2",)], writes=[("T1",)])
            FILL()
            p.op("dve", lambda e: e.tensor_tensor_scan(out=T0, data0=resetm, data1=T1, initial=0.0, op0=ALU.mult, op1=ALU.add),
                 reads=[("T1",), ("resetm",)], writes=[("T0",)])
            FILL()
            p.op("act", lambda e: e.activation(out=ebT, in_=T0, func=AF.Exp), reads=[("T0",)], writes=[("ebT",)])
            p.op("act", lambda e: e.activation(out=enbT, in_=T0, func=AF.Exp, scale=-1.0), reads=[("T0",)], writes=[("enbT",)])
            FILL()
            bl = T0[:, 63:S:64].unsqueeze(2).broadcast_to([128, 32, 64])
            p.op("dve", lambda e, bl=bl: e.tensor_tensor(out=T2.rearrange("p (a b) -> p a b", a=32), in0=bl,
                                                        in1=T0.rearrange("p (a b) -> p a b", a=32), op=ALU.subtract),
                 reads=[("T0",)], writes=[("T2",)])
            p.op("act", lambda e: e.activation(out=ek2T, in_=T2, func=AF.Exp), reads=[("T2",)], writes=[("ek2T",)])
            p.op("dve", lambda e, c=c: e.tensor_copy(out=ebl[:, c, :], in_=ebT[:, 63:S:64]), reads=[("ebT",)], writes=[("ebl", c)])
            while fill:
                FILL()
            if c == 0:
                wdma(wA[2], w_in_cols(2576, 3088), ("w", 2), "w2")
            if c == 0:
                dbg("bT", T0, [("T0",)])
                dbg("ebT", ebT, [("ebT",)])

            def cons_q(tb, pap, b, c=c):
                tsl = slice(tb * 512, (tb + 1) * 512)
                p.op("dve", lambda e: e.scalar_tensor_tensor(out=q_decT[:, c, tsl], in0=pap, scalar=0.125, in1=ebT[:, tsl],
                                                             op0=ALU.mult, op1=ALU.mult),
                     reads=[PS(b), ("ebT",)], writes=[("q_decT", c, tb)])
            proj_fm(wA[0], c * 128, 128, ("w", 0), cons_q)

            def cons_k(tb, pap, b, c=c):
                tsl = slice(tb * 512, (tb + 1) * 512)
                p.op("dve", lambda e: e.tensor_tensor(out=k_decT[:, c, tsl], in0=pap, in1=enbT[:, tsl], op=ALU.mult),
                     reads=[PS(b), ("enbT",)], writes=[("k_decT", c, tb)])
                p.op("dve", lambda e: e.tensor_tensor(out=kteT[:, c, tsl], in0=pap, in1=ek2T[:, tsl], op=ALU.mult),
                     reads=[PS(b), ("ek2T",)], writes=[("kteT", c, tb)])
            proj_fm(wA[0], 256 + c * 128, 128, ("w", 0), cons_k)

        for c in range(2):
            for g in range(4):
                b = psrot[0] % 4
                psrot[0] += 1
                pst = bank(b)[:, 0:256].bitcast(BF16).rearrange("p (a b) -> p a b", a=4)
                for i in range(4):
                    tkb = g * 4 + i
                    p.op("pe", lambda e, i=i, tkb=tkb, c=c, pst=pst: e.transpose(out=pst[:, i, :], in_=kteT[:, c, tkb * 128:(tkb + 1) * 128], identity=ident),
                         reads=[("kteT", c, tkb // 4), ("ident",)], writes=[PS(b)])
                p.op("act", lambda e, g=g, c=c, pst=pst: e.activation(out=kte_tok[:, g * 4:(g + 1) * 4, c * 128:(c + 1) * 128], in_=pst, func=AF.Copy),
                     reads=[PS(b)], writes=[("kte_tok", c, g)])


        wdma(wA[0], w_in_cols(0, 512), ("w", 0), "w0")
        wdma(wA[1], w_in_cols(512, 1024), ("w", 1), "w1")

        dbg("q_decT", q_decT, [("q_decT",)])
        dbg("k_decT", k_decT, [("k_decT",)])
        dbg("kte_tok", kte_tok, [("kte_tok",)])
        dbg("gv_tok", gv_tok, [("gv_tok",)])
        dbg("sgo", sgo, [("sgo",)])
        dbg("ebl", ebl, [("ebl",)])
        p.barrier(exempt=("dma:w0", "dma:w1", "dma:w2"))
        A.release("kteT", "grT", "resetm", "T0", "T1", "T2", "ebT", "enbT", "ek2T")
        if stop_after == "A2":
            return finish(nc, p, es, None)

        cc = A.view("cc", BF16, [KC, S], top=True)
        Sbf = A.view("Sbf", BF16, [33, 2, 128])
        aT_halves = [A.view("aT_lo", BF16, [8, 4, 128]), A.view("aT_hi", BF16, [8, 4, 128])]

        def aTv(tkb):
            return aT_halves[tkb // 8][:, tkb % 8]
        Sst2 = A.view("Sst", F32, [2, 2, 128])
        e_sq = A.view("e_sq", BF16, [512], top=True)
        e_rs = A.view("e_rs", F32, [512], top=True)
        e_sq1 = A.view("e_t", BF16, [512], top=True)
        e_rs1 = A.view("e_rs1", F32, [512], top=True)

        for tkb in range(16):
            pair = (tkb % 2) * 2
            tsl = slice(tkb * 128, (tkb + 1) * 128)
            for h in range(4):
                c, hh = h // 2, h % 2
                rows = slice(hh * 64, (hh + 1) * 64)
                b = pair + hh
                p.op("pe", lambda e, b=b, h=h, c=c, rows=rows, tsl=tsl: e.matmul(bank(b)[:, c * 128:(c + 1) * 128], lhsT=k_decT[rows, c, tsl],
                                                                                  rhs=q_decT[rows, c, tsl], start=True, stop=True),
                     reads=[("k_decT", c, tkb // 4), ("q_decT", c, tkb // 4)], writes=[PS(b)])
            for hh in range(2):
                b = pair + hh
                p.op("dve", lambda e, b=b, tkb=tkb, hh=hh: e.tensor_tensor(out=aTv(tkb)[:, hh:4:2, :], in0=bank(b)[:, 0:256].rearrange("p (a b) -> p a b", a=2),
                                                                       in1=mask2.unsqueeze(1).broadcast_to([128, 2, 128]), op=ALU.mult),
                     reads=[PS(b), ("mask2",)], writes=[("aT", tkb, hh)])

        for c4 in range(4):
            p.op("act", lambda e, c4=c4: e.activation(out=sgo[:, c4, :], in_=sgo[:, c4, :], func=AF.Silu), reads=[("sgo", c4)], writes=[("sgo", c4)])

        p.op("pool", lambda e: e.memset(Sbf[:, 0, :, :], 0.0), writes=[("Sbf", 0)])
        p.op("pool", lambda e: e.memset(Sst2[:, 0, :, :], 0.0), writes=[("Sst", 0)])
        for tkb in range(16):
            bp = 4 + 2 * (tkb % 2)
            for c in range(2):
                for hh in range(2):
                    h = 2 * c + hh
                    orow = slice(hh * 64, (hh + 1) * 64)
                    for s in range(2):
                        trow = slice(s * 64, (s + 1) * 64)
                        p.op("pe", lambda e, c=c, hh=hh, h=h, trow=trow, orow=orow, s=s, bp=bp, tkb=tkb: e.matmul(
                            bank(bp + s)[orow, c * 128:(c + 1) * 128], lhsT=kte_tok[trow, tkb, c * 128 + hh * 64:c * 128 + hh * 64 + 64],
                            rhs=gv_tok[trow, tkb, h * 128:(h + 1) * 128], start=True, stop=True),
                            reads=[("kte_tok", c, tkb // 4), ("gv_tok", tkb)], writes=[PS(bp + s)])
            for s in range(2):
                n = 2 * tkb + s
                so, sn = n % 2, (n + 1) % 2
                for c in range(2):
                    p.op("dve", lambda e, c=c, n=n, so=so, sn=sn, s=s, bp=bp: e.scalar_tensor_tensor(out=Sst2[:, sn, c, :], in0=Sst2[:, so, c, :], scalar=ebl[:, c, n:n + 1],
                                                                                            in1=bank(bp + s)[:, c * 128:(c + 1) * 128], op0=ALU.mult, op1=ALU.add),
                         reads=[PS(bp + s), ("Sst", so, c), ("ebl", c)], writes=[("Sst", sn, c)])
                p.op("act", lambda e, n=n, sn=sn: e.activation(out=Sbf[:, n + 1, :, :], in_=Sst2[:, sn, :, :], func=AF.Copy), reads=[("Sst", sn)], writes=[("Sbf", n + 1)])

        it = 0
        epi_prev = []

        def a3_epi(tb, h):
            b = h
            tsl = slice(tb * 512, (tb + 1) * 512)
            k = it_box[0] % 2
            it_box[0] += 1
            sq_, rs_ = (e_sq, e_rs) if k == 0 else (e_sq1, e_rs1)
            tq, tr = ("e_sqA", k), ("e_rsA", k)
            bs = 6 + k

            def part1():
                p.op("act", lambda e: e.activation(out=sq_, in_=bank(b), func=AF.Square), reads=[PS(b)], writes=[tq])

            def part2():
                p.op("pe", lambda e: e.matmul(bank(bs), lhsT=ones_bf, rhs=sq_, start=True, stop=True), reads=[tq, ("ones",)], writes=[PS(bs)])
                rstd_from_ss(bank(bs), rs_, 128, [PS(bs)], [tr])
                p.op("dve", lambda e: e.scalar_tensor_tensor(out=rs_, in0=bank(b), scalar=cst[:, C_GNW:C_GNW + 1], in1=rs_, op0=ALU.mult, op1=ALU.mult),
                     reads=[PS(b), tr, ("cst",)], writes=[tr])
                p.op("dve", lambda e: e.tensor_tensor(out=cc[:, 4 + h, tsl], in0=rs_, in1=sgo[:, h, tsl], op=ALU.mult),
                     reads=[tr, ("sgo", h, tb)], writes=[("cc", 4 + h, tb)])
            return part1, part2

        it_box = [0]
        carry = None
        for tb in range(4):
            for c in range(2):
                hs = (2 * c, 2 * c + 1)
                for i in range(4):
                    tkb = tb * 4 + i
                    osl = slice(i * 128, (i + 1) * 128)
                    for h in hs:
                        p.op("pe", lambda e, h=h, osl=osl, tkb=tkb: e.matmul(bank(h)[:, osl], lhsT=gv_tok[:, tkb, h * 128:(h + 1) * 128], rhs=aTv(tkb)[:, h, :], start=True, stop=False),
                             reads=[("gv_tok", tkb), ("aT", tkb)], writes=[PS(h)])
                    for s in range(2):
                        n = 2 * tkb + s
                        o2 = slice(i * 128 + s * 64, i * 128 + s * 64 + 64)
                        t2 = slice(tkb * 128 + s * 64, tkb * 128 + s * 64 + 64)
                        for h in hs:
                            rows = slice((h % 2) * 64, (h % 2) * 64 + 64)
                            p.op("pe", lambda e, h=h, rows=rows, s=s, n=n, o2=o2, t2=t2, c=c: e.matmul(bank(h)[:, o2], lhsT=Sbf[rows, n, c, :], rhs=q_decT[rows, c, t2],
                                                                                              start=False, stop=(s == 1)),
                                 reads=[("Sbf", n), ("q_decT", c, tkb // 4)], writes=[PS(h)])
                for h in hs:
                    p1, p2 = a3_epi(tb, h)
                    p1()
                    if carry is not None:
                        carry()
                    carry = p2
        if carry is not None:
            carry()
        p.barrier(exempt=("dma:w0", "dma:w1", "dma:w2"))
        A.release("aT_lo", "aT_hi", "Sbf", "Sst", "q_decT", "k_decT", "kte_tok", "gv_tok", "sgo", "ebl")
        if stop_after == "A3":
            return finish(nc, p, es, (cc, "cc"), dbg_out)

        qT = A.view("qT", BF16, [4, S])
        kT = A.view("kT", BF16, [4, S])
        flip = [0]

        def cons_qk(dst, name, h):
            def cons(tb, pap, b):
                flip[0] ^= 1
                if flip[0]:
                    p.op("act", lambda e: e.activation(out=dst[:, h, tb * 512:(tb + 1) * 512], in_=pap, func=AF.Copy), reads=[PS(b)], writes=[(name, h, tb)])
                else:
                    p.op("dve", lambda e: e.tensor_copy(out=dst[:, h, tb * 512:(tb + 1) * 512], in_=pap), reads=[PS(b)], writes=[(name, h, tb)])
            return cons

        def cons_dv(tkb, pap, b):
            flip[0] ^= 1
            if flip[0]:
                p.op("act", lambda e: e.activation(out=v_tok[:, tkb, :], in_=pap, func=AF.Copy), reads=[PS(b)], writes=[("v_tok", tkb)])
            else:
                p.op("dve", lambda e: e.tensor_copy(out=v_tok[:, tkb, :], in_=pap), reads=[PS(b)], writes=[("v_tok", tkb)])
        for h in range(4):
            proj_fm(wA[0], h * 128, 128, ("w", 0), cons_qk(qT, "qT", h))
            proj_fm(wA[1], h * 128, 128, ("w", 1), cons_qk(kT, "kT", h))
        w_o = wview(0, [KC, D])
        wdma(w_o, w_out.rearrange("(kc p) n -> p kc n", p=128), None, "w0")

        def filler_ops(hn):
            out = []
            for wi, dst, name in ((0, qT, "qT"), (1, kT, "kT")):
                for tb in range(4):
                    tsl = slice(tb * 512, (tb + 1) * 512)
                    for kc in range(KC):
                        def mo(wi=wi, dst=dst, name=name, tb=tb, tsl=tsl, kc=kc):
                            p.op("pe", lambda e: e.matmul(bank(7), lhsT=wA[wi][:, kc, hn * 128:(hn + 1) * 128], rhs=hT[:, kc, tsl],
                                                          start=(kc == 0), stop=(kc == KC - 1)),
                                 reads=[("w", wi), ("hT", tb)], writes=[PS(7)])
                            if kc == KC - 1:
                                p.op("dve", lambda e: e.tensor_copy(out=dst[:, hn, tsl], in_=bank(7)), reads=[PS(7)], writes=[(name, hn, tb)])
                        out.append(mo)
            return out

        NP = 4
        p_sb = A.view("p_sb", BF16, [NP, 2, 512])
        a_r = A.view("a_r", F32, [2, 512])
        a_o = A.view("a_o", F32, [512])
        xsB = [A.view(f"xs{i}", F32, [KC, 512], top=True) for i in range(2)]
        o_s = A.view("o_s", F32, [2, 512])
        item = [0]
        pending = []

        def flush_one():
            if pending:
                pending.pop(0)()

        blocks = [(h, qb) for h in range(4) for qb in range(4)]
        steps = [(bi, kb) for bi, (h, qb) in enumerate(blocks) for kb in range(4 * (qb + 1))]

        def emit_S(step):
            bi, kb = step
            h, qb = blocks[bi]
            qs = qb * 512
            j = kb - 4 * qb
            q0 = 128 * j if j > 0 else 0
            sl_ = item[0] % NP
            sbk = 2 * (item[0] % 2)
            item[0] += 1
            for c in range(2):
                rows = slice(c * 64, (c + 1) * 64)
                p.op("pe", lambda e, c=c, rows=rows: e.matmul(bank(sbk + c)[:, q0:512], lhsT=kT[rows, h, kb * 128:(kb + 1) * 128],
                                                              rhs=qT[rows, h, qs + q0:qs + 512], start=True, stop=True),
                     reads=[("kT", h, kb // 4), ("qT", h, qb)], writes=[PS(sbk + c)])
            for c in range(2):
                p.op("act", lambda e, c=c: e.activation(out=p_sb[:, sl_, c, q0:512], in_=bank(sbk + c)[:, q0:512], func=AF.Exp, scale=0.125),
                     reads=[PS(sbk + c)], writes=[("p_sb", sl_, c)])
                if j >= 0:
                    d0 = 128 * j
                    p.op("pool", lambda e, c=c, d0=d0: e.tensor_tensor(out=p_sb[:, sl_, c, d0:d0 + 128], in0=p_sb[:, sl_, c, d0:d0 + 128], in1=maskT, op=ALU.mult),
                         reads=[("p_sb", sl_, c), ("maskT",)], writes=[("p_sb", sl_, c)])
            return (sl_, q0)

        def emit_AV(step, res):
            bi, kb = step
            h, qb = blocks[bi]
            nkb = 4 * (qb + 1)
            sl_, q0 = res
            st, sp_ = (kb == 0), (kb == nkb - 1)
            for c in range(2):
                p.op("pe", lambda e, c=c: e.matmul(bank(4 + c)[:, q0:512], lhsT=v_tok[:, kb, h * 128:(h + 1) * 128], rhs=p_sb[:, sl_, c, q0:512], start=st, stop=sp_),
                     reads=[("v_tok", kb), ("p_sb", sl_, c)], writes=[PS(4 + c)])
                p.op("pe", lambda e, c=c: e.matmul(bank(6 + c)[:, q0:512], lhsT=ones_bf, rhs=p_sb[:, sl_, c, q0:512], start=st, stop=sp_),
                     reads=[("ones",), ("p_sb", sl_, c)], writes=[PS(6 + c)])

        def make_epilogue(bi):
            h, qb = blocks[bi]
            tsl = slice(qb * 512, qb * 512 + 512)

            def s2():
                p.op("act", lambda e: e.activation(out=a_r, in_=a_r, func=AF.Exp, scale=-1.0), reads=[("a_r",)], writes=[("a_r",)])

            def s3():
                p.op("dve", lambda e: e.tensor_tensor(out=o_s, in0=o_s, in1=a_r, op=ALU.mult), reads=[("o_s",), ("a_r",)], writes=[("o_s",)])

            def s4():
                p.op("dve", lambda e: e.scalar_tensor_tensor(out=a_o, in0=o_s[:, 1, :], scalar=neg_lam, in1=o_s[:, 0, :], op0=ALU.mult, op1=ALU.add),
                     reads=[("o_s",), ("misc", "nl")], writes=[("a_o",)])
                p.op("dve", lambda e: e.tensor_tensor(out=e_sq, in0=a_o, in1=a_o, op=ALU.mult), reads=[("a_o",)], writes=[("e_sq",)])

            def s5():
                b_ = 2 * ((item[0] + 1) % 2)
                p.op("pe", lambda e: e.matmul(bank(b_), lhsT=ones_bf, rhs=e_sq, start=True, stop=True), reads=[("e_sq",), ("ones",)], writes=[PS(b_)])
                rstd_from_ss(bank(b_), e_rs, 128, [PS(b_)], [("e_rs",)])

            def s7():
                p.op("dve", lambda e: e.scalar_tensor_tensor(out=cc[:, h, tsl], in0=a_o, scalar=w8, in1=e_rs, op0=ALU.mult, op1=ALU.mult),
                     reads=[("a_o",), ("e_rs",), ("misc", "w8")], writes=[("cc", h, qb)])
            return [s2, s3, s4, s5, s7]

        prev = emit_S(steps[0])
        for si, step in enumerate(steps):
            bi, kb = step
            nkb = 4 * (blocks[bi][1] + 1)
            nxt = emit_S(steps[si + 1]) if si + 1 < len(steps) else None
            emit_AV(step, prev)
            prev = nxt
            for _ in range(2 if nkb == 4 else 1):
                flush_one()
            if kb == nkb - 1:
                while pending:
                    flush_one()
                for c in range(2):
                    p.op("dve", lambda e, c=c: e.tensor_copy(out=o_s[:, c, :], in_=bank(4 + c)), reads=[PS(4 + c)], writes=[("o_s", c)])
                    p.op("act", lambda e, c=c: e.activation(out=a_r[:, c, :], in_=bank(6 + c), func=AF.Ln), reads=[PS(6 + c)], writes=[("a_r", c)])
                pending = make_epilogue(bi)
        while pending:
            flush_one()
        for tb_ in range(2):
            p.op("sp", lambda e, tb_=tb_: e.dma_start(out=xsB[tb_], in_=xT[:, :, tb_ * 512:(tb_ + 1) * 512]), writes=[("xs", tb_)], dma=f"xs{tb_}")
        p.barrier(exempt=("dma:w0", "dma:w1", "dma:w2", "dma:xs0", "dma:xs1"))
        A.release("qT", "kT", "v_tok", "p_sb", "a_r", "a_o", "e_t", "e_rs1", "hT", "o_s")
        if stop_after == "A5":
            return finish(nc, p, es, (cc, "cc"), dbg_out)

        dbg("w_o", w_o, [("w",)])
        dbg("cc", cc, [("cc",)])
        x1T = A.view("x1T", F32, [KC, S])
        rstd2 = A.view("rstd2", F32, [S])
        sqB = A.view("sq1", BF16, [KC, 512])
        h2 = A.view("h2", BF16, [KC, 1024])
        wu = [wview(16384, [KC, 2, 128]), wview(20480, [KC, 2, 128]), wview(0, [KC, 2, 128])]
        wd = [wview(4096 + i * 5632, [NFF, 128]) for i in range(2)]

        def load_wu(ffc, slot):
            for g in range(2):
                c0 = g * DFF + ffc * 128
                wdma(wu[slot][:, :, g, :], w_up[:, c0:c0 + 128].rearrange("(kc p) n -> p kc n", p=128), ("wu", slot, g), f"wu{slot}{g}")

        def load_wd(dmc, slot):
            wdma(wd[slot], w_down[:, dmc * 128:(dmc + 1) * 128].rearrange("(f p) n -> p f n", p=128), ("wd", slot), f"wd{slot}")

        load_wu(0, 0)
        load_wu(1, 1)

        def h2_compute(th):
            t0 = th * 1024
            for kc in range(KC):
                for tb2 in range(2):
                    tsl = slice(t0 + tb2 * 512, t0 + (tb2 + 1) * 512)
                    p.op("dve", lambda e, kc=kc, tb2=tb2, tsl=tsl: e.scalar_tensor_tensor(
                        out=h2[:, kc, tb2 * 512:(tb2 + 1) * 512], in0=x1T[:, kc, tsl], scalar=cst[:, C_LN2 + kc:C_LN2 + kc + 1], in1=rstd2[:, tsl],
                        op0=ALU.mult, op1=ALU.mult),
                        reads=[("x1T", 2 * th + tb2, kc), ("rstd2", 2 * th + tb2), ("cst",)], writes=[("h2", kc, tb2)])

        def b_proj(tb):
            sl = tb % 2
            tsl = slice(tb * 512, (tb + 1) * 512)
            if tb >= 2:
                p.op("sp", lambda e: e.dma_start(out=xsB[sl], in_=xT[:, :, tsl]), writes=[("xs", sl)], dma=f"xs{sl}")
            for dmc in range(KC):
                b = psrot[0] % 4
                psrot[0] += 1
                for kc in range(KC):
                    p.op("pe", lambda e, b=b, kc=kc, dmc=dmc: e.matmul(bank(b), lhsT=w_o[:, kc, dmc * 128:(dmc + 1) * 128], rhs=cc[:, kc, tsl],
                                                                     start=(kc == 0), stop=(kc == KC - 1)),
                         reads=[("w", dmc // 4), ("cc",)], writes=[PS(b)])
                p.op("dve", lambda e, b=b, dmc=dmc: e.tensor_tensor(out=x1T[:, dmc, tsl], in0=bank(b), in1=xsB[sl][:, dmc, :], op=ALU.add),
                     reads=[PS(b), ("xs", sl)], writes=[("x1T", tb, dmc)])

        def b_stats(tb):
            tsl = slice(tb * 512, (tb + 1) * 512)
            p.op("act", lambda e: e.activation(out=sqB, in_=x1T[:, :, tsl], func=AF.Square), reads=[("x1T", tb)], writes=[("sq1",)])
            bs = 4 + tb % 2
            for kc in range(KC):
                p.op("pe", lambda e, kc=kc: e.matmul(bank(bs), lhsT=ones_bf, rhs=sqB[:, kc, :], start=(kc == 0), stop=(kc == KC - 1)),
                     reads=[("ones",), ("sq1",)], writes=[PS(bs)])
            rstd_from_ss(bank(bs), rstd2[:, tsl], D, [PS(bs)], [("rstd2", tb)])

        b_proj(0)
        for tb in range(1, 4):
            b_proj(tb)
            b_stats(tb - 1)
            if tb == 2:
                h2_compute(0)
        b_stats(3)
        p.barrier()
        A.release("cc", "xs0", "xs1", "e_sq", "e_rs")
        if stop_after == "B":
            return finish(nc, p, es, (x1T, "x1T"), dbg_out)

        actT = A.view("actT", BF16, [NFF, 1024])
        uh = A.view("uh", F32, [2 * NFF, 2])
        NCB = 2
        u_sb = [[A.view(f"u_sb{i}{g}", F32, [1026]) for g in range(2)] for i in range(NCB)]
        y_sb = [[A.view(f"y_sb{i}{g}", F32, [1024]) for g in range(2)] for i in range(NCB)]
        for i_ in range(NCB):
            for g_ in range(2):
                p.op("pool", lambda e, us=u_sb[i_][g_]: e.memset(us[:, 0:2], 0.0), writes=[("u_sb", i_, g_, "h")])

        rsf = A.view("rsf", F32, [512])
        fin_pending = []

        def final_stages(tb):
            tsl = slice(tb * 512, (tb + 1) * 512)

            def f1():
                p.op("act", lambda e: e.activation(out=sqB, in_=x1T[:, :, tsl], func=AF.Square), reads=[("x1T", tb)], writes=[("sq1",)])

            def f2():
                bs = psrot[0] % 8
                psrot[0] += 1
                for kc in range(KC):
                    p.op("pe", lambda e, kc=kc: e.matmul(bank(bs), lhsT=ones_bf, rhs=sqB[:, kc, :], start=(kc == 0), stop=(kc == KC - 1)),
                         reads=[("ones",), ("sq1",)], writes=[PS(bs)])
                rstd_from_ss(bank(bs), rsf, D, [PS(bs)], [("rsf",)])

            def f3():
                for kc in range(KC):
                    p.op("dve", lambda e, kc=kc: e.scalar_tensor_tensor(out=x1T[:, kc, tsl], in0=x1T[:, kc, tsl], scalar=cst[:, C_LNF + kc:C_LNF + kc + 1],
                                                                        in1=rsf, op0=ALU.mult, op1=ALU.mult),
                         reads=[("x1T", tb, kc), ("rsf",), ("cst",)], writes=[("x1T", tb, kc)])
                p.op("sp", lambda e: e.dma_start(out=outT[:, :, tsl], in_=x1T[:, :, tsl]), reads=[("x1T", tb)], writes=[("out", tb)], dma=f"out{tb % 2}")
            return [f1, f2, f3]

        pair_it = 0
        for th in range(2):
            t0 = th * 1024
            for ffc in range(NFF):
                slot = pair_it % 3
                cb = pair_it % NCB
                pair_it += 1
                if ffc + 2 < NFF:
                    load_wu(ffc + 2, (slot + 2) % 3)
                if ffc == NFF - 4:
                    load_wd(0, 0)
                    load_wd(1, 1)
                pb = 4 * (ffc % 2)
                for g in range(2):
                    for tb2 in range(2):
                        b = pb + 2 * g + tb2
                        for kc in range(KC):
                            p.op("pe", lambda e, b=b, kc=kc, g=g, tb2=tb2, slot=slot: e.matmul(bank(b), lhsT=wu[slot][:, kc, g, :], rhs=h2[:, kc, tb2 * 512:(tb2 + 1) * 512],
                                                                                             start=(kc == 0), stop=(kc == KC - 1)),
                                 reads=[("wu", slot, g), ("h2",)], writes=[PS(b)])
                for g in range(2):
                    ch = g * NFF + ffc
                    b0 = pb + 2 * g
                    pss = ps[:, b0:b0 + 2, :]
                    rd = [PS(b0), PS(b0 + 1)]
                    us = u_sb[cb][g]
                    ys = y_sb[cb][g]
                    cw = lambda i, ch=ch: cst[:, C_CW + ch * 3 + i:C_CW + ch * 3 + i + 1]
                    p.op("act", lambda e, us=us, pss=pss: e.activation(out=us[:, 2:1026].rearrange("p (a b) -> p a b", a=2), in_=pss, func=AF.Copy),
                         reads=rd, writes=[("u_sb", cb, g, "m")])
                    if th == 0:
                        p.op("act", lambda e, ch=ch, b0=b0: e.activation(out=uh[:, ch, :], in_=bank(b0 + 1)[:, 510:512], func=AF.Copy),
                             reads=[PS(b0 + 1)], writes=[("uh", ch)])
                    else:
                        p.op("dve", lambda e, us=us, ch=ch: e.tensor_copy(out=us[:, 0:2], in_=uh[:, ch, :]), reads=[("uh", ch)], writes=[("u_sb", cb, g, "h")])
                    p.op("act", lambda e, ys=ys, pss=pss, ch=ch, cw=cw: e.activation(out=ys.rearrange("p (a b) -> p a b", a=2), in_=pss, func=AF.Identity,
                                                                                    scale=cw(2), bias=cst[:, C_CB + ch:C_CB + ch + 1]),
                         reads=rd + [("cst",)], writes=[("y_sb", cb, g)])
                    p.op("dve", lambda e, ys=ys, us=us, cw=cw: e.scalar_tensor_tensor(out=ys, in0=us[:, 1:1025], scalar=cw(1), in1=ys, op0=ALU.mult, op1=ALU.add),
                         reads=[("u_sb", cb, g), ("y_sb", cb, g), ("cst",)], writes=[("y_sb", cb, g)])
                    p.op("dve", lambda e, ys=ys, us=us, cw=cw: e.scalar_tensor_tensor(out=ys, in0=us[:, 0:1024], scalar=cw(0), in1=ys, op0=ALU.mult, op1=ALU.add),
                         reads=[("u_sb", cb, g), ("y_sb", cb, g), ("cst",)], writes=[("y_sb", cb, g)])
                yg, yv = y_sb[cb][0], y_sb[cb][1]
                p.op("act", lambda e, yg=yg: e.activation(out=yg, in_=yg, func=AF.Silu), reads=[("y_sb", cb, 0)], writes=[("y_sb", cb, 0)])
                p.op("dve", lambda e, yg=yg, yv=yv, ffc=ffc: e.tensor_tensor(out=actT[:, ffc, :], in0=yg, in1=yv, op=ALU.mult),
                     reads=[("y_sb", cb, 0), ("y_sb", cb, 1)], writes=[("actT", ffc)])
            if th == 0:
                h2_compute(1)
                load_wu(0, (pair_it) % 3)
                load_wu(1, (pair_it + 1) % 3)
            def dn_mm(b, f, slot, tb2):
                p.op("pe", lambda e: e.matmul(bank(b), lhsT=wd[slot][:, f, :], rhs=actT[:, f, tb2 * 512:(tb2 + 1) * 512],
                                              start=(f == 0), stop=(f == NFF - 1)),
                     reads=[("wd", slot), ("actT", f)], writes=[PS(b)])

            def dn_evac(b, dmc, tb2):
                tsl = slice(t0 + tb2 * 512, t0 + (tb2 + 1) * 512)
                p.op("dve", lambda e: e.tensor_tensor(out=x1T[:, dmc, tsl], in0=bank(b), in1=x1T[:, dmc, tsl], op=ALU.add),
                     reads=[PS(b), ("x1T", 2 * th + tb2, dmc)], writes=[("x1T", 2 * th + tb2, dmc)])

            NL = 4
            first = []
            for dmc in (0, 1):
                for tb2 in (0, 1):
                    b = psrot[0] % 8
                    psrot[0] += 1
                    first.append((b, dmc, tb2))
                    for f in range(NFF - NL):
                        dn_mm(b, f, dmc % 2, tb2)
            for (b, dmc, tb2) in first:
                for f in range(NFF - NL, NFF):
                    dn_mm(b, f, dmc % 2, tb2)
                dn_evac(b, dmc, tb2)
            load_wd(2, 0)
            load_wd(3, 1)
            for dmc in range(2, KC):
                slot = dmc % 2
                for tb2 in (0, 1):
                    b = psrot[0] % 8
                    psrot[0] += 1
                    for f in range(NFF):
                        dn_mm(b, f, slot, tb2)
                    dn_evac(b, dmc, tb2)
                if dmc + 2 < KC:
                    load_wd(dmc + 2, slot)
                if fin_pending:
                    fin_pending.pop(0)()
            if th == 0:
                fin_pending.extend(final_stages(0) + final_stages(1))
            else:
                a2, a3 = final_stages(2), final_stages(3)
                for f in (a2[0], a2[1], a3[0], a2[2], a3[1], a3[2]):
                    f()
        return finish(nc, p, es, None, dbg_out)


def finish(nc, p, es, dump, dbg_out=None):
    if dump is not None and dbg_out:
        ap, name = dump
        if name in dbg_out:
            p.op("sp", lambda e: e.dma_start(out=dbg_out[name], in_=ap), reads=[(name,)], writes=[("dbg", name)], dma="dbg_" + name)
    fin = p.op("sp", lambda e: e.nop())
    for s, i in p.last.items():
        if s.startswith("dma:") and i != fin.id:
            fin.deps.add(i)
    p.finalize()
    sems = {}
    for s in p.streams:
        sems[s] = es.enter_context(nc.semaphore("s_" + s.replace(":", "_")))
    block = es.enter_context(nc.Block())
    p.emit(nc, block, sems)
    return nc


def make_inputs(x, ln1_w, w_in, diff_lq1, diff_lk1, diff_lq2, diff_lk2, diff_subln_w, gla_wg2, gla_bg,
                gla_norm_w, w_out, ln2_w, w_up, conv_w, conv_b, w_down, lnf_w):
    f = lambda a: np.ascontiguousarray(np.asarray(a, dtype=np.float32))
    cst = np.zeros((128, NCST), np.float32)
    cst[:, C_LN1:C_LN1 + 8] = f(ln1_w)[0].reshape(8, 128).T
    cst[:, C_LN2:C_LN2 + 8] = f(ln2_w)[0].reshape(8, 128).T
    cst[:, C_LNF:C_LNF + 8] = f(lnf_w).reshape(8, 128).T
    cst[:, C_BG:C_BG + 2] = f(gla_bg)[0].reshape(2, 128).T
    cst[:, C_SUB] = f(diff_subln_w)[0]
    cst[:, C_GNW] = f(gla_norm_w)[0]
    cst[:, C_CB:C_CB + 44] = f(conv_b)[0].reshape(44, 128).T
    cst[:, C_CW:C_CW + 132] = f(conv_w)[0].T.reshape(44, 128, 3).transpose(1, 0, 2).reshape(128, 132)
    cst[:, C_LQ1:C_LQ1 + 64] = f(diff_lq1)[0][None, :]
    cst[:, C_LK1:C_LK1 + 64] = f(diff_lk1)[0][None, :]
    cst[:, C_LQ2:C_LQ2 + 64] = f(diff_lq2)[0][None, :]
    cst[:, C_LK2:C_LK2 + 64] = f(diff_lk2)[0][None, :]
    shared = {"w_in": f(w_in)[0], "w_out": f(w_out)[0], "w_up": f(w_up)[0], "w_down": f(w_down)[0],
              "wg2": f(gla_wg2)[0], "cst": cst}
    xf = f(x)
    maps = []
    for b in range(xf.shape[0]):
        xt = np.ascontiguousarray(xf[b].T.reshape(KC, 128, S).transpose(1, 0, 2))
        m = dict(shared)
        m["xT"] = xt
        maps.append(m)
    return maps


_NC_CACHE = {}


def kernel(**inputs):
    maps = make_inputs(**inputs)
    if "nc" not in _NC_CACHE:
        _NC_CACHE["nc"] = build_nc()
    nc = _NC_CACHE["nc"]
    n = len(maps)
    res = run_bass_kernel_spmd(nc, maps, core_ids=list(range(n)))
    out = np.empty((n, S, D), np.float32)
    for b in range(n):
        o = res.results[b]["outT"]
        out[b] = o.transpose(1, 0, 2).reshape(D, S).T
    return out
```

```python
import numpy as np
import concourse.bass as bass
import concourse.mybir as mybir
from concourse.bass_utils import run_bass_kernel_spmd

F32 = mybir.dt.float32
BF16 = mybir.dt.bfloat16
AF = mybir.ActivationFunctionType
ALU = mybir.AluOpType

S = 2048
D = 1024
KC = 8
DFF = 2816
NFF = 22
EPS = 1e-6
SAME_ENGINE_SYNC = True

C_LN1, C_LN2, C_LNF, C_BG, C_SUB, C_GNW, C_CB, C_CW = 0, 8, 16, 24, 26, 27, 28, 72
C_LQ1, C_LK1, C_LQ2, C_LK2 = 204, 268, 332, 396
NCST = 460


class Op:
    __slots__ = ("id", "eng", "fn", "deps", "dma", "stream", "signal", "tick", "waits")

    def __init__(self, id, eng, fn, dma):
        self.id, self.eng, self.fn, self.dma = id, eng, fn, dma
        self.stream = ("dma:" + dma) if dma else eng
        self.deps = set()
        self.signal = False
        self.tick = 0
        self.waits = {}


class Prog:
    ENGS = ("pe", "act", "dve", "pool", "sp")

    def __init__(self):
        self.ops = []
        self.state = {}
        self.desc = {}
        self.last = {}
        self.barrier_op = None

    def _related(self, t):
        out = []
        for i in range(1, len(t) + 1):
            pre = t[:i]
            if pre in self.state:
                out.append(pre)
        for d in self.desc.get(t, ()):
            if d in self.state:
                out.append(d)
        return out

    def _ensure(self, t):
        if t not in self.state:
            self.state[t] = [None, {}]
            for i in range(1, len(t)):
                self.desc.setdefault(t[:i], set()).add(t)

    def op(self, eng, fn, reads=(), writes=(), dma=None, nobar=False):
        o = Op(len(self.ops), eng, fn, dma)
        deps = o.deps
        for t in reads:
            for r in self._related(t):
                w = self.state[r][0]
                if w is not None:
                    deps.add(w)
        for t in writes:
            for r in self._related(t):
                st = self.state[r]
                if st[0] is not None:
                    deps.add(st[0])
                deps.update(st[1].values())
        if self.barrier_op is not None and not nobar:
            deps.add(self.barrier_op)
        for t in reads:
            self._ensure(t)
            self.state[t][1][o.stream] = o.id
        for t in writes:
            for d in list(self.desc.get(t, ())):
                self.state.pop(d, None)
            self.desc.pop(t, None)
            self._ensure(t)
            self.state[t] = [o.id, {}]
        deps.discard(o.id)
        self.ops.append(o)
        self.last[o.stream] = o.id
        return o

    def barrier(self, exempt=()):
        o = self.op("pool", lambda e: e.nop())
        for s, i in self.last.items():
            if s not in exempt and i != o.id:
                o.deps.add(i)
        self.barrier_op = o.id
        return o

    def finalize(self):
        ops = self.ops
        for o in ops:
            for d in list(o.deps):
                do = ops[d]
                if do.stream == o.stream and not do.dma:
                    if o.eng == "pe" or o.eng == "sp" or not SAME_ENGINE_SYNC:
                        o.deps.discard(d)
                        continue
                do.signal = True
        cnt = {}
        for o in ops:
            if o.dma:
                o.signal = True
            if o.signal:
                inc = 16 if o.dma else 1
                cnt[o.stream] = cnt.get(o.stream, 0) + inc
                o.tick = cnt[o.stream]
        for s, v in cnt.items():
            assert v < 30000, (s, v)
        self.streams = sorted(cnt.keys())
        clock = {e: {} for e in self.ENGS}
        snap = {}
        for o in ops:
            need = {}
            for d in o.deps:
                do = ops[d]
                if need.get(do.stream, 0) < do.tick:
                    need[do.stream] = do.tick
            ck = clock[o.eng]
            for d in o.deps:
                do = ops[d]
                if ck.get(do.stream, 0) >= do.tick:
                    for k, v in snap[d].items():
                        if ck.get(k, 0) < v:
                            ck[k] = v
            waits = {}
            for k, v in need.items():
                if ck.get(k, 0) < v:
                    waits[k] = v
            for k, v in waits.items():
                ck[k] = v
            for d in o.deps:
                for k, v in snap[d].items():
                    if ck.get(k, 0) < v:
                        ck[k] = v
            o.waits = waits
            if o.signal:
                sn = dict(ck)
                sn[o.stream] = o.tick
                snap[o.id] = sn

    def emit(self, nc, block, sems):
        by_eng = {e: [o for o in self.ops if o.eng == e] for e in self.ENGS}

        def run(eng_name, e):
            for o in by_eng[eng_name]:
                for k, v in o.waits.items():
                    e.wait_ge(sems[k], v)
                inst = o.fn(e)
                if o.signal:
                    inst.then_inc(sems[o.stream], 16 if o.dma else 1)

        @block.tensor
        def _(e):
            run("pe", e)

        @block.scalar
        def _(e):
            run("act", e)

        @block.vector
        def _(e):
            run("dve", e)

        @block.gpsimd
        def _(e):
            run("pool", e)

        @block.sync
        def _(e):
            run("sp", e)


class Arena:
    def __init__(self, big, nbytes):
        self.big = big
        self.free = [(0, nbytes)]
        self.used = {}

    def alloc(self, name, nbytes, top=False):
        nbytes = (nbytes + 63) // 64 * 64
        if top:
            for i in range(len(self.free) - 1, -1, -1):
                o, n = self.free[i]
                if n >= nbytes:
                    self.free[i] = (o, n - nbytes)
                    self.used[name] = (o + n - nbytes, nbytes)
                    return o + n - nbytes
        for i, (o, n) in enumerate(self.free):
            if n >= nbytes:
                self.free[i] = (o + nbytes, n - nbytes)
                self.used[name] = (o, nbytes)
                return o
        raise RuntimeError(f"arena out of space for {name} {nbytes}: {self.free}")

    def release(self, *names):
        for name in names:
            o, n = self.used.pop(name)
            self.free.append((o, n))
        self.free.sort()
        m = []
        for o, n in self.free:
            if n == 0:
                continue
            if m and m[-1][0] + m[-1][1] == o:
                m[-1] = (m[-1][0], m[-1][1] + n)
            else:
                m.append((o, n))
        self.free = m

    def view(self, name, dtype, shape, parts=128, top=False):
        esz = 4 if dtype == F32 else 2
        n = int(np.prod(shape))
        o = self.alloc(name, n * esz, top=top)
        ap = self.big[0:parts, o // 4:(o + n * esz) // 4]
        if dtype != F32:
            ap = ap.bitcast(dtype)
        if len(shape) == 2:
            ap = ap.rearrange("p (a b) -> p a b", a=shape[0])
        elif len(shape) == 3:
            ap = ap.rearrange("p (a b c) -> p a b c", a=shape[0], b=shape[1])
        return ap


def build_nc(stop_after=None, debug=()):
    nc = bass.Bass("TRN2", target_bir_lowering=False)
    xT = nc.dram_tensor("xT", [128, KC, S], F32, kind="ExternalInput").ap()
    w_in = nc.dram_tensor("w_in", [D, 3088], F32, kind="ExternalInput").ap()
    w_out = nc.dram_tensor("w_out", [D, D], F32, kind="ExternalInput").ap()
    w_up = nc.dram_tensor("w_up", [D, 2 * DFF], F32, kind="ExternalInput").ap()
    w_down = nc.dram_tensor("w_down", [DFF, D], F32, kind="ExternalInput").ap()
    wg2 = nc.dram_tensor("wg2", [16, 256], F32, kind="ExternalInput").ap()
    cst_d = nc.dram_tensor("cst", [128, NCST], F32, kind="ExternalInput").ap()
    outT = nc.dram_tensor("outT", [128, KC, S], F32, kind="ExternalOutput").ap()
    dbg_out = {}
    for name, shape, dt in debug:
        dbg_out[name] = nc.dram_tensor("dbg_" + name, list(shape), dt, kind="ExternalOutput").ap()

    p = Prog()
    ARENA_BYTES = 207 * 1024
    import contextlib
    es = contextlib.ExitStack()
    with es:
        big = es.enter_context(nc.sbuf_tensor("big", [128, ARENA_BYTES // 4], F32))
        ps = es.enter_context(nc.psum_tensor("ps", [128, 8, 512], F32))
        A = Arena(big, ARENA_BYTES)

        def bank(b):
            return ps[:, b, :]

        PS = lambda b: ("ps", b)

        cst = A.view("cst", F32, [NCST])
        ones_bf = A.view("ones", BF16, [128])
        ident = A.view("ident", BF16, [128])
        maskT = A.view("maskT", BF16, [128])
        mask2 = A.view("mask2", BF16, [128])
        misc = A.view("misc", F32, [16])
        wg2_sb = A.view("wg2", F32, [256])
        wslot_bytes = 8192
        wsl = [A.alloc(f"wslot{i}", wslot_bytes) for i in range(3)]
        wbase = wsl[0]
        assert wsl[1] == wbase + wslot_bytes and wsl[2] == wbase + 2 * wslot_bytes

        def wview(byte_off, shape, dtype=BF16):
            n = int(np.prod(shape))
            esz = 2
            o = wbase + byte_off
            ap = big[:, o // 4:(o + n * esz) // 4].bitcast(dtype)
            if len(shape) == 2:
                ap = ap.rearrange("p (a b) -> p a b", a=shape[0])
            elif len(shape) == 3:
                ap = ap.rearrange("p (a b c) -> p a b c", a=shape[0], b=shape[1])
            return ap

        p.op("sp", lambda e: e.dma_start(out=cst, in_=cst_d), writes=[("cst",)], dma="cst")
        p.op("sp", lambda e: e.dma_start(out=wg2_sb[0:16, :], in_=wg2), writes=[("wg2",)], dma="wg2")
        p.op("pool", lambda e: e.memset(ones_bf, 1.0), writes=[("ones",)])
        p.op("pool", lambda e: e.affine_select(out=maskT, in_=ones_bf, pattern=[[1, 128]], compare_op=ALU.is_ge,
                                               fill=0.0, base=0, channel_multiplier=-1),
             reads=[("ones",)], writes=[("maskT",)])
        p.op("pool", lambda e: e.affine_select(out=ident, in_=ones_bf, pattern=[[1, 128]], compare_op=ALU.is_equal,
                                               fill=0.0, base=0, channel_multiplier=-1),
             reads=[("ones",)], writes=[("ident",)])
        p.op("pool", lambda e: e.tensor_copy(out=mask2, in_=maskT), reads=[("maskT",)], writes=[("mask2",)])
        p.op("pool", lambda e: e.memset(mask2[0:64, 64:128], 0.0), writes=[("mask2",)])
        neg_lam = misc[:, 0:1]
        w8 = misc[:, 1:2]
        lt = A.view("lamtmp", F32, [2, 64])
        p.op("dve", lambda e: e.tensor_tensor(out=lt[:, 0, :], in0=cst[:, C_LQ1:C_LQ1 + 64], in1=cst[:, C_LK1:C_LK1 + 64], op=ALU.mult),
             reads=[("cst",)], writes=[("lt", 0)])
        p.op("dve", lambda e: e.tensor_tensor(out=lt[:, 1, :], in0=cst[:, C_LQ2:C_LQ2 + 64], in1=cst[:, C_LK2:C_LK2 + 64], op=ALU.mult),
             reads=[("cst",)], writes=[("lt", 1)])
        p.op("dve", lambda e: e.reduce_sum(out=misc[:, 2:4], in_=lt, axis=mybir.AxisListType.X), reads=[("lt",)], writes=[("misc", "s")])
        p.op("act", lambda e: e.activation(out=misc[:, 4:6], in_=misc[:, 2:4], func=AF.Exp), reads=[("misc", "s")], writes=[("misc", "e")])
        p.op("dve", lambda e: e.tensor_tensor(out=misc[:, 6:7], in0=misc[:, 5:6], in1=misc[:, 4:5], op=ALU.subtract),
             reads=[("misc", "e")], writes=[("misc", "d")])
        p.op("dve", lambda e: e.tensor_scalar(out=neg_lam, in0=misc[:, 6:7], scalar1=-0.2, scalar2=None, op0=ALU.add),
             reads=[("misc", "d")], writes=[("misc", "nl")])
        p.op("dve", lambda e: e.tensor_scalar(out=w8, in0=cst[:, C_SUB:C_SUB + 1], scalar1=0.8, scalar2=None, op0=ALU.mult),
             reads=[("cst",)], writes=[("misc", "w8")])

        def wdma(dst, src, tok, key):
            toks = [("w", 0), ("w", 1)] if tok is None else [tok]
            p.op("pool", lambda e: e.dma_start(out=dst, in_=src), writes=toks, dma=key)

        def w_in_cols(c0, c1):
            return w_in[:, c0:c1].rearrange("(kc p) n -> p kc n", p=128)

        def rstd_from_ss(ss_ps_ap, out_ap, n, reads, writes):
            p.op("act", lambda e: e.activation(out=out_ap, in_=ss_ps_ap, func=AF.Ln, scale=1.0 / n, bias=eps_ap),
                 reads=reads + [("misc", "eps")], writes=writes)
            p.op("act", lambda e: e.activation(out=out_ap, in_=out_ap, func=AF.Exp, scale=-0.5),
                 reads=writes, writes=writes)

        eps_ap = misc[:, 8:9]
        one_ap = misc[:, 9:10]
        p.op("pool", lambda e: e.memset(eps_ap, EPS), writes=[("misc", "eps")])
        p.op("pool", lambda e: e.memset(one_ap, 1.0), writes=[("misc", "one")])

        def dbg(name, ap, reads):
            if name in dbg_out:
                p.op("sp", lambda e: e.dma_start(out=dbg_out[name], in_=ap), reads=reads, writes=[("dbg", name)], dma="dbg_" + name)

        wA = [wview(i * wslot_bytes, [KC, 512]) for i in range(3)]
        w_r = A.view("w_r", BF16, [KC, 16])
        wdma(w_r, w_in_cols(2560, 2576), ("w_r",), "wr")
        wdma(wA[1], w_in_cols(2048, 2560), ("w", 1), "w1")
        wdma(wA[2], w_in_cols(1024, 1536), ("w", 2), "w2")
        wdma(wA[0], w_in_cols(1536, 2048), ("w", 0), "w0")
        gv_tok = A.view("gv_tok", BF16, [16, 512])
        grT = A.view("grT", F32, [S])
        v_tok = A.view("v_tok", BF16, [16, 512], top=True)
        psrot = [0]

        def tm_group(tkb, wtile, tok_w, dst, dst_name, evac, bank_base):
            b = bank_base + psrot[0] % 4
            psrot[0] += 1
            tsl = slice(tkb * 128, (tkb + 1) * 128)
            for kc in range(KC):
                p.op("pe", lambda e, kc=kc: e.matmul(bank(b), lhsT=hT[:, kc, tsl], rhs=wtile[:, kc, :], start=(kc == 0), stop=(kc == KC - 1)),
                     reads=[tok_w, ("hT", tkb // 4)], writes=[PS(b)])
            if evac == "act":
                p.op("act", lambda e: e.activation(out=dst[:, tkb, :], in_=bank(b), func=AF.Copy), reads=[PS(b)], writes=[(dst_name, tkb)])
            else:
                p.op("dve", lambda e: e.tensor_copy(out=dst[:, tkb, :], in_=bank(b)), reads=[PS(b)], writes=[(dst_name, tkb)])
        hT = A.view("hT", BF16, [KC, S])
        xs = [A.view(f"xs{i}", F32, [KC, 512]) for i in range(2)]
        sq1 = A.view("sq1", BF16, [KC, 512])
        rs1 = A.view("rs1", F32, [512])
        def a1_proj(tb):
            tsl = slice(tb * 512, (tb + 1) * 512)
            for kc in range(KC):
                p.op("pe", lambda e, kc=kc: e.matmul(bank(2)[0:16, :], lhsT=w_r[:, kc, 0:16], rhs=hT[:, kc, tsl], start=(kc == 0), stop=(kc == KC - 1)),
                     reads=[("w_r",), ("hT", tb)], writes=[PS(2)])
            p.op("act", lambda e: e.activation(out=grT[0:16, tb * 512:(tb + 1) * 512], in_=bank(2)[0:16, :], func=AF.Copy),
                 reads=[PS(2)], writes=[("grT", tb)])
            for i in range(4):
                tm_group(4 * tb + i, wA[1], ("w", 1), gv_tok, "gv_tok", "dve", 4)

        for tb in range(4):
            sl = tb % 2
            tsl = slice(tb * 512, (tb + 1) * 512)
            p.op("sp", lambda e, sl=sl, tsl=tsl: e.dma_start(out=xs[sl], in_=xT[:, :, tsl]), writes=[("xs", sl)], dma=f"xs{sl}")
            p.op("act", lambda e, sl=sl: e.activation(out=sq1, in_=xs[sl], func=AF.Square), reads=[("xs", sl)], writes=[("sq1",)])
            b = tb % 2
            for kc in range(KC):
                p.op("pe", lambda e, kc=kc, b=b: e.matmul(bank(b), lhsT=ones_bf, rhs=sq1[:, kc, :], start=(kc == 0), stop=(kc == KC - 1)),
                     reads=[("ones",), ("sq1",)], writes=[PS(b)])
            rstd_from_ss(bank(b), rs1, D, [PS(b)], [("rs1",)])
            for kc in range(KC):
                eng = "dve" if kc % 2 == 0 else "dve"
                p.op(eng, lambda e, kc=kc, sl=sl, tsl=tsl: e.scalar_tensor_tensor(
                    out=hT[:, kc, tsl], in0=xs[sl][:, kc, :], scalar=cst[:, C_LN1 + kc:C_LN1 + kc + 1], in1=rs1,
                    op0=ALU.mult, op1=ALU.mult),
                    reads=[("xs", sl), ("rs1",), ("cst",)], writes=[("hT", tb, kc)])
            if tb >= 1:
                a1_proj(tb - 1)
        a1_proj(3)
        dbg("hT", hT, [("hT",)])
        p.barrier()
        A.release("xs0", "xs1", "sq1", "rs1", "lamtmp")
        if stop_after == "A1":
            return finish(nc, p, es, None)


        def proj_fm(wtile, c0, ncols, tok_w, consume):
            for tb in range(4):
                b = psrot[0] % 4
                psrot[0] += 1
                tsl = slice(tb * 512, (tb + 1) * 512)
                for kc in range(KC):
                    p.op("pe", lambda e, kc=kc, b=b, tsl=tsl: e.matmul(bank(b)[0:ncols, :], lhsT=wtile[:, kc, c0:c0 + ncols], rhs=hT[:, kc, tsl],
                                                                      start=(kc == 0), stop=(kc == KC - 1)),
                         reads=[tok_w, ("hT", tb)], writes=[PS(b)], nobar=True)
                consume(tb, bank(b)[0:ncols, :], b)

        def proj_tm(wtile, tok_w, consume):
            for tkb in range(16):
                b = psrot[0] % 4
                psrot[0] += 1
                tsl = slice(tkb * 128, (tkb + 1) * 128)
                for kc in range(KC):
                    p.op("pe", lambda e, kc=kc, b=b, tsl=tsl: e.matmul(bank(b), lhsT=hT[:, kc, tsl], rhs=wtile[:, kc, :],
                                                                      start=(kc == 0), stop=(kc == KC - 1)),
                         reads=[tok_w, ("hT", tkb // 4)], writes=[PS(b)])
                consume(tkb, bank(b), b)


        q_decT = A.view("q_decT", BF16, [2, S])
        k_decT = A.view("k_decT", BF16, [2, S])
        kteT = A.view("kteT", BF16, [2, S])
        kte_tok = A.view("kte_tok", BF16, [16, 256])
        sgo = A.view("sgo", BF16, [4, S])
        resetm = A.view("resetm", F32, [S])
        T0 = A.view("T0", F32, [S])
        T1 = A.view("T1", F32, [S])
        T2 = A.view("T2", F32, [S])
        ebT = A.view("ebT", F32, [S])
        enbT = A.view("enbT", F32, [S])
        ek2T = A.view("ek2T", F32, [S])
        ebl = A.view("ebl", F32, [2, 32])

        p.op("pool", lambda e: e.memset(resetm, 1.0), writes=[("resetm",)])
        p.op("pool", lambda e: e.memset(resetm[:, 0:S:64], 0.0), writes=[("resetm",)])


        def go_groups(c):
            out = []
            for c4 in (2 * c, 2 * c + 1):
                for tb in range(4):
                    def g(c4=c4, tb=tb):
                        b = psrot[0] % 4
                        psrot[0] += 1
                        tsl = slice(tb * 512, (tb + 1) * 512)
                        for kc in range(KC):
                            p.op("pe", lambda e, kc=kc: e.matmul(bank(b), lhsT=wA[2][:, kc, c4 * 128:(c4 + 1) * 128], rhs=hT[:, kc, tsl],
                                                                 start=(kc == 0), stop=(kc == KC - 1)),
                                 reads=[("w", 2), ("hT", tb)], writes=[PS(b)])
                        p.op("act", lambda e: e.activation(out=sgo[:, c4, tsl], in_=bank(b), func=AF.Copy), reads=[PS(b)], writes=[("sgo", c4, tb)])
                    out.append(g)
            return out

        fill = []

        def FILL(n=2):
            for _ in range(n):
                if fill:
                    fill.pop(0)()

        for c in range(2):
            if c == 0:
                fill.extend([(lambda tkb=tkb: tm_group(tkb, wA[2], ("w", 2), v_tok, "v_tok", "act", 0)) for tkb in range(16)])
            else:
                fill.extend(go_groups(0) + go_groups(1))
            for tb in range(4):
                p.op("pe", lambda e, tb=tb, c=c: e.matmul(bank(4 + tb), lhsT=wg2_sb[0:16, c * 128:(c + 1) * 128],
                                                         rhs=grT[0:16, tb * 512:(tb + 1) * 512], start=True, stop=True),
                     reads=[("wg2",), ("grT", tb)], writes=[PS(4 + tb)], nobar=True)
            zps = ps[:, 4:8, :]
            T0v = T0.rearrange("p (a b) -> p a b", a=4)
            rd4 = [PS(4), PS(5), PS(6), PS(7)]
            p.op("act", lambda e, c=c: e.activation(out=T0v, in_=zps, func=AF.Identity, bias=cst[:, C_BG + c:C_BG + c + 1], scale=1.0),
                 reads=rd4 + [("cst",)], writes=[("T0",)])
            FILL()
            p.op("act", lambda e: e.activation(out=T1, in_=T0, func=AF.Abs), reads=[("T0",)], writes=[("T1",)])
            FILL()
            p.op("act", lambda e: e.activation(out=T1, in_=T1, func=AF.Exp, scale=-1.0), reads=[("T1",)], writes=[("T1",)])
            FILL()
            p.op("act", lambda e: e.activation(out=T1, in_=T1, func=AF.Ln, bias=one_ap, scale=1.0), reads=[("T1",), ("misc", "one")], writes=[("T1",)])
            FILL()
            p.op("dve", lambda e: e.tensor_scalar(out=T2, in0=T0, scalar1=0.0, scalar2=1.0 / 16.0, op0=ALU.min, op1=ALU.mult),
                 reads=[("T0",)], writes=[("T2",)])
            p.op("dve", lambda e: e.scalar_tensor_tensor(out=T1, in0=T1, scalar=-1.0 / 16.0, in1=T2, op0=ALU.mult, op1=ALU.add),
                 reads=[("T1",), ("T2",)], writes=[("T1",)])
            FILL()
            p.op("dve", lambda e: e.tensor_tensor_scan(out=T0, data0=resetm, data1=T1, initial=0.0, op0=ALU.mult, op1=ALU.add),
                 reads=[("T1",), ("resetm",)], writes=[("T0",)])
            FILL()
            p.op("act", lambda e: e.activation(out=ebT, in_=T0, func=AF.Exp), reads=[("T0",)], writes=[("ebT",)])
            p.op("act", lambda e: e.activation(out=enbT, in_=T0, func=AF.Exp, scale=-1.0), reads=[("T0",)], writes=[("enbT",)])
            FILL()
            bl = T0[:, 63:S:64].unsqueeze(2).broadcast_to([128, 32, 64])
            p.op("dve", lambda e, bl=bl: e.tensor_tensor(out=T2.rearrange("p (a b) -> p a b", a=32), in0=bl,
                                                        in1=T0.rearrange("p (a b) -> p a b", a=32), op=ALU.subtract),
                 reads=[("T0",)], writes=[("T2",)])
            p.op("act", lambda e: e.activation(out=ek2T, in_=T2, func=AF.Exp), reads=[("T2",)], writes=[("ek2T",)])
            p.op("dve", lambda e, c=c: e.tensor_copy(out=ebl[:, c, :], in_=ebT[:, 63:S:64]), reads=[("ebT",)], writes=[("ebl", c)])
            while fill:
                FILL()
            if c == 0:
                wdma(wA[2], w_in_cols(2576, 3088), ("w", 2), "w2")
            if c == 0:
                dbg("bT", T0, [("T0",)])
                dbg("ebT", ebT, [("ebT",)])

            def cons_q(tb, pap, b, c=c):
                tsl = slice(tb * 512, (tb + 1) * 512)
                p.op("dve", lambda e: e.scalar_tensor_tensor(out=q_decT[:, c, tsl], in0=pap, scalar=0.125, in1=ebT[:, tsl],
                                                             op0=ALU.mult, op1=ALU.mult),
                     reads=[PS(b), ("ebT",)], writes=[("q_decT", c, tb)])
            proj_fm(wA[0], c * 128, 128, ("w", 0), cons_q)

            def cons_k(tb, pap, b, c=c):
                tsl = slice(tb * 512, (tb + 1) * 512)
                p.op("dve", lambda e: e.tensor_tensor(out=k_decT[:, c, tsl], in0=pap, in1=enbT[:, tsl], op=ALU.mult),
                     reads=[PS(b), ("enbT",)], writes=[("k_decT", c, tb)])
                p.op("dve", lambda e: e.tensor_tensor(out=kteT[:, c, tsl], in0=pap, in1=ek2T[:, tsl], op=ALU.mult),
                     reads=[PS(b), ("ek2T",)], writes=[("kteT", c, tb)])
            proj_fm(wA[0], 256 + c * 128, 128, ("w", 0), cons_k)

        for c in range(2):
            for g in range(4):
                b = psrot[0] % 4
                psrot[0] += 1
                pst = bank(b)[:, 0:256].bitcast(BF16).rearrange("p (a b) -> p a b", a=4)
                for i in range(4):
                    tkb = g * 4 + i
                    p.op("pe", lambda e, i=i, tkb=tkb, c=c, pst=pst: e.transpose(out=pst[:, i, :], in_=kteT[:, c, tkb * 128:(tkb + 1) * 128], identity=ident),
                         reads=[("kteT", c, tkb // 4), ("ident",)], writes=[PS(b)])
                p.op("act", lambda e, g=g, c=c, pst=pst: e.activation(out=kte_tok[:, g * 4:(g + 1) * 4, c * 128:(c + 1) * 128], in_=pst, func=AF.Copy),
                     reads=[PS(b)], writes=[("kte_tok", c, g)])


        wdma(wA[0], w_in_cols(0, 512), ("w", 0), "w0")
        wdma(wA[1], w_in_cols(512, 1024), ("w", 1), "w1")

        dbg("q_decT", q_decT, [("q_decT",)])
        dbg("k_decT", k_decT, [("k_decT",)])
        dbg("kte_tok", kte_tok, [("kte_tok",)])
        dbg("gv_tok", gv_tok, [("gv_tok",)])
        dbg("sgo", sgo, [("sgo",)])
        dbg("ebl", ebl, [("ebl",)])
        p.barrier(exempt=("dma:w0", "dma:w1", "dma:w2"))
        A.release("kteT", "grT", "resetm", "T0", "T1", "T2", "ebT", "enbT", "ek2T")
        if stop_after == "A2":
            return finish(nc, p, es, None)

        cc = A.view("cc", BF16, [KC, S], top=True)
        Sbf = A.view("Sbf", BF16, [33, 2, 128])
        aT_halves = [A.view("aT_lo", BF16, [8, 4, 128]), A.view("aT_hi", BF16, [8, 4, 128])]

        def aTv(tkb):
            return aT_halves[tkb // 8][:, tkb % 8]
        Sst2 = A.view("Sst", F32, [2, 2, 128])
        e_sq = A.view("e_sq", BF16, [512], top=True)
        e_rs = A.view("e_rs", F32, [512], top=True)
        e_sq1 = A.view("e_t", BF16, [512], top=True)
        e_rs1 = A.view("e_rs1", F32, [512], top=True)

        for tkb in range(16):
            pair = (tkb % 2) * 2
            tsl = slice(tkb * 128, (tkb + 1) * 128)
            for h in range(4):
                c, hh = h // 2, h % 2
                rows = slice(hh * 64, (hh + 1) * 64)
                b = pair + hh
                p.op("pe", lambda e, b=b, h=h, c=c, rows=rows, tsl=tsl: e.matmul(bank(b)[:, c * 128:(c + 1) * 128], lhsT=k_decT[rows, c, tsl],
                                                                                  rhs=q_decT[rows, c, tsl], start=True, stop=True),
                     reads=[("k_decT", c, tkb // 4), ("q_decT", c, tkb // 4)], writes=[PS(b)], nobar=True)
            for hh in range(2):
                b = pair + hh
                p.op("dve", lambda e, b=b, tkb=tkb, hh=hh: e.tensor_tensor(out=aTv(tkb)[:, hh:4:2, :], in0=bank(b)[:, 0:256].rearrange("p (a b) -> p a b", a=2),
                                                                       in1=mask2.unsqueeze(1).broadcast_to([128, 2, 128]), op=ALU.mult),
                     reads=[PS(b), ("mask2",)], writes=[("aT", tkb, hh)])

        for c4 in range(4):
            p.op("act", lambda e, c4=c4: e.activation(out=sgo[:, c4, :], in_=sgo[:, c4, :], func=AF.Silu), reads=[("sgo", c4)], writes=[("sgo", c4)])

        p.op("pool", lambda e: e.memset(Sbf[:, 0, :, :], 0.0), writes=[("Sbf", 0)])
        p.op("pool", lambda e: e.memset(Sst2[:, 0, :, :], 0.0), writes=[("Sst", 0)])
        for tkb in range(16):
            bp = 4 + 2 * (tkb % 2)
            for c in range(2):
                for hh in range(2):
                    h = 2 * c + hh
                    orow = slice(hh * 64, (hh + 1) * 64)
                    for s in range(2):
                        trow = slice(s * 64, (s + 1) * 64)
                        p.op("pe", lambda e, c=c, hh=hh, h=h, trow=trow, orow=orow, s=s, bp=bp, tkb=tkb: e.matmul(
                            bank(bp + s)[orow, c * 128:(c + 1) * 128], lhsT=kte_tok[trow, tkb, c * 128 + hh * 64:c * 128 + hh * 64 + 64],
                            rhs=gv_tok[trow, tkb, h * 128:(h + 1) * 128], start=True, stop=True),
                            reads=[("kte_tok", c, tkb // 4), ("gv_tok", tkb)], writes=[PS(bp + s)])
            for s in range(2):
                n = 2 * tkb + s
                so, sn = n % 2, (n + 1) % 2
                for c in range(2):
                    p.op("dve", lambda e, c=c, n=n, so=so, sn=sn, s=s, bp=bp: e.scalar_tensor_tensor(out=Sst2[:, sn, c, :], in0=Sst2[:, so, c, :], scalar=ebl[:, c, n:n + 1],
                                                                                            in1=bank(bp + s)[:, c * 128:(c + 1) * 128], op0=ALU.mult, op1=ALU.add),
                         reads=[PS(bp + s), ("Sst", so, c), ("ebl", c)], writes=[("Sst", sn, c)])
                p.op("act", lambda e, n=n, sn=sn: e.activation(out=Sbf[:, n + 1, :, :], in_=Sst2[:, sn, :, :], func=AF.Copy), reads=[("Sst", sn)], writes=[("Sbf", n + 1)])

        it = 0
        epi_prev = []

        def a3_epi(tb, h):
            b = h
            tsl = slice(tb * 512, (tb + 1) * 512)
            k = it_box[0] % 2
            it_box[0] += 1
            sq_, rs_ = (e_sq, e_rs) if k == 0 else (e_sq1, e_rs1)
            tq, tr = ("e_sqA", k), ("e_rsA", k)
            bs = 6 + k

            def part1():
                p.op("act", lambda e: e.activation(out=sq_, in_=bank(b), func=AF.Square), reads=[PS(b)], writes=[tq])

            def part2():
                p.op("pe", lambda e: e.matmul(bank(bs), lhsT=ones_bf, rhs=sq_, start=True, stop=True), reads=[tq, ("ones",)], writes=[PS(bs)])
                rstd_from_ss(bank(bs), rs_, 128, [PS(bs)], [tr])
                p.op("dve", lambda e: e.scalar_tensor_tensor(out=rs_, in0=bank(b), scalar=cst[:, C_GNW:C_GNW + 1], in1=rs_, op0=ALU.mult, op1=ALU.mult),
                     reads=[PS(b), tr, ("cst",)], writes=[tr])
                p.op("dve", lambda e: e.tensor_tensor(out=cc[:, 4 + h, tsl], in0=rs_, in1=sgo[:, h, tsl], op=ALU.mult),
                     reads=[tr, ("sgo", h, tb)], writes=[("cc", 4 + h, tb)])
            return part1, part2

        it_box = [0]
        carry = None
        for tb in range(4):
            for c in range(2):
                hs = (2 * c, 2 * c + 1)
                for i in range(4):
                    tkb = tb * 4 + i
                    osl = slice(i * 128, (i + 1) * 128)
                    for h in hs:
                        p.op("pe", lambda e, h=h, osl=osl, tkb=tkb: e.matmul(bank(h)[:, osl], lhsT=gv_tok[:, tkb, h * 128:(h + 1) * 128], rhs=aTv(tkb)[:, h, :], start=True, stop=False),
                             reads=[("gv_tok", tkb), ("aT", tkb)], writes=[PS(h)])
                    for s in range(2):
                        n = 2 * tkb + s
                        o2 = slice(i * 128 + s * 64, i * 128 + s * 64 + 64)
                        t2 = slice(tkb * 128 + s * 64, tkb * 128 + s * 64 + 64)
                        for h in hs:
                            rows = slice((h % 2) * 64, (h % 2) * 64 + 64)
                            p.op("pe", lambda e, h=h, rows=rows, s=s, n=n, o2=o2, t2=t2, c=c: e.matmul(bank(h)[:, o2], lhsT=Sbf[rows, n, c, :], rhs=q_decT[rows, c, t2],
                                                                                              start=False, stop=(s == 1)),
                                 reads=[("Sbf", n), ("q_decT", c, tkb // 4)], writes=[PS(h)])
                for h in hs:
                    p1, p2 = a3_epi(tb, h)
                    p1()
                    if carry is not None:
                        carry()
                    carry = p2
        if carry is not None:
            carry()
        p.barrier(exempt=("dma:w0", "dma:w1", "dma:w2"))
        A.release("aT_lo", "aT_hi", "Sbf", "Sst", "q_decT", "k_decT", "kte_tok", "gv_tok", "sgo", "ebl")
        if stop_after == "A3":
            return finish(nc, p, es, (cc, "cc"), dbg_out)

        qT = A.view("qT", BF16, [4, S])
        kT = A.view("kT", BF16, [4, S])
        flip = [0]

        def cons_qk(dst, name, h):
            def cons(tb, pap, b):
                flip[0] ^= 1
                if flip[0]:
                    p.op("act", lambda e: e.activation(out=dst[:, h, tb * 512:(tb + 1) * 512], in_=pap, func=AF.Copy), reads=[PS(b)], writes=[(name, h, tb)])
                else:
                    p.op("dve", lambda e: e.tensor_copy(out=dst[:, h, tb * 512:(tb + 1) * 512], in_=pap), reads=[PS(b)], writes=[(name, h, tb)])
            return cons

        def cons_dv(tkb, pap, b):
            flip[0] ^= 1
            if flip[0]:
                p.op("act", lambda e: e.activation(out=v_tok[:, tkb, :], in_=pap, func=AF.Copy), reads=[PS(b)], writes=[("v_tok", tkb)])
            else:
                p.op("dve", lambda e: e.tensor_copy(out=v_tok[:, tkb, :], in_=pap), reads=[PS(b)], writes=[("v_tok", tkb)])
        for h in range(4):
            proj_fm(wA[0], h * 128, 128, ("w", 0), cons_qk(qT, "qT", h))
            proj_fm(wA[1], h * 128, 128, ("w", 1), cons_qk(kT, "kT", h))
        w_o = wview(0, [KC, D])
        wdma(w_o, w_out.rearrange("(kc p) n -> p kc n", p=128), None, "w0")

        def filler_ops(hn):
            out = []
            for wi, dst, name in ((0, qT, "qT"), (1, kT, "kT")):
                for tb in range(4):
                    tsl = slice(tb * 512, (tb + 1) * 512)
                    for kc in range(KC):
                        def mo(wi=wi, dst=dst, name=name, tb=tb, tsl=tsl, kc=kc):
                            p.op("pe", lambda e: e.matmul(bank(7), lhsT=wA[wi][:, kc, hn * 128:(hn + 1) * 128], rhs=hT[:, kc, tsl],
                                                          start=(kc == 0), stop=(kc == KC - 1)),
                                 reads=[("w", wi), ("hT", tb)], writes=[PS(7)])
                            if kc == KC - 1:
                                p.op("dve", lambda e: e.tensor_copy(out=dst[:, hn, tsl], in_=bank(7)), reads=[PS(7)], writes=[(name, hn, tb)])
                        out.append(mo)
            return out

        NP = 4
        p_sb = A.view("p_sb", BF16, [NP, 2, 512])
        a_r = A.view("a_r", F32, [2, 512])
        a_o = A.view("a_o", F32, [512])
        xsB = [A.view(f"xs{i}", F32, [KC, 512], top=True) for i in range(2)]
        o_s = A.view("o_s", F32, [2, 512])
        item = [0]
        pending = []

        def flush_one():
            if pending:
                pending.pop(0)()

        blocks = [(h, qb) for h in range(4) for qb in range(4)]
        steps = [(bi, kb) for bi, (h, qb) in enumerate(blocks) for kb in range(4 * (qb + 1))]

        def emit_S(step):
            bi, kb = step
            h, qb = blocks[bi]
            qs = qb * 512
            j = kb - 4 * qb
            q0 = 128 * j if j > 0 else 0
            sl_ = item[0] % NP
            sbk = 2 * (item[0] % 2)
            item[0] += 1
            for c in range(2):
                rows = slice(c * 64, (c + 1) * 64)
                p.op("pe", lambda e, c=c, rows=rows: e.matmul(bank(sbk + c)[:, q0:512], lhsT=kT[rows, h, kb * 128:(kb + 1) * 128],
                                                              rhs=qT[rows, h, qs + q0:qs + 512], start=True, stop=True),
                     reads=[("kT", h, kb // 4), ("qT", h, qb)], writes=[PS(sbk + c)])
            for c in range(2):
                p.op("act", lambda e, c=c: e.activation(out=p_sb[:, sl_, c, q0:512], in_=bank(sbk + c)[:, q0:512], func=AF.Exp, scale=0.125),
                     reads=[PS(sbk + c)], writes=[("p_sb", sl_, c)])
                if j >= 0:
                    d0 = 128 * j
                    p.op("pool", lambda e, c=c, d0=d0: e.tensor_tensor(out=p_sb[:, sl_, c, d0:d0 + 128], in0=p_sb[:, sl_, c, d0:d0 + 128], in1=maskT, op=ALU.mult),
                         reads=[("p_sb", sl_, c), ("maskT",)], writes=[("p_sb", sl_, c)])
            return (sl_, q0)

        def emit_AV(step, res):
            bi, kb = step
            h, qb = blocks[bi]
            nkb = 4 * (qb + 1)
            sl_, q0 = res
            st, sp_ = (kb == 0), (kb == nkb - 1)
            for c in range(2):
                p.op("pe", lambda e, c=c: e.matmul(bank(4 + c)[:, q0:512], lhsT=v_tok[:, kb, h * 128:(h + 1) * 128], rhs=p_sb[:, sl_, c, q0:512], start=st, stop=sp_),
                     reads=[("v_tok", kb), ("p_sb", sl_, c)], writes=[PS(4 + c)])
                p.op("pe", lambda e, c=c: e.matmul(bank(6 + c)[:, q0:512], lhsT=ones_bf, rhs=p_sb[:, sl_, c, q0:512], start=st, stop=sp_),
                     reads=[("ones",), ("p_sb", sl_, c)], writes=[PS(6 + c)])

        def make_epilogue(bi):
            h, qb = blocks[bi]
            tsl = slice(qb * 512, qb * 512 + 512)

            def s2():
                p.op("act", lambda e: e.activation(out=a_r, in_=a_r, func=AF.Exp, scale=-1.0), reads=[("a_r",)], writes=[("a_r",)])

            def s3():
                p.op("dve", lambda e: e.tensor_tensor(out=o_s, in0=o_s, in1=a_r, op=ALU.mult), reads=[("o_s",), ("a_r",)], writes=[("o_s",)])

            def s4():
                p.op("dve", lambda e: e.scalar_tensor_tensor(out=a_o, in0=o_s[:, 1, :], scalar=neg_lam, in1=o_s[:, 0, :], op0=ALU.mult, op1=ALU.add),
                     reads=[("o_s",), ("misc", "nl")], writes=[("a_o",)])
                p.op("dve", lambda e: e.tensor_tensor(out=e_sq, in0=a_o, in1=a_o, op=ALU.mult), reads=[("a_o",)], writes=[("e_sq",)])

            def s5():
                b_ = 2 * ((item[0] + 1) % 2)
                p.op("pe", lambda e: e.matmul(bank(b_), lhsT=ones_bf, rhs=e_sq, start=True, stop=True), reads=[("e_sq",), ("ones",)], writes=[PS(b_)])
                rstd_from_ss(bank(b_), e_rs, 128, [PS(b_)], [("e_rs",)])

            def s7():
                p.op("dve", lambda e: e.scalar_tensor_tensor(out=cc[:, h, tsl], in0=a_o, scalar=w8, in1=e_rs, op0=ALU.mult, op1=ALU.mult),
                     reads=[("a_o",), ("e_rs",), ("misc", "w8")], writes=[("cc", h, qb)])
            return [s2, s3, s4, s5, s7]

        prev = emit_S(steps[0])
        for si, step in enumerate(steps):
            bi, kb = step
            nkb = 4 * (blocks[bi][1] + 1)
            nxt = emit_S(steps[si + 1]) if si + 1 < len(steps) else None
            emit_AV(step, prev)
            prev = nxt
            for _ in range(2 if nkb == 4 else 1):
                flush_one()
            if kb == nkb - 1:
                while pending:
                    flush_one()
                for c in range(2):
                    p.op("dve", lambda e, c=c: e.tensor_copy(out=o_s[:, c, :], in_=bank(4 + c)), reads=[PS(4 + c)], writes=[("o_s", c)])
                    p.op("act", lambda e, c=c: e.activation(out=a_r[:, c, :], in_=bank(6 + c), func=AF.Ln), reads=[PS(6 + c)], writes=[("a_r", c)])
                pending = make_epilogue(bi)
        while pending:
            flush_one()
        for tb_ in range(2):
            p.op("sp", lambda e, tb_=tb_: e.dma_start(out=xsB[tb_], in_=xT[:, :, tb_ * 512:(tb_ + 1) * 512]), writes=[("xs", tb_)], dma=f"xs{tb_}")
        p.barrier(exempt=("dma:w0", "dma:w1", "dma:w2", "dma:xs0", "dma:xs1"))
        A.release("qT", "kT", "v_tok", "p_sb", "a_r", "a_o", "e_t", "e_rs1", "hT", "o_s")
        if stop_after == "A5":
            return finish(nc, p, es, (cc, "cc"), dbg_out)

        dbg("w_o", w_o, [("w",)])
        dbg("cc", cc, [("cc",)])
        x1T = A.view("x1T", F32, [KC, S])
        rstd2 = A.view("rstd2", F32, [S])
        sqB = A.view("sq1", BF16, [KC, 512])
        h2 = A.view("h2", BF16, [KC, 1024])
        wu = [wview(16384, [KC, 2, 128]), wview(20480, [KC, 2, 128]), wview(0, [KC, 2, 128])]
        wd = [wview(4096 + i * 5632, [NFF, 128]) for i in range(2)]

        def load_wu(ffc, slot):
            for g in range(2):
                c0 = g * DFF + ffc * 128
                wdma(wu[slot][:, :, g, :], w_up[:, c0:c0 + 128].rearrange("(kc p) n -> p kc n", p=128), ("wu", slot, g), f"wu{slot}{g}")

        def load_wd(dmc, slot):
            wdma(wd[slot], w_down[:, dmc * 128:(dmc + 1) * 128].rearrange("(f p) n -> p f n", p=128), ("wd", slot), f"wd{slot}")

        load_wu(0, 0)
        load_wu(1, 1)

        def h2_compute(th):
            t0 = th * 1024
            for kc in range(KC):
                for tb2 in range(2):
                    tsl = slice(t0 + tb2 * 512, t0 + (tb2 + 1) * 512)
                    p.op("dve", lambda e, kc=kc, tb2=tb2, tsl=tsl: e.scalar_tensor_tensor(
                        out=h2[:, kc, tb2 * 512:(tb2 + 1) * 512], in0=x1T[:, kc, tsl], scalar=cst[:, C_LN2 + kc:C_LN2 + kc + 1], in1=rstd2[:, tsl],
                        op0=ALU.mult, op1=ALU.mult),
                        reads=[("x1T", 2 * th + tb2, kc), ("rstd2", 2 * th + tb2), ("cst",)], writes=[("h2", kc, tb2)])

        def b_proj(tb):
            sl = tb % 2
            tsl = slice(tb * 512, (tb + 1) * 512)
            if tb >= 2:
                p.op("sp", lambda e: e.dma_start(out=xsB[sl], in_=xT[:, :, tsl]), writes=[("xs", sl)], dma=f"xs{sl}")
            for dmc in range(KC):
                b = psrot[0] % 4
                psrot[0] += 1
                for kc in range(KC):
                    p.op("pe", lambda e, b=b, kc=kc, dmc=dmc: e.matmul(bank(b), lhsT=w_o[:, kc, dmc * 128:(dmc + 1) * 128], rhs=cc[:, kc, tsl],
                                                                     start=(kc == 0), stop=(kc == KC - 1)),
                         reads=[("w", dmc // 4), ("cc",)], writes=[PS(b)], nobar=True)
                p.op("dve", lambda e, b=b, dmc=dmc: e.tensor_tensor(out=x1T[:, dmc, tsl], in0=bank(b), in1=xsB[sl][:, dmc, :], op=ALU.add),
                     reads=[PS(b), ("xs", sl)], writes=[("x1T", tb, dmc)])

        def b_stats_sq(tb):
            tsl = slice(tb * 512, (tb + 1) * 512)
            p.op("act", lambda e: e.activation(out=sqB, in_=x1T[:, :, tsl], func=AF.Square), reads=[("x1T", tb)], writes=[("sq1",)])

        def b_stats_mm(tb, bs=None):
            tsl = slice(tb * 512, (tb + 1) * 512)
            if bs is None:
                bs = 4 + tb % 2
            for kc in range(KC):
                p.op("pe", lambda e, kc=kc: e.matmul(bank(bs), lhsT=ones_bf, rhs=sqB[:, kc, :], start=(kc == 0), stop=(kc == KC - 1)),
                     reads=[("ones",), ("sq1",)], writes=[PS(bs)])
            rstd_from_ss(bank(bs), rstd2[:, tsl], D, [PS(bs)], [("rstd2", tb)])

        def b_stats(tb):
            b_stats_sq(tb)
            b_stats_mm(tb)

        b_proj(0)
        for tb in range(1, 4):
            b_proj(tb)
            if tb <= 2:
                b_stats(tb - 1)
            if tb == 2:
                h2_compute(0)
        p.barrier()
        A.release("cc", "xs0", "xs1", "e_sq", "e_rs")
        if stop_after == "B":
            return finish(nc, p, es, (x1T, "x1T"), dbg_out)

        actT = A.view("actT", BF16, [NFF, 1024])
        uh = A.view("uh", F32, [2 * NFF, 2])
        NCB = 2
        u_sb = [[A.view(f"u_sb{i}{g}", F32, [1026]) for g in range(2)] for i in range(NCB)]
        y_sb = [[A.view(f"y_sb{i}{g}", F32, [1024]) for g in range(2)] for i in range(NCB)]
        for i_ in range(NCB):
            for g_ in range(2):
                p.op("pool", lambda e, us=u_sb[i_][g_]: e.memset(us[:, 0:2], 0.0), writes=[("u_sb", i_, g_, "h")])

        rsf = A.view("rsf", F32, [512])
        fin_pending = []

        def final_stages(tb):
            tsl = slice(tb * 512, (tb + 1) * 512)

            def f1():
                p.op("act", lambda e: e.activation(out=sqB, in_=x1T[:, :, tsl], func=AF.Square), reads=[("x1T", tb)], writes=[("sq1",)])

            def f2():
                bs = psrot[0] % 8
                psrot[0] += 1
                for kc in range(KC):
                    p.op("pe", lambda e, kc=kc: e.matmul(bank(bs), lhsT=ones_bf, rhs=sqB[:, kc, :], start=(kc == 0), stop=(kc == KC - 1)),
                         reads=[("ones",), ("sq1",)], writes=[PS(bs)])
                rstd_from_ss(bank(bs), rsf, D, [PS(bs)], [("rsf",)])

            def f3():
                for kc in range(KC):
                    p.op("dve", lambda e, kc=kc: e.scalar_tensor_tensor(out=x1T[:, kc, tsl], in0=x1T[:, kc, tsl], scalar=cst[:, C_LNF + kc:C_LNF + kc + 1],
                                                                        in1=rsf, op0=ALU.mult, op1=ALU.mult),
                         reads=[("x1T", tb, kc), ("rsf",), ("cst",)], writes=[("x1T", tb, kc)])
                p.op("sp", lambda e: e.dma_start(out=outT[:, :, tsl], in_=x1T[:, :, tsl]), reads=[("x1T", tb)], writes=[("out", tb)], dma=f"out{tb % 2}")
            return [f1, f2, f3]

        pair_it = 0
        for th in range(2):
            t0 = th * 1024
            for ffc in range(NFF):
                slot = pair_it % 3
                cb = pair_it % NCB
                pair_it += 1
                if ffc + 2 < NFF:
                    load_wu(ffc + 2, (slot + 2) % 3)
                if ffc == NFF - 4:
                    load_wd(0, 0)
                    load_wd(1, 1)
                pb = 4 * (ffc % 2)
                for g in range(2):
                    for tb2 in range(2):
                        b = pb + 2 * g + tb2
                        for kc in range(KC):
                            p.op("pe", lambda e, b=b, kc=kc, g=g, tb2=tb2, slot=slot: e.matmul(bank(b), lhsT=wu[slot][:, kc, g, :], rhs=h2[:, kc, tb2 * 512:(tb2 + 1) * 512],
                                                                                             start=(kc == 0), stop=(kc == KC - 1)),
                                 reads=[("wu", slot, g), ("h2",)], writes=[PS(b)], nobar=True)
                for g in range(2):
                    ch = g * NFF + ffc
                    b0 = pb + 2 * g
                    pss = ps[:, b0:b0 + 2, :]
                    rd = [PS(b0), PS(b0 + 1)]
                    us = u_sb[cb][g]
                    ys = y_sb[cb][g]
                    cw = lambda i, ch=ch: cst[:, C_CW + ch * 3 + i:C_CW + ch * 3 + i + 1]
                    p.op("act", lambda e, us=us, pss=pss: e.activation(out=us[:, 2:1026].rearrange("p (a b) -> p a b", a=2), in_=pss, func=AF.Copy),
                         reads=rd, writes=[("u_sb", cb, g, "m")])
                    if th == 0:
                        p.op("act", lambda e, ch=ch, b0=b0: e.activation(out=uh[:, ch, :], in_=bank(b0 + 1)[:, 510:512], func=AF.Copy),
                             reads=[PS(b0 + 1)], writes=[("uh", ch)])
                    else:
                        p.op("dve", lambda e, us=us, ch=ch: e.tensor_copy(out=us[:, 0:2], in_=uh[:, ch, :]), reads=[("uh", ch)], writes=[("u_sb", cb, g, "h")])
                    p.op("act", lambda e, ys=ys, pss=pss, ch=ch, cw=cw: e.activation(out=ys.rearrange("p (a b) -> p a b", a=2), in_=pss, func=AF.Identity,
                                                                                    scale=cw(2), bias=cst[:, C_CB + ch:C_CB + ch + 1]),
                         reads=rd + [("cst",)], writes=[("y_sb", cb, g)])
                    p.op("dve", lambda e, ys=ys, us=us, cw=cw: e.scalar_tensor_tensor(out=ys, in0=us[:, 1:1025], scalar=cw(1), in1=ys, op0=ALU.mult, op1=ALU.add),
                         reads=[("u_sb", cb, g), ("y_sb", cb, g), ("cst",)], writes=[("y_sb", cb, g)])
                    p.op("dve", lambda e, ys=ys, us=us, cw=cw: e.scalar_tensor_tensor(out=ys, in0=us[:, 0:1024], scalar=cw(0), in1=ys, op0=ALU.mult, op1=ALU.add),
                         reads=[("u_sb", cb, g), ("y_sb", cb, g), ("cst",)], writes=[("y_sb", cb, g)])
                yg, yv = y_sb[cb][0], y_sb[cb][1]
                p.op("act", lambda e, yg=yg: e.activation(out=yg, in_=yg, func=AF.Silu), reads=[("y_sb", cb, 0)], writes=[("y_sb", cb, 0)])
                p.op("dve", lambda e, yg=yg, yv=yv, ffc=ffc: e.tensor_tensor(out=actT[:, ffc, :], in0=yg, in1=yv, op=ALU.mult),
                     reads=[("y_sb", cb, 0), ("y_sb", cb, 1)], writes=[("actT", ffc)])
            if th == 0:
                load_wu(0, (pair_it) % 3)
                load_wu(1, (pair_it + 1) % 3)
                b_stats_sq(2)
            def dn_mm(b, f, slot, tb2):
                p.op("pe", lambda e: e.matmul(bank(b), lhsT=wd[slot][:, f, :], rhs=actT[:, f, tb2 * 512:(tb2 + 1) * 512],
                                              start=(f == 0), stop=(f == NFF - 1)),
                     reads=[("wd", slot), ("actT", f)], writes=[PS(b)])

            def dn_evac(b, dmc, tb2):
                tsl = slice(t0 + tb2 * 512, t0 + (tb2 + 1) * 512)
                p.op("dve", lambda e: e.tensor_tensor(out=x1T[:, dmc, tsl], in0=bank(b), in1=x1T[:, dmc, tsl], op=ALU.add),
                     reads=[PS(b), ("x1T", 2 * th + tb2, dmc)], writes=[("x1T", 2 * th + tb2, dmc)])

            NL = 4
            first = []
            for dmc in (0, 1):
                for tb2 in (0, 1):
                    b = psrot[0] % 8
                    psrot[0] += 1
                    first.append((b, dmc, tb2))
                    for f in range(NFF - NL):
                        dn_mm(b, f, dmc % 2, tb2)
            if th == 0:
                bs_ = psrot[0] % 8
                psrot[0] += 1
                b_stats_mm(2, bs_)
                b_stats_sq(3)
            for (b, dmc, tb2) in first:
                for f in range(NFF - NL, NFF):
                    dn_mm(b, f, dmc % 2, tb2)
                dn_evac(b, dmc, tb2)
            if th == 0:
                bs_ = psrot[0] % 8
                psrot[0] += 1
                b_stats_mm(3, bs_)
                h2_compute(1)
            load_wd(2, 0)
            load_wd(3, 1)
            for dmc in range(2, KC):
                slot = dmc % 2
                for tb2 in (0, 1):
                    b = psrot[0] % 8
                    psrot[0] += 1
                    for f in range(NFF):
                        dn_mm(b, f, slot, tb2)
                    dn_evac(b, dmc, tb2)
                if dmc + 2 < KC:
                    load_wd(dmc + 2, slot)
                if fin_pending:
                    fin_pending.pop(0)()
            if th == 0:
                fin_pending.extend(final_stages(0) + final_stages(1))
            else:
                a2, a3 = final_stages(2), final_stages(3)
                for f in (a2[0], a2[1], a3[0], a2[2], a3[1], a3[2]):
                    f()
        return finish(nc, p, es, None, dbg_out)


def finish(nc, p, es, dump, dbg_out=None):
    if dump is not None and dbg_out:
        ap, name = dump
        if name in dbg_out:
            p.op("sp", lambda e: e.dma_start(out=dbg_out[name], in_=ap), reads=[(name,)], writes=[("dbg", name)], dma="dbg_" + name)
    fin = p.op("sp", lambda e: e.nop())
    for s, i in p.last.items():
        if s.startswith("dma:") and i != fin.id:
            fin.deps.add(i)
    p.finalize()
    sems = {}
    for s in p.streams:
        sems[s] = es.enter_context(nc.semaphore("s_" + s.replace(":", "_")))
    block = es.enter_context(nc.Block())
    p.emit(nc, block, sems)
    return nc


def make_inputs(x, ln1_w, w_in, diff_lq1, diff_lk1, diff_lq2, diff_lk2, diff_subln_w, gla_wg2, gla_bg,
                gla_norm_w, w_out, ln2_w, w_up, conv_w, conv_b, w_down, lnf_w):
    f = lambda a: np.ascontiguousarray(np.asarray(a, dtype=np.float32))
    cst = np.zeros((128, NCST), np.float32)
    cst[:, C_LN1:C_LN1 + 8] = f(ln1_w)[0].reshape(8, 128).T
    cst[:, C_LN2:C_LN2 + 8] = f(ln2_w)[0].reshape(8, 128).T
    cst[:, C_LNF:C_LNF + 8] = f(lnf_w).reshape(8, 128).T
    cst[:, C_BG:C_BG + 2] = f(gla_bg)[0].reshape(2, 128).T
    cst[:, C_SUB] = f(diff_subln_w)[0]
    cst[:, C_GNW] = f(gla_norm_w)[0]
    cst[:, C_CB:C_CB + 44] = f(conv_b)[0].reshape(44, 128).T
    cst[:, C_CW:C_CW + 132] = f(conv_w)[0].T.reshape(44, 128, 3).transpose(1, 0, 2).reshape(128, 132)
    cst[:, C_LQ1:C_LQ1 + 64] = f(diff_lq1)[0][None, :]
    cst[:, C_LK1:C_LK1 + 64] = f(diff_lk1)[0][None, :]
    cst[:, C_LQ2:C_LQ2 + 64] = f(diff_lq2)[0][None, :]
    cst[:, C_LK2:C_LK2 + 64] = f(diff_lk2)[0][None, :]
    shared = {"w_in": f(w_in)[0], "w_out": f(w_out)[0], "w_up": f(w_up)[0], "w_down": f(w_down)[0],
              "wg2": f(gla_wg2)[0], "cst": cst}
    xf = f(x)
    maps = []
    for b in range(xf.shape[0]):
        xt = np.ascontiguousarray(xf[b].T.reshape(KC, 128, S).transpose(1, 0, 2))
        m = dict(shared)
        m["xT"] = xt
        maps.append(m)
    return maps


_NC_CACHE = {}


def kernel(**inputs):
    maps = make_inputs(**inputs)
    if "nc" not in _NC_CACHE:
        _NC_CACHE["nc"] = build_nc()
    nc = _NC_CACHE["nc"]
    n = len(maps)
    res = run_bass_kernel_spmd(nc, maps, core_ids=list(range(n)))
    out = np.empty((n, S, D), np.float32)
    for b in range(n):
        o = res.results[b]["outT"]
        out[b] = o.transpose(1, 0, 2).reshape(D, S).T
    return out
```

```python
import numpy as np
import concourse.bass as bass
import concourse.mybir as mybir
from concourse.bass_utils import run_bass_kernel_spmd

F32 = mybir.dt.float32
BF16 = mybir.dt.bfloat16
AF = mybir.ActivationFunctionType
ALU = mybir.AluOpType

S = 2048
D = 1024
KC = 8
DFF = 2816
NFF = 22
EPS = 1e-6
SAME_ENGINE_SYNC = True

C_LN1, C_LN2, C_LNF, C_BG, C_SUB, C_GNW, C_CB, C_CW = 0, 8, 16, 24, 26, 27, 28, 72
C_LQ1, C_LK1, C_LQ2, C_LK2 = 204, 268, 332, 396
NCST = 460


class Op:
    __slots__ = ("id", "eng", "fn", "deps", "dma", "stream", "signal", "tick", "waits")

    def __init__(self, id, eng, fn, dma):
        self.id, self.eng, self.fn, self.dma = id, eng, fn, dma
        self.stream = ("dma:" + dma) if dma else eng
        self.deps = set()
        self.signal = False
        self.tick = 0
        self.waits = {}


class Prog:
    ENGS = ("pe", "act", "dve", "pool", "sp")

    def __init__(self):
        self.ops = []
        self.state = {}
        self.desc = {}
        self.last = {}
        self.barrier_op = None

    def _related(self, t):
        out = []
        for i in range(1, len(t) + 1):
            pre = t[:i]
            if pre in self.state:
                out.append(pre)
        for d in self.desc.get(t, ()):
            if d in self.state:
                out.append(d)
        return out

    def _ensure(self, t):
        if t not in self.state:
            self.state[t] = [None, {}]
            for i in range(1, len(t)):
                self.desc.setdefault(t[:i], set()).add(t)

    def op(self, eng, fn, reads=(), writes=(), dma=None, nobar=False):
        o = Op(len(self.ops), eng, fn, dma)
        deps = o.deps
        for t in reads:
            for r in self._related(t):
                w = self.state[r][0]
                if w is not None:
                    deps.add(w)
        for t in writes:
            for r in self._related(t):
                st = self.state[r]
                if st[0] is not None:
                    deps.add(st[0])
                deps.update(st[1].values())
        if self.barrier_op is not None and not nobar:
            deps.add(self.barrier_op)
        for t in reads:
            self._ensure(t)
            self.state[t][1][o.stream] = o.id
        for t in writes:
            for d in list(self.desc.get(t, ())):
                self.state.pop(d, None)
            self.desc.pop(t, None)
            self._ensure(t)
            self.state[t] = [o.id, {}]
        deps.discard(o.id)
        self.ops.append(o)
        self.last[o.stream] = o.id
        return o

    def barrier(self, exempt=()):
        o = self.op("pool", lambda e: e.nop())
        for s, i in self.last.items():
            if s not in exempt and i != o.id:
                o.deps.add(i)
        self.barrier_op = o.id
        return o

    def finalize(self):
        ops = self.ops
        for o in ops:
            for d in list(o.deps):
                do = ops[d]
                if do.stream == o.stream and not do.dma:
                    if o.eng == "pe" or o.eng == "sp" or not SAME_ENGINE_SYNC:
                        o.deps.discard(d)
                        continue
                do.signal = True
        cnt = {}
        for o in ops:
            if o.dma:
                o.signal = True
            if o.signal:
                inc = 16 if o.dma else 1
                cnt[o.stream] = cnt.get(o.stream, 0) + inc
                o.tick = cnt[o.stream]
        for s, v in cnt.items():
            assert v < 30000, (s, v)
        self.streams = sorted(cnt.keys())
        clock = {e: {} for e in self.ENGS}
        snap = {}
        for o in ops:
            need = {}
            for d in o.deps:
                do = ops[d]
                if need.get(do.stream, 0) < do.tick:
                    need[do.stream] = do.tick
            ck = clock[o.eng]
            for d in o.deps:
                do = ops[d]
                if ck.get(do.stream, 0) >= do.tick:
                    for k, v in snap[d].items():
                        if ck.get(k, 0) < v:
                            ck[k] = v
            waits = {}
            for k, v in need.items():
                if ck.get(k, 0) < v:
                    waits[k] = v
            for k, v in waits.items():
                ck[k] = v
            for d in o.deps:
                for k, v in snap[d].items():
                    if ck.get(k, 0) < v:
                        ck[k] = v
            o.waits = waits
            if o.signal:
                sn = dict(ck)
                sn[o.stream] = o.tick
                snap[o.id] = sn

    def emit(self, nc, block, sems):
        by_eng = {e: [o for o in self.ops if o.eng == e] for e in self.ENGS}

        def run(eng_name, e):
            for o in by_eng[eng_name]:
                for k, v in o.waits.items():
                    e.wait_ge(sems[k], v)
                inst = o.fn(e)
                if o.signal:
                    inst.then_inc(sems[o.stream], 16 if o.dma else 1)

        @block.tensor
        def _(e):
            run("pe", e)

        @block.scalar
        def _(e):
            run("act", e)

        @block.vector
        def _(e):
            run("dve", e)

        @block.gpsimd
        def _(e):
            run("pool", e)

        @block.sync
        def _(e):
            run("sp", e)


class Arena:
    def __init__(self, big, nbytes):
        self.big = big
        self.free = [(0, nbytes)]
        self.used = {}

    def alloc(self, name, nbytes, top=False):
        nbytes = (nbytes + 63) // 64 * 64
        if top:
            for i in range(len(self.free) - 1, -1, -1):
                o, n = self.free[i]
                if n >= nbytes:
                    self.free[i] = (o, n - nbytes)
                    self.used[name] = (o + n - nbytes, nbytes)
                    return o + n - nbytes
        for i, (o, n) in enumerate(self.free):
            if n >= nbytes:
                self.free[i] = (o + nbytes, n - nbytes)
                self.used[name] = (o, nbytes)
                return o
        raise RuntimeError(f"arena out of space for {name} {nbytes}: {self.free}")

    def release(self, *names):
        for name in names:
            o, n = self.used.pop(name)
            self.free.append((o, n))
        self.free.sort()
        m = []
        for o, n in self.free:
            if n == 0:
                continue
            if m and m[-1][0] + m[-1][1] == o:
                m[-1] = (m[-1][0], m[-1][1] + n)
            else:
                m.append((o, n))
        self.free = m

    def view(self, name, dtype, shape, parts=128, top=False):
        esz = 4 if dtype == F32 else 2
        n = int(np.prod(shape))
        o = self.alloc(name, n * esz, top=top)
        ap = self.big[0:parts, o // 4:(o + n * esz) // 4]
        if dtype != F32:
            ap = ap.bitcast(dtype)
        if len(shape) == 2:
            ap = ap.rearrange("p (a b) -> p a b", a=shape[0])
        elif len(shape) == 3:
            ap = ap.rearrange("p (a b c) -> p a b c", a=shape[0], b=shape[1])
        return ap


def build_nc(stop_after=None, debug=()):
    nc = bass.Bass("TRN2", target_bir_lowering=False)
    xT = nc.dram_tensor("xT", [128, KC, S], F32, kind="ExternalInput").ap()
    w_in = nc.dram_tensor("w_in", [D, 3088], F32, kind="ExternalInput").ap()
    w_out = nc.dram_tensor("w_out", [D, D], F32, kind="ExternalInput").ap()
    w_up = nc.dram_tensor("w_up", [D, 2 * DFF], F32, kind="ExternalInput").ap()
    w_down = nc.dram_tensor("w_down", [DFF, D], F32, kind="ExternalInput").ap()
    wg2 = nc.dram_tensor("wg2", [16, 256], F32, kind="ExternalInput").ap()
    cst_d = nc.dram_tensor("cst", [128, NCST], F32, kind="ExternalInput").ap()
    outT = nc.dram_tensor("outT", [128, KC, S], F32, kind="ExternalOutput").ap()
    dbg_out = {}
    for name, shape, dt in debug:
        dbg_out[name] = nc.dram_tensor("dbg_" + name, list(shape), dt, kind="ExternalOutput").ap()

    p = Prog()
    ARENA_BYTES = 207 * 1024
    import contextlib
    es = contextlib.ExitStack()
    with es:
        big = es.enter_context(nc.sbuf_tensor("big", [128, ARENA_BYTES // 4], F32))
        ps = es.enter_context(nc.psum_tensor("ps", [128, 8, 512], F32))
        A = Arena(big, ARENA_BYTES)

        def bank(b):
            return ps[:, b, :]

        PS = lambda b: ("ps", b)

        cst = A.view("cst", F32, [NCST])
        ones_bf = A.view("ones", BF16, [128])
        ident = A.view("ident", BF16, [128])
        maskT = A.view("maskT", BF16, [128])
        mask2 = A.view("mask2", BF16, [128])
        misc = A.view("misc", F32, [16])
        wg2_sb = A.view("wg2", F32, [256])
        wslot_bytes = 8192
        wsl = [A.alloc(f"wslot{i}", wslot_bytes) for i in range(3)]
        wbase = wsl[0]
        assert wsl[1] == wbase + wslot_bytes and wsl[2] == wbase + 2 * wslot_bytes

        def wview(byte_off, shape, dtype=BF16):
            n = int(np.prod(shape))
            esz = 2
            o = wbase + byte_off
            ap = big[:, o // 4:(o + n * esz) // 4].bitcast(dtype)
            if len(shape) == 2:
                ap = ap.rearrange("p (a b) -> p a b", a=shape[0])
            elif len(shape) == 3:
                ap = ap.rearrange("p (a b c) -> p a b c", a=shape[0], b=shape[1])
            return ap

        p.op("sp", lambda e: e.dma_start(out=cst, in_=cst_d), writes=[("cst",)], dma="cst")
        p.op("sp", lambda e: e.dma_start(out=wg2_sb[0:16, :], in_=wg2), writes=[("wg2",)], dma="wg2")
        p.op("pool", lambda e: e.memset(ones_bf, 1.0), writes=[("ones",)])
        p.op("pool", lambda e: e.affine_select(out=maskT, in_=ones_bf, pattern=[[1, 128]], compare_op=ALU.is_ge,
                                               fill=0.0, base=0, channel_multiplier=-1),
             reads=[("ones",)], writes=[("maskT",)])
        p.op("pool", lambda e: e.affine_select(out=ident, in_=ones_bf, pattern=[[1, 128]], compare_op=ALU.is_equal,
                                               fill=0.0, base=0, channel_multiplier=-1),
             reads=[("ones",)], writes=[("ident",)])
        p.op("pool", lambda e: e.tensor_copy(out=mask2, in_=maskT), reads=[("maskT",)], writes=[("mask2",)])
        p.op("pool", lambda e: e.memset(mask2[0:64, 64:128], 0.0), writes=[("mask2",)])
        neg_lam = misc[:, 0:1]
        w8 = misc[:, 1:2]
        lt = A.view("lamtmp", F32, [2, 64])
        p.op("dve", lambda e: e.tensor_tensor(out=lt[:, 0, :], in0=cst[:, C_LQ1:C_LQ1 + 64], in1=cst[:, C_LK1:C_LK1 + 64], op=ALU.mult),
             reads=[("cst",)], writes=[("lt", 0)])
        p.op("dve", lambda e: e.tensor_tensor(out=lt[:, 1, :], in0=cst[:, C_LQ2:C_LQ2 + 64], in1=cst[:, C_LK2:C_LK2 + 64], op=ALU.mult),
             reads=[("cst",)], writes=[("lt", 1)])
        p.op("dve", lambda e: e.reduce_sum(out=misc[:, 2:4], in_=lt, axis=mybir.AxisListType.X), reads=[("lt",)], writes=[("misc", "s")])
        p.op("act", lambda e: e.activation(out=misc[:, 4:6], in_=misc[:, 2:4], func=AF.Exp), reads=[("misc", "s")], writes=[("misc", "e")])
        p.op("dve", lambda e: e.tensor_tensor(out=misc[:, 6:7], in0=misc[:, 5:6], in1=misc[:, 4:5], op=ALU.subtract),
             reads=[("misc", "e")], writes=[("misc", "d")])
        p.op("dve", lambda e: e.tensor_scalar(out=neg_lam, in0=misc[:, 6:7], scalar1=-0.2, scalar2=None, op0=ALU.add),
             reads=[("misc", "d")], writes=[("misc", "nl")])
        p.op("dve", lambda e: e.tensor_scalar(out=w8, in0=cst[:, C_SUB:C_SUB + 1], scalar1=0.8, scalar2=None, op0=ALU.mult),
             reads=[("cst",)], writes=[("misc", "w8")])

        def wdma(dst, src, tok, key):
            toks = [("w", 0), ("w", 1)] if tok is None else [tok]
            p.op("pool", lambda e: e.dma_start(out=dst, in_=src), writes=toks, dma=key)

        def w_in_cols(c0, c1):
            return w_in[:, c0:c1].rearrange("(kc p) n -> p kc n", p=128)

        def rstd_from_ss(ss_ps_ap, out_ap, n, reads, writes):
            p.op("act", lambda e: e.activation(out=out_ap, in_=ss_ps_ap, func=AF.Ln, scale=1.0 / n, bias=eps_ap),
                 reads=reads + [("misc", "eps")], writes=writes)
            p.op("act", lambda e: e.activation(out=out_ap, in_=out_ap, func=AF.Exp, scale=-0.5),
                 reads=writes, writes=writes)

        eps_ap = misc[:, 8:9]
        one_ap = misc[:, 9:10]
        p.op("pool", lambda e: e.memset(eps_ap, EPS), writes=[("misc", "eps")])
        p.op("pool", lambda e: e.memset(one_ap, 1.0), writes=[("misc", "one")])

        def dbg(name, ap, reads):
            if name in dbg_out:
                p.op("sp", lambda e: e.dma_start(out=dbg_out[name], in_=ap), reads=reads, writes=[("dbg", name)], dma="dbg_" + name)

        wA = [wview(i * wslot_bytes, [KC, 512]) for i in range(3)]
        w_r = A.view("w_r", BF16, [KC, 16])
        wdma(w_r, w_in_cols(2560, 2576), ("w_r",), "wr")
        wdma(wA[1], w_in_cols(2048, 2560), ("w", 1), "w1")
        gv_tok = A.view("gv_tok", BF16, [16, 512])
        grT = A.view("grT", F32, [S])
        v_tok = A.view("v_tok", BF16, [16, 512], top=True)
        psrot = [0]

        def tm_group(tkb, wtile, tok_w, dst, dst_name, evac, bank_base):
            b = bank_base + psrot[0] % 4
            psrot[0] += 1
            tsl = slice(tkb * 128, (tkb + 1) * 128)
            for kc in range(KC):
                p.op("pe", lambda e, kc=kc: e.matmul(bank(b), lhsT=hT[:, kc, tsl], rhs=wtile[:, kc, :], start=(kc == 0), stop=(kc == KC - 1)),
                     reads=[tok_w, ("hT", tkb // 4)], writes=[PS(b)])
            if evac == "act":
                p.op("act", lambda e: e.activation(out=dst[:, tkb, :], in_=bank(b), func=AF.Copy), reads=[PS(b)], writes=[(dst_name, tkb)])
            else:
                p.op("dve", lambda e: e.tensor_copy(out=dst[:, tkb, :], in_=bank(b)), reads=[PS(b)], writes=[(dst_name, tkb)])
        hT = A.view("hT", BF16, [KC, S])
        xs = [A.view(f"xs{i}", F32, [KC, 512]) for i in range(2)]
        sq1 = A.view("sq1", BF16, [KC, 512])
        rs1 = A.view("rs1", F32, [512])
        def a1_proj(tb):
            tsl = slice(tb * 512, (tb + 1) * 512)
            for kc in range(KC):
                p.op("pe", lambda e, kc=kc: e.matmul(bank(2)[0:16, :], lhsT=w_r[:, kc, 0:16], rhs=hT[:, kc, tsl], start=(kc == 0), stop=(kc == KC - 1)),
                     reads=[("w_r",), ("hT", tb)], writes=[PS(2)])
            p.op("act", lambda e: e.activation(out=grT[0:16, tb * 512:(tb + 1) * 512], in_=bank(2)[0:16, :], func=AF.Copy),
                 reads=[PS(2)], writes=[("grT", tb)])
            for i in range(4):
                tm_group(4 * tb + i, wA[1], ("w", 1), gv_tok, "gv_tok", "dve", 4)

        for tb in range(4):
            sl = tb % 2
            tsl = slice(tb * 512, (tb + 1) * 512)
            p.op("sp", lambda e, sl=sl, tsl=tsl: e.dma_start(out=xs[sl], in_=xT[:, :, tsl]), writes=[("xs", sl)], dma=f"xs{sl}")
            p.op("act", lambda e, sl=sl: e.activation(out=sq1, in_=xs[sl], func=AF.Square), reads=[("xs", sl)], writes=[("sq1",)])
            b = tb % 2
            for kc in range(KC):
                p.op("pe", lambda e, kc=kc, b=b: e.matmul(bank(b), lhsT=ones_bf, rhs=sq1[:, kc, :], start=(kc == 0), stop=(kc == KC - 1)),
                     reads=[("ones",), ("sq1",)], writes=[PS(b)])
            rstd_from_ss(bank(b), rs1, D, [PS(b)], [("rs1",)])
            for kc in range(KC):
                eng = "dve" if kc % 2 == 0 else "dve"
                p.op(eng, lambda e, kc=kc, sl=sl, tsl=tsl: e.scalar_tensor_tensor(
                    out=hT[:, kc, tsl], in0=xs[sl][:, kc, :], scalar=cst[:, C_LN1 + kc:C_LN1 + kc + 1], in1=rs1,
                    op0=ALU.mult, op1=ALU.mult),
                    reads=[("xs", sl), ("rs1",), ("cst",)], writes=[("hT", tb, kc)])
            if tb >= 1:
                a1_proj(tb - 1)
            if tb == 1:
                p.op("pool", lambda e: e.dma_start(out=wA[2], in_=w_in_cols(1024, 1536)), reads=[("xs", 1)], writes=[("w", 2)], dma="w2")
                p.op("pool", lambda e: e.dma_start(out=wA[0], in_=w_in_cols(1536, 2048)), reads=[("xs", 1)], writes=[("w", 0)], dma="w0")
        a1_proj(3)
        dbg("hT", hT, [("hT",)])
        p.barrier()
        A.release("xs0", "xs1", "sq1", "rs1", "lamtmp")
        if stop_after == "A1":
            return finish(nc, p, es, None)


        def proj_fm(wtile, c0, ncols, tok_w, consume):
            for tb in range(4):
                b = psrot[0] % 4
                psrot[0] += 1
                tsl = slice(tb * 512, (tb + 1) * 512)
                for kc in range(KC):
                    p.op("pe", lambda e, kc=kc, b=b, tsl=tsl: e.matmul(bank(b)[0:ncols, :], lhsT=wtile[:, kc, c0:c0 + ncols], rhs=hT[:, kc, tsl],
                                                                      start=(kc == 0), stop=(kc == KC - 1)),
                         reads=[tok_w, ("hT", tb)], writes=[PS(b)], nobar=True)
                consume(tb, bank(b)[0:ncols, :], b)

        def proj_tm(wtile, tok_w, consume):
            for tkb in range(16):
                b = psrot[0] % 4
                psrot[0] += 1
                tsl = slice(tkb * 128, (tkb + 1) * 128)
                for kc in range(KC):
                    p.op("pe", lambda e, kc=kc, b=b, tsl=tsl: e.matmul(bank(b), lhsT=hT[:, kc, tsl], rhs=wtile[:, kc, :],
                                                                      start=(kc == 0), stop=(kc == KC - 1)),
                         reads=[tok_w, ("hT", tkb // 4)], writes=[PS(b)])
                consume(tkb, bank(b), b)


        q_decT = A.view("q_decT", BF16, [2, S])
        k_decT = A.view("k_decT", BF16, [2, S])
        kteT = A.view("kteT", BF16, [2, S])
        kte_tok = A.view("kte_tok", BF16, [16, 256])
        sgo = A.view("sgo", BF16, [4, S])
        resetm = A.view("resetm", F32, [S])
        T0 = A.view("T0", F32, [S])
        T1 = A.view("T1", F32, [S])
        T2 = A.view("T2", F32, [S])
        ebT = A.view("ebT", F32, [S])
        enbT = A.view("enbT", F32, [S])
        ek2T = A.view("ek2T", F32, [S])
        ebl = A.view("ebl", F32, [2, 32])

        p.op("pool", lambda e: e.memset(resetm, 1.0), writes=[("resetm",)])
        p.op("pool", lambda e: e.memset(resetm[:, 0:S:64], 0.0), writes=[("resetm",)])


        def go_groups(c):
            out = []
            for c4 in (2 * c, 2 * c + 1):
                for tb in range(4):
                    def g(c4=c4, tb=tb):
                        b = psrot[0] % 4
                        psrot[0] += 1
                        tsl = slice(tb * 512, (tb + 1) * 512)
                        for kc in range(KC):
                            p.op("pe", lambda e, kc=kc: e.matmul(bank(b), lhsT=wA[2][:, kc, c4 * 128:(c4 + 1) * 128], rhs=hT[:, kc, tsl],
                                                                 start=(kc == 0), stop=(kc == KC - 1)),
                                 reads=[("w", 2), ("hT", tb)], writes=[PS(b)])
                        p.op("act", lambda e: e.activation(out=sgo[:, c4, tsl], in_=bank(b), func=AF.Copy), reads=[PS(b)], writes=[("sgo", c4, tb)])
                    out.append(g)
            return out

        fill = []

        def FILL(n=2):
            for _ in range(n):
                if fill:
                    fill.pop(0)()

        for c in range(2):
            if c == 0:
                fill.extend([(lambda tkb=tkb: tm_group(tkb, wA[2], ("w", 2), v_tok, "v_tok", "act", 0)) for tkb in range(16)])
            else:
                fill.extend(go_groups(0) + go_groups(1))
            for tb in range(4):
                p.op("pe", lambda e, tb=tb, c=c: e.matmul(bank(4 + tb), lhsT=wg2_sb[0:16, c * 128:(c + 1) * 128],
                                                         rhs=grT[0:16, tb * 512:(tb + 1) * 512], start=True, stop=True),
                     reads=[("wg2",), ("grT", tb)], writes=[PS(4 + tb)], nobar=True)
            zps = ps[:, 4:8, :]
            T0v = T0.rearrange("p (a b) -> p a b", a=4)
            rd4 = [PS(4), PS(5), PS(6), PS(7)]
            p.op("act", lambda e, c=c: e.activation(out=T0v, in_=zps, func=AF.Identity, bias=cst[:, C_BG + c:C_BG + c + 1], scale=1.0),
                 reads=rd4 + [("cst",)], writes=[("T0",)])
            FILL()
            p.op("act", lambda e: e.activation(out=T1, in_=T0, func=AF.Abs), reads=[("T0",)], writes=[("T1",)])
            FILL()
            p.op("act", lambda e: e.activation(out=T1, in_=T1, func=AF.Exp, scale=-1.0), reads=[("T1",)], writes=[("T1",)])
            FILL()
            p.op("act", lambda e: e.activation(out=T1, in_=T1, func=AF.Ln, bias=one_ap, scale=1.0), reads=[("T1",), ("misc", "one")], writes=[("T1",)])
            FILL()
            p.op("dve", lambda e: e.tensor_scalar(out=T2, in0=T0, scalar1=0.0, scalar2=1.0 / 16.0, op0=ALU.min, op1=ALU.mult),
                 reads=[("T0",)], writes=[("T2",)])
            p.op("dve", lambda e: e.scalar_tensor_tensor(out=T1, in0=T1, scalar=-1.0 / 16.0, in1=T2, op0=ALU.mult, op1=ALU.add),
                 reads=[("T1",), ("T2",)], writes=[("T1",)])
            FILL()
            p.op("dve", lambda e: e.tensor_tensor_scan(out=T0, data0=resetm, data1=T1, initial=0.0, op0=ALU.mult, op1=ALU.add),
                 reads=[("T1",), ("resetm",)], writes=[("T0",)])
            FILL()
            p.op("act", lambda e: e.activation(out=ebT, in_=T0, func=AF.Exp), reads=[("T0",)], writes=[("ebT",)])
            p.op("act", lambda e: e.activation(out=enbT, in_=T0, func=AF.Exp, scale=-1.0), reads=[("T0",)], writes=[("enbT",)])
            FILL()
            bl = T0[:, 63:S:64].unsqueeze(2).broadcast_to([128, 32, 64])
            p.op("dve", lambda e, bl=bl: e.tensor_tensor(out=T2.rearrange("p (a b) -> p a b", a=32), in0=bl,
                                                        in1=T0.rearrange("p (a b) -> p a b", a=32), op=ALU.subtract),
                 reads=[("T0",)], writes=[("T2",)])
            p.op("act", lambda e: e.activation(out=ek2T, in_=T2, func=AF.Exp), reads=[("T2",)], writes=[("ek2T",)])
            p.op("dve", lambda e, c=c: e.tensor_copy(out=ebl[:, c, :], in_=ebT[:, 63:S:64]), reads=[("ebT",)], writes=[("ebl", c)])
            while fill:
                FILL()
            if c == 0:
                wdma(wA[2], w_in_cols(2576, 3088), ("w", 2), "w2")
            if c == 0:
                dbg("bT", T0, [("T0",)])
                dbg("ebT", ebT, [("ebT",)])

            def cons_q(tb, pap, b, c=c):
                tsl = slice(tb * 512, (tb + 1) * 512)
                p.op("dve", lambda e: e.scalar_tensor_tensor(out=q_decT[:, c, tsl], in0=pap, scalar=0.125, in1=ebT[:, tsl],
                                                             op0=ALU.mult, op1=ALU.mult),
                     reads=[PS(b), ("ebT",)], writes=[("q_decT", c, tb)])
            proj_fm(wA[0], c * 128, 128, ("w", 0), cons_q)

            def cons_k(tb, pap, b, c=c):
                tsl = slice(tb * 512, (tb + 1) * 512)
                p.op("dve", lambda e: e.tensor_tensor(out=k_decT[:, c, tsl], in0=pap, in1=enbT[:, tsl], op=ALU.mult),
                     reads=[PS(b), ("enbT",)], writes=[("k_decT", c, tb)])
                p.op("dve", lambda e: e.tensor_tensor(out=kteT[:, c, tsl], in0=pap, in1=ek2T[:, tsl], op=ALU.mult),
                     reads=[PS(b), ("ek2T",)], writes=[("kteT", c, tb)])
            proj_fm(wA[0], 256 + c * 128, 128, ("w", 0), cons_k)

        for c in range(2):
            for g in range(4):
                b = psrot[0] % 4
                psrot[0] += 1
                pst = bank(b)[:, 0:256].bitcast(BF16).rearrange("p (a b) -> p a b", a=4)
                for i in range(4):
                    tkb = g * 4 + i
                    p.op("pe", lambda e, i=i, tkb=tkb, c=c, pst=pst: e.transpose(out=pst[:, i, :], in_=kteT[:, c, tkb * 128:(tkb + 1) * 128], identity=ident),
                         reads=[("kteT", c, tkb // 4), ("ident",)], writes=[PS(b)])
                p.op("act", lambda e, g=g, c=c, pst=pst: e.activation(out=kte_tok[:, g * 4:(g + 1) * 4, c * 128:(c + 1) * 128], in_=pst, func=AF.Copy),
                     reads=[PS(b)], writes=[("kte_tok", c, g)])


        wdma(wA[0], w_in_cols(0, 512), ("w", 0), "w0")
        wdma(wA[1], w_in_cols(512, 1024), ("w", 1), "w1")

        dbg("q_decT", q_decT, [("q_decT",)])
        dbg("k_decT", k_decT, [("k_decT",)])
        dbg("kte_tok", kte_tok, [("kte_tok",)])
        dbg("gv_tok", gv_tok, [("gv_tok",)])
        dbg("sgo", sgo, [("sgo",)])
        dbg("ebl", ebl, [("ebl",)])
        p.barrier(exempt=("dma:w0", "dma:w1", "dma:w2"))
        A.release("kteT", "grT", "resetm", "T0", "T1", "T2", "ebT", "enbT", "ek2T")
        if stop_after == "A2":
            return finish(nc, p, es, None)

        cc = A.view("cc", BF16, [KC, S], top=True)
        Sbf = A.view("Sbf", BF16, [33, 2, 128])
        aT_halves = [A.view("aT_lo", BF16, [8, 4, 128]), A.view("aT_hi", BF16, [8, 4, 128])]

        def aTv(tkb):
            return aT_halves[tkb // 8][:, tkb % 8]
        Sst2 = A.view("Sst", F32, [2, 2, 128])
        e_sq = A.view("e_sq", BF16, [512], top=True)
        e_rs = A.view("e_rs", F32, [512], top=True)
        e_sq1 = A.view("e_t", BF16, [512], top=True)
        e_rs1 = A.view("e_rs1", F32, [512], top=True)

        for tkb in range(16):
            pair = (tkb % 2) * 2
            tsl = slice(tkb * 128, (tkb + 1) * 128)
            for h in range(4):
                c, hh = h // 2, h % 2
                rows = slice(hh * 64, (hh + 1) * 64)
                b = pair + hh
                p.op("pe", lambda e, b=b, h=h, c=c, rows=rows, tsl=tsl: e.matmul(bank(b)[:, c * 128:(c + 1) * 128], lhsT=k_decT[rows, c, tsl],
                                                                                  rhs=q_decT[rows, c, tsl], start=True, stop=True),
                     reads=[("k_decT", c, tkb // 4), ("q_decT", c, tkb // 4)], writes=[PS(b)], nobar=True)
            for hh in range(2):
                b = pair + hh
                p.op("dve", lambda e, b=b, tkb=tkb, hh=hh: e.tensor_tensor(out=aTv(tkb)[:, hh:4:2, :], in0=bank(b)[:, 0:256].rearrange("p (a b) -> p a b", a=2),
                                                                       in1=mask2.unsqueeze(1).broadcast_to([128, 2, 128]), op=ALU.mult),
                     reads=[PS(b), ("mask2",)], writes=[("aT", tkb, hh)])

        for c4 in range(4):
            p.op("act", lambda e, c4=c4: e.activation(out=sgo[:, c4, :], in_=sgo[:, c4, :], func=AF.Silu), reads=[("sgo", c4)], writes=[("sgo", c4)])

        p.op("pool", lambda e: e.memset(Sbf[:, 0, :, :], 0.0), writes=[("Sbf", 0)])
        p.op("pool", lambda e: e.memset(Sst2[:, 0, :, :], 0.0), writes=[("Sst", 0)])
        for tkb in range(16):
            bp = 4 + 2 * (tkb % 2)
            for c in range(2):
                for hh in range(2):
                    h = 2 * c + hh
                    orow = slice(hh * 64, (hh + 1) * 64)
                    for s in range(2):
                        trow = slice(s * 64, (s + 1) * 64)
                        p.op("pe", lambda e, c=c, hh=hh, h=h, trow=trow, orow=orow, s=s, bp=bp, tkb=tkb: e.matmul(
                            bank(bp + s)[orow, c * 128:(c + 1) * 128], lhsT=kte_tok[trow, tkb, c * 128 + hh * 64:c * 128 + hh * 64 + 64],
                            rhs=gv_tok[trow, tkb, h * 128:(h + 1) * 128], start=True, stop=True),
                            reads=[("kte_tok", c, tkb // 4), ("gv_tok", tkb)], writes=[PS(bp + s)])
            for s in range(2):
                n = 2 * tkb + s
                so, sn = n % 2, (n + 1) % 2
                for c in range(2):
                    p.op("dve", lambda e, c=c, n=n, so=so, sn=sn, s=s, bp=bp: e.scalar_tensor_tensor(out=Sst2[:, sn, c, :], in0=Sst2[:, so, c, :], scalar=ebl[:, c, n:n + 1],
                                                                                            in1=bank(bp + s)[:, c * 128:(c + 1) * 128], op0=ALU.mult, op1=ALU.add),
                         reads=[PS(bp + s), ("Sst", so, c), ("ebl", c)], writes=[("Sst", sn, c)])
                p.op("act", lambda e, n=n, sn=sn: e.activation(out=Sbf[:, n + 1, :, :], in_=Sst2[:, sn, :, :], func=AF.Copy), reads=[("Sst", sn)], writes=[("Sbf", n + 1)])

        it = 0
        epi_prev = []

        def a3_epi(tb, h):
            b = h
            tsl = slice(tb * 512, (tb + 1) * 512)
            k = it_box[0] % 2
            it_box[0] += 1
            sq_, rs_ = (e_sq, e_rs) if k == 0 else (e_sq1, e_rs1)
            tq, tr = ("e_sqA", k), ("e_rsA", k)
            bs = 6 + k

            def part1():
                p.op("act", lambda e: e.activation(out=sq_, in_=bank(b), func=AF.Square), reads=[PS(b)], writes=[tq])

            def part2():
                p.op("pe", lambda e: e.matmul(bank(bs), lhsT=ones_bf, rhs=sq_, start=True, stop=True), reads=[tq, ("ones",)], writes=[PS(bs)])
                rstd_from_ss(bank(bs), rs_, 128, [PS(bs)], [tr])
                p.op("dve", lambda e: e.scalar_tensor_tensor(out=rs_, in0=bank(b), scalar=cst[:, C_GNW:C_GNW + 1], in1=rs_, op0=ALU.mult, op1=ALU.mult),
                     reads=[PS(b), tr, ("cst",)], writes=[tr])
                p.op("dve", lambda e: e.tensor_tensor(out=cc[:, 4 + h, tsl], in0=rs_, in1=sgo[:, h, tsl], op=ALU.mult),
                     reads=[tr, ("sgo", h, tb)], writes=[("cc", 4 + h, tb)])
            return part1, part2

        it_box = [0]
        carry = None
        for tb in range(4):
            for c in range(2):
                hs = (2 * c, 2 * c + 1)
                for i in range(4):
                    tkb = tb * 4 + i
                    osl = slice(i * 128, (i + 1) * 128)
                    for h in hs:
                        p.op("pe", lambda e, h=h, osl=osl, tkb=tkb: e.matmul(bank(h)[:, osl], lhsT=gv_tok[:, tkb, h * 128:(h + 1) * 128], rhs=aTv(tkb)[:, h, :], start=True, stop=False),
                             reads=[("gv_tok", tkb), ("aT", tkb)], writes=[PS(h)])
                    for s in range(2):
                        n = 2 * tkb + s
                        o2 = slice(i * 128 + s * 64, i * 128 + s * 64 + 64)
                        t2 = slice(tkb * 128 + s * 64, tkb * 128 + s * 64 + 64)
                        for h in hs:
                            rows = slice((h % 2) * 64, (h % 2) * 64 + 64)
                            p.op("pe", lambda e, h=h, rows=rows, s=s, n=n, o2=o2, t2=t2, c=c: e.matmul(bank(h)[:, o2], lhsT=Sbf[rows, n, c, :], rhs=q_decT[rows, c, t2],
                                                                                              start=False, stop=(s == 1)),
                                 reads=[("Sbf", n), ("q_decT", c, tkb // 4)], writes=[PS(h)])
                for h in hs:
                    p1, p2 = a3_epi(tb, h)
                    p1()
                    if carry is not None:
                        carry()
                    carry = p2
        if carry is not None:
            carry()
        p.barrier(exempt=("dma:w0", "dma:w1", "dma:w2"))
        A.release("aT_lo", "aT_hi", "Sbf", "Sst", "q_decT", "k_decT", "kte_tok", "gv_tok", "sgo", "ebl")
        if stop_after == "A3":
            return finish(nc, p, es, (cc, "cc"), dbg_out)

        qT = A.view("qT", BF16, [4, S])
        kT = A.view("kT", BF16, [4, S])
        flip = [0]

        def cons_qk(dst, name, h):
            def cons(tb, pap, b):
                flip[0] ^= 1
                if flip[0]:
                    p.op("act", lambda e: e.activation(out=dst[:, h, tb * 512:(tb + 1) * 512], in_=pap, func=AF.Copy), reads=[PS(b)], writes=[(name, h, tb)])
                else:
                    p.op("dve", lambda e: e.tensor_copy(out=dst[:, h, tb * 512:(tb + 1) * 512], in_=pap), reads=[PS(b)], writes=[(name, h, tb)])
            return cons

        def cons_dv(tkb, pap, b):
            flip[0] ^= 1
            if flip[0]:
                p.op("act", lambda e: e.activation(out=v_tok[:, tkb, :], in_=pap, func=AF.Copy), reads=[PS(b)], writes=[("v_tok", tkb)])
            else:
                p.op("dve", lambda e: e.tensor_copy(out=v_tok[:, tkb, :], in_=pap), reads=[PS(b)], writes=[("v_tok", tkb)])
        for h in range(4):
            proj_fm(wA[0], h * 128, 128, ("w", 0), cons_qk(qT, "qT", h))
            proj_fm(wA[1], h * 128, 128, ("w", 1), cons_qk(kT, "kT", h))
        w_o = wview(0, [KC, D])
        wdma(w_o, w_out.rearrange("(kc p) n -> p kc n", p=128), None, "w0")

        def filler_ops(hn):
            out = []
            for wi, dst, name in ((0, qT, "qT"), (1, kT, "kT")):
                for tb in range(4):
                    tsl = slice(tb * 512, (tb + 1) * 512)
                    for kc in range(KC):
                        def mo(wi=wi, dst=dst, name=name, tb=tb, tsl=tsl, kc=kc):
                            p.op("pe", lambda e: e.matmul(bank(7), lhsT=wA[wi][:, kc, hn * 128:(hn + 1) * 128], rhs=hT[:, kc, tsl],
                                                          start=(kc == 0), stop=(kc == KC - 1)),
                                 reads=[("w", wi), ("hT", tb)], writes=[PS(7)])
                            if kc == KC - 1:
                                p.op("dve", lambda e: e.tensor_copy(out=dst[:, hn, tsl], in_=bank(7)), reads=[PS(7)], writes=[(name, hn, tb)])
                        out.append(mo)
            return out

        NP = 4
        p_sb = A.view("p_sb", BF16, [NP, 2, 512])
        a_r = A.view("a_r", F32, [2, 512])
        a_o = A.view("a_o", F32, [512])
        xsB = [A.view(f"xs{i}", F32, [KC, 512], top=True) for i in range(2)]
        o_s = A.view("o_s", F32, [2, 512])
        item = [0]
        pending = []

        def flush_one():
            if pending:
                pending.pop(0)()

        blocks = [(h, qb) for h in range(4) for qb in range(4)]
        steps = [(bi, kb) for bi, (h, qb) in enumerate(blocks) for kb in range(4 * (qb + 1))]

        def emit_S(step):
            bi, kb = step
            h, qb = blocks[bi]
            qs = qb * 512
            j = kb - 4 * qb
            q0 = 128 * j if j > 0 else 0
            sl_ = item[0] % NP
            sbk = 2 * (item[0] % 2)
            item[0] += 1
            for c in range(2):
                rows = slice(c * 64, (c + 1) * 64)
                p.op("pe", lambda e, c=c, rows=rows: e.matmul(bank(sbk + c)[:, q0:512], lhsT=kT[rows, h, kb * 128:(kb + 1) * 128],
                                                              rhs=qT[rows, h, qs + q0:qs + 512], start=True, stop=True),
                     reads=[("kT", h, kb // 4), ("qT", h, qb)], writes=[PS(sbk + c)])
            for c in range(2):
                p.op("act", lambda e, c=c: e.activation(out=p_sb[:, sl_, c, q0:512], in_=bank(sbk + c)[:, q0:512], func=AF.Exp, scale=0.125),
                     reads=[PS(sbk + c)], writes=[("p_sb", sl_, c)])
                if j >= 0:
                    d0 = 128 * j
                    p.op("dve", lambda e, c=c, d0=d0: e.tensor_tensor(out=p_sb[:, sl_, c, d0:d0 + 128], in0=p_sb[:, sl_, c, d0:d0 + 128], in1=maskT, op=ALU.mult),
                         reads=[("p_sb", sl_, c), ("maskT",)], writes=[("p_sb", sl_, c)])
            return (sl_, q0)

        def emit_AV(step, res):
            bi, kb = step
            h, qb = blocks[bi]
            nkb = 4 * (qb + 1)
            sl_, q0 = res
            st, sp_ = (kb == 0), (kb == nkb - 1)
            for c in range(2):
                p.op("pe", lambda e, c=c: e.matmul(bank(4 + c)[:, q0:512], lhsT=v_tok[:, kb, h * 128:(h + 1) * 128], rhs=p_sb[:, sl_, c, q0:512], start=st, stop=sp_),
                     reads=[("v_tok", kb), ("p_sb", sl_, c)], writes=[PS(4 + c)])
                p.op("pe", lambda e, c=c: e.matmul(bank(6 + c)[:, q0:512], lhsT=ones_bf, rhs=p_sb[:, sl_, c, q0:512], start=st, stop=sp_),
                     reads=[("ones",), ("p_sb", sl_, c)], writes=[PS(6 + c)])

        def make_epilogue(bi):
            h, qb = blocks[bi]
            tsl = slice(qb * 512, qb * 512 + 512)

            def s2():
                p.op("act", lambda e: e.activation(out=a_r, in_=a_r, func=AF.Exp, scale=-1.0), reads=[("a_r",)], writes=[("a_r",)])

            def s3():
                p.op("dve", lambda e: e.tensor_tensor(out=o_s, in0=o_s, in1=a_r, op=ALU.mult), reads=[("o_s",), ("a_r",)], writes=[("o_s",)])

            def s4():
                p.op("dve", lambda e: e.scalar_tensor_tensor(out=a_o, in0=o_s[:, 1, :], scalar=neg_lam, in1=o_s[:, 0, :], op0=ALU.mult, op1=ALU.add),
                     reads=[("o_s",), ("misc", "nl")], writes=[("a_o",)])
                p.op("dve", lambda e: e.tensor_tensor(out=e_sq, in0=a_o, in1=a_o, op=ALU.mult), reads=[("a_o",)], writes=[("e_sq",)])

            def s5():
                b_ = 2 * ((item[0] + 1) % 2)
                p.op("pe", lambda e: e.matmul(bank(b_), lhsT=ones_bf, rhs=e_sq, start=True, stop=True), reads=[("e_sq",), ("ones",)], writes=[PS(b_)])
                rstd_from_ss(bank(b_), e_rs, 128, [PS(b_)], [("e_rs",)])

            def s7():
                p.op("dve", lambda e: e.scalar_tensor_tensor(out=cc[:, h, tsl], in0=a_o, scalar=w8, in1=e_rs, op0=ALU.mult, op1=ALU.mult),
                     reads=[("a_o",), ("e_rs",), ("misc", "w8")], writes=[("cc", h, qb)])
            return [s2, s3, s4, s5, s7]

        prev = emit_S(steps[0])
        for si, step in enumerate(steps):
            bi, kb = step
            nkb = 4 * (blocks[bi][1] + 1)
            nxt = emit_S(steps[si + 1]) if si + 1 < len(steps) else None
            emit_AV(step, prev)
            prev = nxt
            for _ in range(2 if nkb == 4 else 1):
                flush_one()
            if kb == nkb - 1:
                while pending:
                    flush_one()
                for c in range(2):
                    p.op("dve", lambda e, c=c: e.tensor_copy(out=o_s[:, c, :], in_=bank(4 + c)), reads=[PS(4 + c)], writes=[("o_s", c)])
                    p.op("act", lambda e, c=c: e.activation(out=a_r[:, c, :], in_=bank(6 + c), func=AF.Ln), reads=[PS(6 + c)], writes=[("a_r", c)])
                pending = make_epilogue(bi)
        while pending:
            flush_one()
        for tb_ in range(2):
            p.op("sp", lambda e, tb_=tb_: e.dma_start(out=xsB[tb_], in_=xT[:, :, tb_ * 512:(tb_ + 1) * 512]), writes=[("xs", tb_)], dma=f"xs{tb_}")
        p.barrier(exempt=("dma:w0", "dma:w1", "dma:w2", "dma:xs0", "dma:xs1"))
        A.release("qT", "kT", "v_tok", "p_sb", "a_r", "a_o", "e_t", "e_rs1", "hT", "o_s")
        if stop_after == "A5":
            return finish(nc, p, es, (cc, "cc"), dbg_out)

        dbg("w_o", w_o, [("w",)])
        dbg("cc", cc, [("cc",)])
        x1T = A.view("x1T", F32, [KC, S])
        rstd2 = A.view("rstd2", F32, [S])
        sqB = A.view("sq1", BF16, [KC, 512])
        h2 = A.view("h2", BF16, [KC, 1024])
        wu = [wview(16384, [KC, 2, 128]), wview(20480, [KC, 2, 128]), wview(0, [KC, 2, 128])]
        wd = [wview(4096 + i * 5632, [NFF, 128]) for i in range(2)]

        def load_wu(ffc, slot):
            for g in range(2):
                c0 = g * DFF + ffc * 128
                wdma(wu[slot][:, :, g, :], w_up[:, c0:c0 + 128].rearrange("(kc p) n -> p kc n", p=128), ("wu", slot, g), f"wu{slot}{g}")

        def load_wd(dmc, slot):
            wdma(wd[slot], w_down[:, dmc * 128:(dmc + 1) * 128].rearrange("(f p) n -> p f n", p=128), ("wd", slot), f"wd{slot}")

        load_wu(0, 0)
        load_wu(1, 1)

        def h2_compute(th):
            t0 = th * 1024
            for kc in range(KC):
                for tb2 in range(2):
                    tsl = slice(t0 + tb2 * 512, t0 + (tb2 + 1) * 512)
                    p.op("dve", lambda e, kc=kc, tb2=tb2, tsl=tsl: e.scalar_tensor_tensor(
                        out=h2[:, kc, tb2 * 512:(tb2 + 1) * 512], in0=x1T[:, kc, tsl], scalar=cst[:, C_LN2 + kc:C_LN2 + kc + 1], in1=rstd2[:, tsl],
                        op0=ALU.mult, op1=ALU.mult),
                        reads=[("x1T", 2 * th + tb2, kc), ("rstd2", 2 * th + tb2), ("cst",)], writes=[("h2", kc, tb2)])

        def b_proj(tb):
            sl = tb % 2
            tsl = slice(tb * 512, (tb + 1) * 512)
            if tb >= 2:
                p.op("sp", lambda e: e.dma_start(out=xsB[sl], in_=xT[:, :, tsl]), writes=[("xs", sl)], dma=f"xs{sl}")
            for dmc in range(KC):
                b = psrot[0] % 4
                psrot[0] += 1
                for kc in range(KC):
                    p.op("pe", lambda e, b=b, kc=kc, dmc=dmc: e.matmul(bank(b), lhsT=w_o[:, kc, dmc * 128:(dmc + 1) * 128], rhs=cc[:, kc, tsl],
                                                                     start=(kc == 0), stop=(kc == KC - 1)),
                         reads=[("w", dmc // 4), ("cc",)], writes=[PS(b)], nobar=True)
                p.op("dve", lambda e, b=b, dmc=dmc: e.tensor_tensor(out=x1T[:, dmc, tsl], in0=bank(b), in1=xsB[sl][:, dmc, :], op=ALU.add),
                     reads=[PS(b), ("xs", sl)], writes=[("x1T", tb, dmc)])

        def b_stats_sq(tb):
            tsl = slice(tb * 512, (tb + 1) * 512)
            p.op("act", lambda e: e.activation(out=sqB, in_=x1T[:, :, tsl], func=AF.Square), reads=[("x1T", tb)], writes=[("sq1",)])

        def b_stats_mm(tb, bs=None):
            tsl = slice(tb * 512, (tb + 1) * 512)
            if bs is None:
                bs = 4 + tb % 2
            for kc in range(KC):
                p.op("pe", lambda e, kc=kc: e.matmul(bank(bs), lhsT=ones_bf, rhs=sqB[:, kc, :], start=(kc == 0), stop=(kc == KC - 1)),
                     reads=[("ones",), ("sq1",)], writes=[PS(bs)])
            rstd_from_ss(bank(bs), rstd2[:, tsl], D, [PS(bs)], [("rstd2", tb)])

        def b_stats(tb):
            b_stats_sq(tb)
            b_stats_mm(tb)

        b_proj(0)
        for tb in range(1, 4):
            b_proj(tb)
            if tb <= 2:
                b_stats(tb - 1)
            if tb == 2:
                h2_compute(0)
        p.barrier()
        A.release("cc", "xs0", "xs1", "e_sq", "e_rs")
        if stop_after == "B":
            return finish(nc, p, es, (x1T, "x1T"), dbg_out)

        actT = A.view("actT", BF16, [NFF, 1024])
        uh = A.view("uh", F32, [2 * NFF, 2])
        NCB = 2
        u_sb = [[A.view(f"u_sb{i}{g}", F32, [1026]) for g in range(2)] for i in range(NCB)]
        y_sb = [[A.view(f"y_sb{i}{g}", F32, [1024]) for g in range(2)] for i in range(NCB)]
        for i_ in range(NCB):
            for g_ in range(2):
                p.op("pool", lambda e, us=u_sb[i_][g_]: e.memset(us[:, 0:2], 0.0), writes=[("u_sb", i_, g_, "h")])

        rsf = A.view("rsf", F32, [512])
        fin_pending = []

        def final_stages(tb):
            tsl = slice(tb * 512, (tb + 1) * 512)

            def f1():
                p.op("act", lambda e: e.activation(out=sqB, in_=x1T[:, :, tsl], func=AF.Square), reads=[("x1T", tb)], writes=[("sq1",)])

            def f2():
                bs = psrot[0] % 8
                psrot[0] += 1
                for kc in range(KC):
                    p.op("pe", lambda e, kc=kc: e.matmul(bank(bs), lhsT=ones_bf, rhs=sqB[:, kc, :], start=(kc == 0), stop=(kc == KC - 1)),
                         reads=[("ones",), ("sq1",)], writes=[PS(bs)])
                rstd_from_ss(bank(bs), rsf, D, [PS(bs)], [("rsf",)])

            def f3():
                for kc in range(KC):
                    p.op("dve", lambda e, kc=kc: e.scalar_tensor_tensor(out=x1T[:, kc, tsl], in0=x1T[:, kc, tsl], scalar=cst[:, C_LNF + kc:C_LNF + kc + 1],
                                                                        in1=rsf, op0=ALU.mult, op1=ALU.mult),
                         reads=[("x1T", tb, kc), ("rsf",), ("cst",)], writes=[("x1T", tb, kc)])
                    if kc % 4 == 3:
                        k0 = kc - 3
                        p.op("sp", lambda e, k0=k0: e.dma_start(out=outT[:, k0:k0 + 4, tsl], in_=x1T[:, k0:k0 + 4, tsl]),
                             reads=[("x1T", tb, k0), ("x1T", tb, k0 + 1), ("x1T", tb, k0 + 2), ("x1T", tb, k0 + 3)], writes=[("out", tb, k0)],
                             dma=f"out{tb % 2}{k0 // 4}")
            return [f1, f2, f3]

        pair_it = 0
        for th in range(2):
            t0 = th * 1024
            for ffc in range(NFF):
                slot = pair_it % 3
                cb = pair_it % NCB
                pair_it += 1
                if ffc + 2 < NFF:
                    load_wu(ffc + 2, (slot + 2) % 3)
                if ffc == NFF - 4:
                    load_wd(0, 0)
                    load_wd(1, 1)
                pb = 4 * (ffc % 2)
                for g in range(2):
                    for tb2 in range(2):
                        b = pb + 2 * g + tb2
                        for kc in range(KC):
                            p.op("pe", lambda e, b=b, kc=kc, g=g, tb2=tb2, slot=slot: e.matmul(bank(b), lhsT=wu[slot][:, kc, g, :], rhs=h2[:, kc, tb2 * 512:(tb2 + 1) * 512],
                                                                                             start=(kc == 0), stop=(kc == KC - 1)),
                                 reads=[("wu", slot, g), ("h2",)], writes=[PS(b)], nobar=True)
                for g in range(2):
                    ch = g * NFF + ffc
                    b0 = pb + 2 * g
                    pss = ps[:, b0:b0 + 2, :]
                    rd = [PS(b0), PS(b0 + 1)]
                    us = u_sb[cb][g]
                    ys = y_sb[cb][g]
                    cw = lambda i, ch=ch: cst[:, C_CW + ch * 3 + i:C_CW + ch * 3 + i + 1]
                    p.op("act", lambda e, us=us, pss=pss: e.activation(out=us[:, 2:1026].rearrange("p (a b) -> p a b", a=2), in_=pss, func=AF.Copy),
                         reads=rd, writes=[("u_sb", cb, g, "m")])
                    if th == 0:
                        p.op("act", lambda e, ch=ch, b0=b0: e.activation(out=uh[:, ch, :], in_=bank(b0 + 1)[:, 510:512], func=AF.Copy),
                             reads=[PS(b0 + 1)], writes=[("uh", ch)])
                    else:
                        p.op("dve", lambda e, us=us, ch=ch: e.tensor_copy(out=us[:, 0:2], in_=uh[:, ch, :]), reads=[("uh", ch)], writes=[("u_sb", cb, g, "h")])
                    p.op("act", lambda e, ys=ys, pss=pss, ch=ch, cw=cw: e.activation(out=ys.rearrange("p (a b) -> p a b", a=2), in_=pss, func=AF.Identity,
                                                                                    scale=cw(2), bias=cst[:, C_CB + ch:C_CB + ch + 1]),
                         reads=rd + [("cst",)], writes=[("y_sb", cb, g)])
                    p.op("dve", lambda e, ys=ys, us=us, cw=cw: e.scalar_tensor_tensor(out=ys, in0=us[:, 1:1025], scalar=cw(1), in1=ys, op0=ALU.mult, op1=ALU.add),
                         reads=[("u_sb", cb, g), ("y_sb", cb, g), ("cst",)], writes=[("y_sb", cb, g)])
                    p.op("dve", lambda e, ys=ys, us=us, cw=cw: e.scalar_tensor_tensor(out=ys, in0=us[:, 0:1024], scalar=cw(0), in1=ys, op0=ALU.mult, op1=ALU.add),
                         reads=[("u_sb", cb, g), ("y_sb", cb, g), ("cst",)], writes=[("y_sb", cb, g)])
                yg, yv = y_sb[cb][0], y_sb[cb][1]
                p.op("act", lambda e, yg=yg: e.activation(out=yg, in_=yg, func=AF.Silu), reads=[("y_sb", cb, 0)], writes=[("y_sb", cb, 0)])
                p.op("dve", lambda e, yg=yg, yv=yv, ffc=ffc: e.tensor_tensor(out=actT[:, ffc, :], in0=yg, in1=yv, op=ALU.mult),
                     reads=[("y_sb", cb, 0), ("y_sb", cb, 1)], writes=[("actT", ffc)])
            if th == 0:
                load_wu(0, (pair_it) % 3)
                load_wu(1, (pair_it + 1) % 3)
                b_stats_sq(2)
            def dn_mm(b, f, slot, tb2):
                p.op("pe", lambda e: e.matmul(bank(b), lhsT=wd[slot][:, f, :], rhs=actT[:, f, tb2 * 512:(tb2 + 1) * 512],
                                              start=(f == 0), stop=(f == NFF - 1)),
                     reads=[("wd", slot), ("actT", f)], writes=[PS(b)])

            def dn_evac(b, dmc, tb2):
                tsl = slice(t0 + tb2 * 512, t0 + (tb2 + 1) * 512)
                p.op("dve", lambda e: e.tensor_tensor(out=x1T[:, dmc, tsl], in0=bank(b), in1=x1T[:, dmc, tsl], op=ALU.add),
                     reads=[PS(b), ("x1T", 2 * th + tb2, dmc)], writes=[("x1T", 2 * th + tb2, dmc)])

            NL = 4
            first = []
            for dmc in (0, 1):
                for tb2 in (0, 1):
                    b = psrot[0] % 8
                    psrot[0] += 1
                    first.append((b, dmc, tb2))
                    for f in range(NFF - NL):
                        dn_mm(b, f, dmc % 2, tb2)
            if th == 0:
                bs_ = psrot[0] % 8
                psrot[0] += 1
                b_stats_mm(2, bs_)
                b_stats_sq(3)
            for (b, dmc, tb2) in first:
                for f in range(NFF - NL, NFF):
                    dn_mm(b, f, dmc % 2, tb2)
                dn_evac(b, dmc, tb2)
            if th == 0:
                bs_ = psrot[0] % 8
                psrot[0] += 1
                b_stats_mm(3, bs_)
                h2_compute(1)
            load_wd(2, 0)
            load_wd(3, 1)
            for dmc in range(2, KC):
                slot = dmc % 2
                for tb2 in (0, 1):
                    b = psrot[0] % 8
                    psrot[0] += 1
                    for f in range(NFF):
                        dn_mm(b, f, slot, tb2)
                    dn_evac(b, dmc, tb2)
                if dmc + 2 < KC:
                    load_wd(dmc + 2, slot)
                if fin_pending:
                    fin_pending.pop(0)()
            if th == 0:
                fin_pending.extend(final_stages(0) + final_stages(1))
            else:
                a2, a3 = final_stages(2), final_stages(3)
                for f in (a2[0], a2[1], a3[0], a2[2], a3[1], a3[2]):
                    f()
        return finish(nc, p, es, None, dbg_out)


def finish(nc, p, es, dump, dbg_out=None):
    if dump is not None and dbg_out:
        ap, name = dump
        if name in dbg_out:
            p.op("sp", lambda e: e.dma_start(out=dbg_out[name], in_=ap), reads=[(name,)], writes=[("dbg", name)], dma="dbg_" + name)
    fin = p.op("sp", lambda e: e.nop())
    for s, i in p.last.items():
        if s.startswith("dma:") and i != fin.id:
            fin.deps.add(i)
    p.finalize()
    sems = {}
    for s in p.streams:
        sems[s] = es.enter_context(nc.semaphore("s_" + s.replace(":", "_")))
    block = es.enter_context(nc.Block())
    p.emit(nc, block, sems)
    return nc


def make_inputs(x, ln1_w, w_in, diff_lq1, diff_lk1, diff_lq2, diff_lk2, diff_subln_w, gla_wg2, gla_bg,
                gla_norm_w, w_out, ln2_w, w_up, conv_w, conv_b, w_down, lnf_w):
    f = lambda a: np.ascontiguousarray(np.asarray(a, dtype=np.float32))
    cst = np.zeros((128, NCST), np.float32)
    cst[:, C_LN1:C_LN1 + 8] = f(ln1_w)[0].reshape(8, 128).T
    cst[:, C_LN2:C_LN2 + 8] = f(ln2_w)[0].reshape(8, 128).T
    cst[:, C_LNF:C_LNF + 8] = f(lnf_w).reshape(8, 128).T
    cst[:, C_BG:C_BG + 2] = f(gla_bg)[0].reshape(2, 128).T
    cst[:, C_SUB] = f(diff_subln_w)[0]
    cst[:, C_GNW] = f(gla_norm_w)[0]
    cst[:, C_CB:C_CB + 44] = f(conv_b)[0].reshape(44, 128).T
    cst[:, C_CW:C_CW + 132] = f(conv_w)[0].T.reshape(44, 128, 3).transpose(1, 0, 2).reshape(128, 132)
    cst[:, C_LQ1:C_LQ1 + 64] = f(diff_lq1)[0][None, :]
    cst[:, C_LK1:C_LK1 + 64] = f(diff_lk1)[0][None, :]
    cst[:, C_LQ2:C_LQ2 + 64] = f(diff_lq2)[0][None, :]
    cst[:, C_LK2:C_LK2 + 64] = f(diff_lk2)[0][None, :]
    shared = {"w_in": f(w_in)[0], "w_out": f(w_out)[0], "w_up": f(w_up)[0], "w_down": f(w_down)[0],
              "wg2": f(gla_wg2)[0], "cst": cst}
    xf = f(x)
    maps = []
    for b in range(xf.shape[0]):
        xt = np.ascontiguousarray(xf[b].T.reshape(KC, 128, S).transpose(1, 0, 2))
        m = dict(shared)
        m["xT"] = xt
        maps.append(m)
    return maps


_NC_CACHE = {}


def kernel(**inputs):
    maps = make_inputs(**inputs)
    if "nc" not in _NC_CACHE:
        _NC_CACHE["nc"] = build_nc()
    nc = _NC_CACHE["nc"]
    n = len(maps)
    res = run_bass_kernel_spmd(nc, maps, core_ids=list(range(n)))
    out = np.empty((n, S, D), np.float32)
    for b in range(n):
        o = res.results[b]["outT"]
        out[b] = o.transpose(1, 0, 2).reshape(D, S).T
    return out
```

```python
import numpy as np
import concourse.bass as bass
import concourse.mybir as mybir
from concourse.bass_utils import run_bass_kernel_spmd

F32 = mybir.dt.float32
BF16 = mybir.dt.bfloat16
AF = mybir.ActivationFunctionType
ALU = mybir.AluOpType

S = 2048
D = 1024
KC = 8
DFF = 2816
NFF = 22
EPS = 1e-6
SAME_ENGINE_SYNC = True

C_LN1, C_LN2, C_LNF, C_BG, C_SUB, C_GNW, C_CB, C_CW = 0, 8, 16, 24, 26, 27, 28, 72
C_LQ1, C_LK1, C_LQ2, C_LK2 = 204, 268, 332, 396
NCST = 460


class Op:
    __slots__ = ("id", "eng", "fn", "deps", "dma", "stream", "signal", "tick", "waits")

    def __init__(self, id, eng, fn, dma):
        self.id, self.eng, self.fn, self.dma = id, eng, fn, dma
        self.stream = ("dma:" + dma) if dma else eng
        self.deps = set()
        self.signal = False
        self.tick = 0
        self.waits = {}


class Prog:
    ENGS = ("pe", "act", "dve", "pool", "sp")

    def __init__(self):
        self.ops = []
        self.state = {}
        self.desc = {}
        self.last = {}
        self.barrier_op = None

    def _related(self, t):
        out = []
        for i in range(1, len(t) + 1):
            pre = t[:i]
            if pre in self.state:
                out.append(pre)
        for d in self.desc.get(t, ()):
            if d in self.state:
                out.append(d)
        return out

    def _ensure(self, t):
        if t not in self.state:
            self.state[t] = [None, {}]
            for i in range(1, len(t)):
                self.desc.setdefault(t[:i], set()).add(t)

    def op(self, eng, fn, reads=(), writes=(), dma=None, nobar=False):
        o = Op(len(self.ops), eng, fn, dma)
        deps = o.deps
        for t in reads:
            for r in self._related(t):
                w = self.state[r][0]
                if w is not None:
                    deps.add(w)
        for t in writes:
            for r in self._related(t):
                st = self.state[r]
                if st[0] is not None:
                    deps.add(st[0])
                deps.update(st[1].values())
        if self.barrier_op is not None and not nobar:
            deps.add(self.barrier_op)
        for t in reads:
            self._ensure(t)
            self.state[t][1][o.stream] = o.id
        for t in writes:
            for d in list(self.desc.get(t, ())):
                self.state.pop(d, None)
            self.desc.pop(t, None)
            self._ensure(t)
            self.state[t] = [o.id, {}]
        deps.discard(o.id)
        self.ops.append(o)
        self.last[o.stream] = o.id
        return o

    def barrier(self, exempt=()):
        o = self.op("pool", lambda e: e.nop())
        for s, i in self.last.items():
            if s not in exempt and i != o.id:
                o.deps.add(i)
        self.barrier_op = o.id
        return o

    def finalize(self):
        ops = self.ops
        for o in ops:
            for d in list(o.deps):
                do = ops[d]
                if do.stream == o.stream and not do.dma:
                    if o.eng == "pe" or o.eng == "sp" or not SAME_ENGINE_SYNC:
                        o.deps.discard(d)
                        continue
                do.signal = True
        cnt = {}
        for o in ops:
            if o.dma:
                o.signal = True
            if o.signal:
                inc = 16 if o.dma else 1
                cnt[o.stream] = cnt.get(o.stream, 0) + inc
                o.tick = cnt[o.stream]
        for s, v in cnt.items():
            assert v < 30000, (s, v)
        self.streams = sorted(cnt.keys())
        clock = {e: {} for e in self.ENGS}
        snap = {}
        for o in ops:
            need = {}
            for d in o.deps:
                do = ops[d]
                if need.get(do.stream, 0) < do.tick:
                    need[do.stream] = do.tick
            ck = clock[o.eng]
            for d in o.deps:
                do = ops[d]
                if ck.get(do.stream, 0) >= do.tick:
                    for k, v in snap[d].items():
                        if ck.get(k, 0) < v:
                            ck[k] = v
            waits = {}
            for k, v in need.items():
                if ck.get(k, 0) < v:
                    waits[k] = v
            for k, v in waits.items():
                ck[k] = v
            for d in o.deps:
                for k, v in snap[d].items():
                    if ck.get(k, 0) < v:
                        ck[k] = v
            o.waits = waits
            if o.signal:
                sn = dict(ck)
                sn[o.stream] = o.tick
                snap[o.id] = sn

    def emit(self, nc, block, sems):
        by_eng = {e: [o for o in self.ops if o.eng == e] for e in self.ENGS}

        def run(eng_name, e):
            for o in by_eng[eng_name]:
                for k, v in o.waits.items():
                    e.wait_ge(sems[k], v)
                inst = o.fn(e)
                if o.signal:
                    inst.then_inc(sems[o.stream], 16 if o.dma else 1)

        @block.tensor
        def _(e):
            run("pe", e)

        @block.scalar
        def _(e):
            run("act", e)

        @block.vector
        def _(e):
            run("dve", e)

        @block.gpsimd
        def _(e):
            run("pool", e)

        @block.sync
        def _(e):
            run("sp", e)


class Arena:
    def __init__(self, big, nbytes):
        self.big = big
        self.free = [(0, nbytes)]
        self.used = {}

    def alloc(self, name, nbytes, top=False):
        nbytes = (nbytes + 63) // 64 * 64
        if top:
            for i in range(len(self.free) - 1, -1, -1):
                o, n = self.free[i]
                if n >= nbytes:
                    self.free[i] = (o, n - nbytes)
                    self.used[name] = (o + n - nbytes, nbytes)
                    return o + n - nbytes
        for i, (o, n) in enumerate(self.free):
            if n >= nbytes:
                self.free[i] = (o + nbytes, n - nbytes)
                self.used[name] = (o, nbytes)
                return o
        raise RuntimeError(f"arena out of space for {name} {nbytes}: {self.free}")

    def release(self, *names):
        for name in names:
            o, n = self.used.pop(name)
            self.free.append((o, n))
        self.free.sort()
        m = []
        for o, n in self.free:
            if n == 0:
                continue
            if m and m[-1][0] + m[-1][1] == o:
                m[-1] = (m[-1][0], m[-1][1] + n)
            else:
                m.append((o, n))
        self.free = m

    def view(self, name, dtype, shape, parts=128, top=False):
        esz = 4 if dtype == F32 else 2
        n = int(np.prod(shape))
        o = self.alloc(name, n * esz, top=top)
        ap = self.big[0:parts, o // 4:(o + n * esz) // 4]
        if dtype != F32:
            ap = ap.bitcast(dtype)
        if len(shape) == 2:
            ap = ap.rearrange("p (a b) -> p a b", a=shape[0])
        elif len(shape) == 3:
            ap = ap.rearrange("p (a b c) -> p a b c", a=shape[0], b=shape[1])
        return ap


def build_nc(stop_after=None, debug=()):
    nc = bass.Bass("TRN2", target_bir_lowering=False)
    xT = nc.dram_tensor("xT", [128, KC, S], F32, kind="ExternalInput").ap()
    w_in = nc.dram_tensor("w_in", [D, 3088], F32, kind="ExternalInput").ap()
    w_out = nc.dram_tensor("w_out", [D, D], F32, kind="ExternalInput").ap()
    w_up = nc.dram_tensor("w_up", [D, 2 * DFF], F32, kind="ExternalInput").ap()
    w_down = nc.dram_tensor("w_down", [DFF, D], F32, kind="ExternalInput").ap()
    wg2 = nc.dram_tensor("wg2", [16, 256], F32, kind="ExternalInput").ap()
    cst_d = nc.dram_tensor("cst", [128, NCST], F32, kind="ExternalInput").ap()
    outT = nc.dram_tensor("outT", [128, KC, S], F32, kind="ExternalOutput").ap()
    dbg_out = {}
    for name, shape, dt in debug:
        dbg_out[name] = nc.dram_tensor("dbg_" + name, list(shape), dt, kind="ExternalOutput").ap()

    p = Prog()
    ARENA_BYTES = 207 * 1024
    import contextlib
    es = contextlib.ExitStack()
    with es:
        big = es.enter_context(nc.sbuf_tensor("big", [128, ARENA_BYTES // 4], F32))
        ps = es.enter_context(nc.psum_tensor("ps", [128, 8, 512], F32))
        A = Arena(big, ARENA_BYTES)

        def bank(b):
            return ps[:, b, :]

        PS = lambda b: ("ps", b)

        cst = A.view("cst", F32, [NCST])
        ones_bf = A.view("ones", BF16, [128])
        ident = A.view("ident", BF16, [128])
        maskT = A.view("maskT", BF16, [128])
        mask2 = A.view("mask2", BF16, [128])
        misc = A.view("misc", F32, [16])
        wg2_sb = A.view("wg2", F32, [256])
        wslot_bytes = 8192
        wsl = [A.alloc(f"wslot{i}", wslot_bytes) for i in range(3)]
        wbase = wsl[0]
        assert wsl[1] == wbase + wslot_bytes and wsl[2] == wbase + 2 * wslot_bytes

        def wview(byte_off, shape, dtype=BF16):
            n = int(np.prod(shape))
            esz = 2
            o = wbase + byte_off
            ap = big[:, o // 4:(o + n * esz) // 4].bitcast(dtype)
            if len(shape) == 2:
                ap = ap.rearrange("p (a b) -> p a b", a=shape[0])
            elif len(shape) == 3:
                ap = ap.rearrange("p (a b c) -> p a b c", a=shape[0], b=shape[1])
            return ap

        p.op("sp", lambda e: e.dma_start(out=cst, in_=cst_d), writes=[("cst",)], dma="cst")
        p.op("sp", lambda e: e.dma_start(out=wg2_sb[0:16, :], in_=wg2), writes=[("wg2",)], dma="wg2")
        p.op("pool", lambda e: e.memset(ones_bf, 1.0), writes=[("ones",)])
        p.op("pool", lambda e: e.affine_select(out=maskT, in_=ones_bf, pattern=[[1, 128]], compare_op=ALU.is_ge,
                                               fill=0.0, base=0, channel_multiplier=-1),
             reads=[("ones",)], writes=[("maskT",)])
        p.op("pool", lambda e: e.affine_select(out=ident, in_=ones_bf, pattern=[[1, 128]], compare_op=ALU.is_equal,
                                               fill=0.0, base=0, channel_multiplier=-1),
             reads=[("ones",)], writes=[("ident",)])
        p.op("pool", lambda e: e.tensor_copy(out=mask2, in_=maskT), reads=[("maskT",)], writes=[("mask2",)])
        p.op("pool", lambda e: e.memset(mask2[0:64, 64:128], 0.0), writes=[("mask2",)])
        neg_lam = misc[:, 0:1]
        w8 = misc[:, 1:2]
        lt = A.view("lamtmp", F32, [2, 64])
        p.op("dve", lambda e: e.tensor_tensor(out=lt[:, 0, :], in0=cst[:, C_LQ1:C_LQ1 + 64], in1=cst[:, C_LK1:C_LK1 + 64], op=ALU.mult),
             reads=[("cst",)], writes=[("lt", 0)])
        p.op("dve", lambda e: e.tensor_tensor(out=lt[:, 1, :], in0=cst[:, C_LQ2:C_LQ2 + 64], in1=cst[:, C_LK2:C_LK2 + 64], op=ALU.mult),
             reads=[("cst",)], writes=[("lt", 1)])
        p.op("dve", lambda e: e.reduce_sum(out=misc[:, 2:4], in_=lt, axis=mybir.AxisListType.X), reads=[("lt",)], writes=[("misc", "s")])
        p.op("act", lambda e: e.activation(out=misc[:, 4:6], in_=misc[:, 2:4], func=AF.Exp), reads=[("misc", "s")], writes=[("misc", "e")])
        p.op("dve", lambda e: e.tensor_tensor(out=misc[:, 6:7], in0=misc[:, 5:6], in1=misc[:, 4:5], op=ALU.subtract),
             reads=[("misc", "e")], writes=[("misc", "d")])
        p.op("dve", lambda e: e.tensor_scalar(out=neg_lam, in0=misc[:, 6:7], scalar1=-0.2, scalar2=None, op0=ALU.add),
             reads=[("misc", "d")], writes=[("misc", "nl")])
        p.op("dve", lambda e: e.tensor_scalar(out=w8, in0=cst[:, C_SUB:C_SUB + 1], scalar1=0.8, scalar2=None, op0=ALU.mult),
             reads=[("cst",)], writes=[("misc", "w8")])

        def wdma(dst, src, tok, key):
            toks = [("w", 0), ("w", 1)] if tok is None else [tok]
            p.op("pool", lambda e: e.dma_start(out=dst, in_=src), writes=toks, dma=key)

        def w_in_cols(c0, c1):
            return w_in[:, c0:c1].rearrange("(kc p) n -> p kc n", p=128)

        def rstd_from_ss(ss_ps_ap, out_ap, n, reads, writes):
            p.op("act", lambda e: e.activation(out=out_ap, in_=ss_ps_ap, func=AF.Ln, scale=1.0 / n, bias=eps_ap),
                 reads=reads + [("misc", "eps")], writes=writes)
            p.op("act", lambda e: e.activation(out=out_ap, in_=out_ap, func=AF.Exp, scale=-0.5),
                 reads=writes, writes=writes)

        eps_ap = misc[:, 8:9]
        one_ap = misc[:, 9:10]
        p.op("pool", lambda e: e.memset(eps_ap, EPS), writes=[("misc", "eps")])
        p.op("pool", lambda e: e.memset(one_ap, 1.0), writes=[("misc", "one")])

        def dbg(name, ap, reads):
            if name in dbg_out:
                p.op("sp", lambda e: e.dma_start(out=dbg_out[name], in_=ap), reads=reads, writes=[("dbg", name)], dma="dbg_" + name)

        wA = [wview(i * wslot_bytes, [KC, 512]) for i in range(3)]
        w_r = A.view("w_r", BF16, [KC, 16])
        wdma(w_r, w_in_cols(2560, 2576), ("w_r",), "wr")
        wdma(wA[1], w_in_cols(2048, 2560), ("w", 1), "w1")
        gv_tok = A.view("gv_tok", BF16, [16, 512])
        grT = A.view("grT", F32, [S])
        v_tok = A.view("v_tok", BF16, [16, 512], top=True)
        psrot = [0]

        def tm_group(tkb, wtile, tok_w, dst, dst_name, evac, bank_base):
            b = bank_base + psrot[0] % 4
            psrot[0] += 1
            tsl = slice(tkb * 128, (tkb + 1) * 128)
            for kc in range(KC):
                p.op("pe", lambda e, kc=kc: e.matmul(bank(b), lhsT=hT[:, kc, tsl], rhs=wtile[:, kc, :], start=(kc == 0), stop=(kc == KC - 1)),
                     reads=[tok_w, ("hT", tkb // 4)], writes=[PS(b)])
            if evac == "act":
                p.op("act", lambda e: e.activation(out=dst[:, tkb, :], in_=bank(b), func=AF.Copy), reads=[PS(b)], writes=[(dst_name, tkb)])
            else:
                p.op("dve", lambda e: e.tensor_copy(out=dst[:, tkb, :], in_=bank(b)), reads=[PS(b)], writes=[(dst_name, tkb)])
        hT = A.view("hT", BF16, [KC, S])
        xs = [A.view(f"xs{i}", F32, [KC, 512]) for i in range(2)]
        sq1 = A.view("sq1", BF16, [KC, 512])
        rs1 = A.view("rs1", F32, [512])
        def a1_proj(tb):
            tsl = slice(tb * 512, (tb + 1) * 512)
            for kc in range(KC):
                p.op("pe", lambda e, kc=kc: e.matmul(bank(2)[0:16, :], lhsT=w_r[:, kc, 0:16], rhs=hT[:, kc, tsl], start=(kc == 0), stop=(kc == KC - 1)),
                     reads=[("w_r",), ("hT", tb)], writes=[PS(2)])
            p.op("act", lambda e: e.activation(out=grT[0:16, tb * 512:(tb + 1) * 512], in_=bank(2)[0:16, :], func=AF.Copy),
                 reads=[PS(2)], writes=[("grT", tb)])
            for i in range(4):
                tm_group(4 * tb + i, wA[1], ("w", 1), gv_tok, "gv_tok", "dve", 4)

        for tb in range(4):
            sl = tb % 2
            tsl = slice(tb * 512, (tb + 1) * 512)
            p.op("sp", lambda e, sl=sl, tsl=tsl: e.dma_start(out=xs[sl], in_=xT[:, :, tsl]), writes=[("xs", sl)], dma=f"xs{sl}")
            p.op("act", lambda e, sl=sl: e.activation(out=sq1, in_=xs[sl], func=AF.Square), reads=[("xs", sl)], writes=[("sq1",)])
            b = tb % 2
            for kc in range(KC):
                p.op("pe", lambda e, kc=kc, b=b: e.matmul(bank(b), lhsT=ones_bf, rhs=sq1[:, kc, :], start=(kc == 0), stop=(kc == KC - 1)),
                     reads=[("ones",), ("sq1",)], writes=[PS(b)])
            rstd_from_ss(bank(b), rs1, D, [PS(b)], [("rs1",)])
            for kc in range(KC):
                eng = "dve" if kc % 2 == 0 else "dve"
                p.op(eng, lambda e, kc=kc, sl=sl, tsl=tsl: e.scalar_tensor_tensor(
                    out=hT[:, kc, tsl], in0=xs[sl][:, kc, :], scalar=cst[:, C_LN1 + kc:C_LN1 + kc + 1], in1=rs1,
                    op0=ALU.mult, op1=ALU.mult),
                    reads=[("xs", sl), ("rs1",), ("cst",)], writes=[("hT", tb, kc)])
            if tb >= 1:
                a1_proj(tb - 1)
            if tb == 1:
                p.op("pool", lambda e: e.dma_start(out=wA[2], in_=w_in_cols(1024, 1536)), reads=[("xs", 1)], writes=[("w", 2)], dma="w2")
                p.op("pool", lambda e: e.dma_start(out=wA[0], in_=w_in_cols(1536, 2048)), reads=[("xs", 1)], writes=[("w", 0)], dma="w0")
        a1_proj(3)
        dbg("hT", hT, [("hT",)])
        p.barrier()
        A.release("xs0", "xs1", "sq1", "rs1", "lamtmp")
        if stop_after == "A1":
            return finish(nc, p, es, None)


        def proj_fm(wtile, c0, ncols, tok_w, consume):
            for tb in range(4):
                b = psrot[0] % 4
                psrot[0] += 1
                tsl = slice(tb * 512, (tb + 1) * 512)
                for kc in range(KC):
                    p.op("pe", lambda e, kc=kc, b=b, tsl=tsl: e.matmul(bank(b)[0:ncols, :], lhsT=wtile[:, kc, c0:c0 + ncols], rhs=hT[:, kc, tsl],
                                                                      start=(kc == 0), stop=(kc == KC - 1)),
                         reads=[tok_w, ("hT", tb)], writes=[PS(b)], nobar=True)
                consume(tb, bank(b)[0:ncols, :], b)

        def proj_tm(wtile, tok_w, consume):
            for tkb in range(16):
                b = psrot[0] % 4
                psrot[0] += 1
                tsl = slice(tkb * 128, (tkb + 1) * 128)
                for kc in range(KC):
                    p.op("pe", lambda e, kc=kc, b=b, tsl=tsl: e.matmul(bank(b), lhsT=hT[:, kc, tsl], rhs=wtile[:, kc, :],
                                                                      start=(kc == 0), stop=(kc == KC - 1)),
                         reads=[tok_w, ("hT", tkb // 4)], writes=[PS(b)])
                consume(tkb, bank(b), b)


        q_decT = A.view("q_decT", BF16, [2, S])
        k_decT = A.view("k_decT", BF16, [2, S])
        kteT = A.view("kteT", BF16, [2, S])
        kte_tok = A.view("kte_tok", BF16, [16, 256])
        sgo = A.view("sgo", BF16, [4, S])
        resetm = A.view("resetm", F32, [S])
        T0 = A.view("T0", F32, [S])
        T1 = A.view("T1", F32, [S])
        T2 = A.view("T2", F32, [S])
        ebT = A.view("ebT", F32, [S])
        enbT = A.view("enbT", F32, [S])
        ek2T = A.view("ek2T", F32, [S])
        ebl = A.view("ebl", F32, [2, 32])

        p.op("pool", lambda e: e.memset(resetm, 1.0), writes=[("resetm",)])
        p.op("pool", lambda e: e.memset(resetm[:, 0:S:64], 0.0), writes=[("resetm",)])


        def go_groups(c):
            out = []
            for c4 in (2 * c, 2 * c + 1):
                for tb in range(4):
                    def g(c4=c4, tb=tb):
                        b = psrot[0] % 4
                        psrot[0] += 1
                        tsl = slice(tb * 512, (tb + 1) * 512)
                        for kc in range(KC):
                            p.op("pe", lambda e, kc=kc: e.matmul(bank(b), lhsT=wA[2][:, kc, c4 * 128:(c4 + 1) * 128], rhs=hT[:, kc, tsl],
                                                                 start=(kc == 0), stop=(kc == KC - 1)),
                                 reads=[("w", 2), ("hT", tb)], writes=[PS(b)])
                        p.op("act", lambda e: e.activation(out=sgo[:, c4, tsl], in_=bank(b), func=AF.Copy), reads=[PS(b)], writes=[("sgo", c4, tb)])
                    out.append(g)
            return out

        fill = []

        def FILL(n=2):
            for _ in range(n):
                if fill:
                    fill.pop(0)()

        for c in range(2):
            if c == 0:
                fill.extend([(lambda tkb=tkb: tm_group(tkb, wA[2], ("w", 2), v_tok, "v_tok", "act", 0)) for tkb in range(16)])
            else:
                fill.extend(go_groups(0) + go_groups(1))
            for tb in range(4):
                p.op("pe", lambda e, tb=tb, c=c: e.matmul(bank(4 + tb), lhsT=wg2_sb[0:16, c * 128:(c + 1) * 128],
                                                         rhs=grT[0:16, tb * 512:(tb + 1) * 512], start=True, stop=True),
                     reads=[("wg2",), ("grT", tb)], writes=[PS(4 + tb)], nobar=True)
            zps = ps[:, 4:8, :]
            T0v = T0.rearrange("p (a b) -> p a b", a=4)
            rd4 = [PS(4), PS(5), PS(6), PS(7)]
            p.op("act", lambda e, c=c: e.activation(out=T0v, in_=zps, func=AF.Identity, bias=cst[:, C_BG + c:C_BG + c + 1], scale=1.0),
                 reads=rd4 + [("cst",)], writes=[("T0",)])
            FILL()
            p.op("act", lambda e: e.activation(out=T1, in_=T0, func=AF.Abs), reads=[("T0",)], writes=[("T1",)])
            FILL()
            p.op("act", lambda e: e.activation(out=T1, in_=T1, func=AF.Exp, scale=-1.0), reads=[("T1",)], writes=[("T1",)])
            FILL()
            p.op("act", lambda e: e.activation(out=T1, in_=T1, func=AF.Ln, bias=one_ap, scale=1.0), reads=[("T1",), ("misc", "one")], writes=[("T1",)])
            FILL()
            p.op("dve", lambda e: e.tensor_scalar(out=T2, in0=T0, scalar1=0.0, scalar2=1.0 / 16.0, op0=ALU.min, op1=ALU.mult),
                 reads=[("T0",)], writes=[("T2",)])
            p.op("dve", lambda e: e.scalar_tensor_tensor(out=T1, in0=T1, scalar=-1.0 / 16.0, in1=T2, op0=ALU.mult, op1=ALU.add),
                 reads=[("T1",), ("T2",)], writes=[("T1",)])
            FILL()
            p.op("dve", lambda e: e.tensor_tensor_scan(out=T0, data0=resetm, data1=T1, initial=0.0, op0=ALU.mult, op1=ALU.add),
                 reads=[("T1",), ("resetm",)], writes=[("T0",)])
            FILL()
            p.op("act", lambda e: e.activation(out=ebT, in_=T0, func=AF.Exp), reads=[("T0",)], writes=[("ebT",)])
            p.op("act", lambda e: e.activation(out=enbT, in_=T0, func=AF.Exp, scale=-1.0), reads=[("T0",)], writes=[("enbT",)])
            FILL()
            bl = T0[:, 63:S:64].unsqueeze(2).broadcast_to([128, 32, 64])
            p.op("dve", lambda e, bl=bl: e.tensor_tensor(out=T2.rearrange("p (a b) -> p a b", a=32), in0=bl,
                                                        in1=T0.rearrange("p (a b) -> p a b", a=32), op=ALU.subtract),
                 reads=[("T0",)], writes=[("T2",)])
            p.op("act", lambda e: e.activation(out=ek2T, in_=T2, func=AF.Exp), reads=[("T2",)], writes=[("ek2T",)])
            p.op("dve", lambda e, c=c: e.tensor_copy(out=ebl[:, c, :], in_=ebT[:, 63:S:64]), reads=[("ebT",)], writes=[("ebl", c)])
            while fill:
                FILL()
            if c == 0:
                wdma(wA[2], w_in_cols(2576, 3088), ("w", 2), "w2")
            if c == 0:
                dbg("bT", T0, [("T0",)])
                dbg("ebT", ebT, [("ebT",)])

            def cons_q(tb, pap, b, c=c):
                tsl = slice(tb * 512, (tb + 1) * 512)
                p.op("dve", lambda e: e.scalar_tensor_tensor(out=q_decT[:, c, tsl], in0=pap, scalar=0.125, in1=ebT[:, tsl],
                                                             op0=ALU.mult, op1=ALU.mult),
                     reads=[PS(b), ("ebT",)], writes=[("q_decT", c, tb)])
            proj_fm(wA[0], c * 128, 128, ("w", 0), cons_q)

            def cons_k(tb, pap, b, c=c):
                tsl = slice(tb * 512, (tb + 1) * 512)
                p.op("dve", lambda e: e.tensor_tensor(out=k_decT[:, c, tsl], in0=pap, in1=enbT[:, tsl], op=ALU.mult),
                     reads=[PS(b), ("enbT",)], writes=[("k_decT", c, tb)])
                p.op("dve", lambda e: e.tensor_tensor(out=kteT[:, c, tsl], in0=pap, in1=ek2T[:, tsl], op=ALU.mult),
                     reads=[PS(b), ("ek2T",)], writes=[("kteT", c, tb)])
            proj_fm(wA[0], 256 + c * 128, 128, ("w", 0), cons_k)

        for c in range(2):
            for g in range(4):
                b = psrot[0] % 4
                psrot[0] += 1
                pst = bank(b)[:, 0:256].bitcast(BF16).rearrange("p (a b) -> p a b", a=4)
                for i in range(4):
                    tkb = g * 4 + i
                    p.op("pe", lambda e, i=i, tkb=tkb, c=c, pst=pst: e.transpose(out=pst[:, i, :], in_=kteT[:, c, tkb * 128:(tkb + 1) * 128], identity=ident),
                         reads=[("kteT", c, tkb // 4), ("ident",)], writes=[PS(b)])
                p.op("act", lambda e, g=g, c=c, pst=pst: e.activation(out=kte_tok[:, g * 4:(g + 1) * 4, c * 128:(c + 1) * 128], in_=pst, func=AF.Copy),
                     reads=[PS(b)], writes=[("kte_tok", c, g)])


        wdma(wA[0], w_in_cols(0, 512), ("w", 0), "w0")
        wdma(wA[1], w_in_cols(512, 1024), ("w", 1), "w1")

        dbg("q_decT", q_decT, [("q_decT",)])
        dbg("k_decT", k_decT, [("k_decT",)])
        dbg("kte_tok", kte_tok, [("kte_tok",)])
        dbg("gv_tok", gv_tok, [("gv_tok",)])
        dbg("sgo", sgo, [("sgo",)])
        dbg("ebl", ebl, [("ebl",)])
        p.barrier(exempt=("dma:w0", "dma:w1", "dma:w2"))
        A.release("kteT", "grT", "resetm", "T0", "T1", "T2", "ebT", "enbT", "ek2T")
        if stop_after == "A2":
            return finish(nc, p, es, None)

        cc = A.view("cc", BF16, [KC, S], top=True)
        Sbf = A.view("Sbf", BF16, [33, 2, 128])
        aT_halves = [A.view("aT_lo", BF16, [8, 4, 128]), A.view("aT_hi", BF16, [8, 4, 128])]

        def aTv(tkb):
            return aT_halves[tkb // 8][:, tkb % 8]
        Sst2 = A.view("Sst", F32, [2, 2, 128])
        e_sq = A.view("e_sq", BF16, [512], top=True)
        e_rs = A.view("e_rs", F32, [512], top=True)
        e_sq1 = A.view("e_t", BF16, [512], top=True)
        e_rs1 = A.view("e_rs1", F32, [512], top=True)

        for tkb in range(16):
            pair = (tkb % 2) * 2
            tsl = slice(tkb * 128, (tkb + 1) * 128)
            for h in range(4):
                c, hh = h // 2, h % 2
                rows = slice(hh * 64, (hh + 1) * 64)
                b = pair + hh
                p.op("pe", lambda e, b=b, h=h, c=c, rows=rows, tsl=tsl: e.matmul(bank(b)[:, c * 128:(c + 1) * 128], lhsT=k_decT[rows, c, tsl],
                                                                                  rhs=q_decT[rows, c, tsl], start=True, stop=True),
                     reads=[("k_decT", c, tkb // 4), ("q_decT", c, tkb // 4)], writes=[PS(b)], nobar=True)
            for hh in range(2):
                b = pair + hh
                p.op("dve", lambda e, b=b, tkb=tkb, hh=hh: e.tensor_tensor(out=aTv(tkb)[:, hh:4:2, :], in0=bank(b)[:, 0:256].rearrange("p (a b) -> p a b", a=2),
                                                                       in1=mask2.unsqueeze(1).broadcast_to([128, 2, 128]), op=ALU.mult),
                     reads=[PS(b), ("mask2",)], writes=[("aT", tkb, hh)])

        for c4 in range(4):
            p.op("act", lambda e, c4=c4: e.activation(out=sgo[:, c4, :], in_=sgo[:, c4, :], func=AF.Silu), reads=[("sgo", c4)], writes=[("sgo", c4)])

        p.op("pool", lambda e: e.memset(Sbf[:, 0, :, :], 0.0), writes=[("Sbf", 0)])
        p.op("pool", lambda e: e.memset(Sst2[:, 0, :, :], 0.0), writes=[("Sst", 0)])
        for tkb in range(16):
            bp = 4 + 2 * (tkb % 2)
            for c in range(2):
                for hh in range(2):
                    h = 2 * c + hh
                    orow = slice(hh * 64, (hh + 1) * 64)
                    for s in range(2):
                        trow = slice(s * 64, (s + 1) * 64)
                        p.op("pe", lambda e, c=c, hh=hh, h=h, trow=trow, orow=orow, s=s, bp=bp, tkb=tkb: e.matmul(
                            bank(bp + s)[orow, c * 128:(c + 1) * 128], lhsT=kte_tok[trow, tkb, c * 128 + hh * 64:c * 128 + hh * 64 + 64],
                            rhs=gv_tok[trow, tkb, h * 128:(h + 1) * 128], start=True, stop=True),
                            reads=[("kte_tok", c, tkb // 4), ("gv_tok", tkb)], writes=[PS(bp + s)])
            for s in range(2):
                n = 2 * tkb + s
                so, sn = n % 2, (n + 1) % 2
                for c in range(2):
                    p.op("dve", lambda e, c=c, n=n, so=so, sn=sn, s=s, bp=bp: e.scalar_tensor_tensor(out=Sst2[:, sn, c, :], in0=Sst2[:, so, c, :], scalar=ebl[:, c, n:n + 1],
                                                                                            in1=bank(bp + s)[:, c * 128:(c + 1) * 128], op0=ALU.mult, op1=ALU.add),
                         reads=[PS(bp + s), ("Sst", so, c), ("ebl", c)], writes=[("Sst", sn, c)])
                p.op("act", lambda e, n=n, sn=sn: e.activation(out=Sbf[:, n + 1, :, :], in_=Sst2[:, sn, :, :], func=AF.Copy), reads=[("Sst", sn)], writes=[("Sbf", n + 1)])

        it = 0
        epi_prev = []

        def a3_epi(tb, h):
            b = h
            tsl = slice(tb * 512, (tb + 1) * 512)
            k = it_box[0] % 2
            it_box[0] += 1
            sq_, rs_ = (e_sq, e_rs) if k == 0 else (e_sq1, e_rs1)
            tq, tr = ("e_sqA", k), ("e_rsA", k)
            bs = 6 + k

            def part1():
                p.op("act", lambda e: e.activation(out=sq_, in_=bank(b), func=AF.Square), reads=[PS(b)], writes=[tq])

            def part2():
                p.op("pe", lambda e: e.matmul(bank(bs), lhsT=ones_bf, rhs=sq_, start=True, stop=True), reads=[tq, ("ones",)], writes=[PS(bs)])
                rstd_from_ss(bank(bs), rs_, 128, [PS(bs)], [tr])
                p.op("dve", lambda e: e.scalar_tensor_tensor(out=rs_, in0=bank(b), scalar=cst[:, C_GNW:C_GNW + 1], in1=rs_, op0=ALU.mult, op1=ALU.mult),
                     reads=[PS(b), tr, ("cst",)], writes=[tr])
                p.op("dve", lambda e: e.tensor_tensor(out=cc[:, 4 + h, tsl], in0=rs_, in1=sgo[:, h, tsl], op=ALU.mult),
                     reads=[tr, ("sgo", h, tb)], writes=[("cc", 4 + h, tb)])
            return part1, part2

        it_box = [0]
        carry = None
        for tb in range(4):
            for c in range(2):
                hs = (2 * c, 2 * c + 1)
                for i in range(4):
                    tkb = tb * 4 + i
                    osl = slice(i * 128, (i + 1) * 128)
                    for h in hs:
                        p.op("pe", lambda e, h=h, osl=osl, tkb=tkb: e.matmul(bank(h)[:, osl], lhsT=gv_tok[:, tkb, h * 128:(h + 1) * 128], rhs=aTv(tkb)[:, h, :], start=True, stop=False),
                             reads=[("gv_tok", tkb), ("aT", tkb)], writes=[PS(h)])
                    for s in range(2):
                        n = 2 * tkb + s
                        o2 = slice(i * 128 + s * 64, i * 128 + s * 64 + 64)
                        t2 = slice(tkb * 128 + s * 64, tkb * 128 + s * 64 + 64)
                        for h in hs:
                            rows = slice((h % 2) * 64, (h % 2) * 64 + 64)
                            p.op("pe", lambda e, h=h, rows=rows, s=s, n=n, o2=o2, t2=t2, c=c: e.matmul(bank(h)[:, o2], lhsT=Sbf[rows, n, c, :], rhs=q_decT[rows, c, t2],
                                                                                              start=False, stop=(s == 1)),
                                 reads=[("Sbf", n), ("q_decT", c, tkb // 4)], writes=[PS(h)])
                for h in hs:
                    p1, p2 = a3_epi(tb, h)
                    p1()
                    if carry is not None:
                        carry()
                    carry = p2
        if carry is not None:
            carry()
        p.barrier(exempt=("dma:w0", "dma:w1", "dma:w2"))
        A.release("aT_lo", "aT_hi", "Sbf", "Sst", "q_decT", "k_decT", "kte_tok", "gv_tok", "sgo", "ebl")
        if stop_after == "A3":
            return finish(nc, p, es, (cc, "cc"), dbg_out)

        qT = A.view("qT", BF16, [4, S])
        kT = A.view("kT", BF16, [4, S])
        flip = [0]

        def cons_qk(dst, name, h):
            def cons(tb, pap, b):
                flip[0] ^= 1
                if flip[0]:
                    p.op("act", lambda e: e.activation(out=dst[:, h, tb * 512:(tb + 1) * 512], in_=pap, func=AF.Copy), reads=[PS(b)], writes=[(name, h, tb)])
                else:
                    p.op("dve", lambda e: e.tensor_copy(out=dst[:, h, tb * 512:(tb + 1) * 512], in_=pap), reads=[PS(b)], writes=[(name, h, tb)])
            return cons

        def cons_dv(tkb, pap, b):
            flip[0] ^= 1
            if flip[0]:
                p.op("act", lambda e: e.activation(out=v_tok[:, tkb, :], in_=pap, func=AF.Copy), reads=[PS(b)], writes=[("v_tok", tkb)])
            else:
                p.op("dve", lambda e: e.tensor_copy(out=v_tok[:, tkb, :], in_=pap), reads=[PS(b)], writes=[("v_tok", tkb)])
        for h in range(4):
            proj_fm(wA[0], h * 128, 128, ("w", 0), cons_qk(qT, "qT", h))
            proj_fm(wA[1], h * 128, 128, ("w", 1), cons_qk(kT, "kT", h))
        w_o = wview(0, [KC, D])
        wdma(w_o, w_out.rearrange("(kc p) n -> p kc n", p=128), None, "w0")

        def filler_ops(hn):
            out = []
            for wi, dst, name in ((0, qT, "qT"), (1, kT, "kT")):
                for tb in range(4):
                    tsl = slice(tb * 512, (tb + 1) * 512)
                    for kc in range(KC):
                        def mo(wi=wi, dst=dst, name=name, tb=tb, tsl=tsl, kc=kc):
                            p.op("pe", lambda e: e.matmul(bank(7), lhsT=wA[wi][:, kc, hn * 128:(hn + 1) * 128], rhs=hT[:, kc, tsl],
                                                          start=(kc == 0), stop=(kc == KC - 1)),
                                 reads=[("w", wi), ("hT", tb)], writes=[PS(7)])
                            if kc == KC - 1:
                                p.op("dve", lambda e: e.tensor_copy(out=dst[:, hn, tsl], in_=bank(7)), reads=[PS(7)], writes=[(name, hn, tb)])
                        out.append(mo)
            return out

        NP = 4
        p_sb = A.view("p_sb", BF16, [NP, 2, 512])
        a_r = A.view("a_r", F32, [2, 512])
        a_o = A.view("a_o", F32, [512])
        xsB = [A.view(f"xs{i}", F32, [KC, 512], top=True) for i in range(2)]
        o_s = A.view("o_s", F32, [2, 512])
        item = [0]
        pending = []

        def flush_one():
            if pending:
                pending.pop(0)()

        blocks = [(h, qb) for h in range(4) for qb in range(4)]
        steps = [(bi, kb) for bi, (h, qb) in enumerate(blocks) for kb in range(4 * (qb + 1))]

        def emit_S(step):
            bi, kb = step
            h, qb = blocks[bi]
            qs = qb * 512
            j = kb - 4 * qb
            q0 = 128 * j if j > 0 else 0
            sl_ = item[0] % NP
            sbk = 2 * (item[0] % 2)
            item[0] += 1
            for c in range(2):
                rows = slice(c * 64, (c + 1) * 64)
                p.op("pe", lambda e, c=c, rows=rows: e.matmul(bank(sbk + c)[:, q0:512], lhsT=kT[rows, h, kb * 128:(kb + 1) * 128],
                                                              rhs=qT[rows, h, qs + q0:qs + 512], start=True, stop=True),
                     reads=[("kT", h, kb // 4), ("qT", h, qb)], writes=[PS(sbk + c)])
            for c in range(2):
                p.op("act", lambda e, c=c: e.activation(out=p_sb[:, sl_, c, q0:512], in_=bank(sbk + c)[:, q0:512], func=AF.Exp, scale=0.125),
                     reads=[PS(sbk + c)], writes=[("p_sb", sl_, c)])
                if j >= 0:
                    d0 = 128 * j
                    p.op("dve", lambda e, c=c, d0=d0: e.tensor_tensor(out=p_sb[:, sl_, c, d0:d0 + 128], in0=p_sb[:, sl_, c, d0:d0 + 128], in1=maskT, op=ALU.mult),
                         reads=[("p_sb", sl_, c), ("maskT",)], writes=[("p_sb", sl_, c)])
            return (sl_, q0)

        def emit_AV(step, res):
            bi, kb = step
            h, qb = blocks[bi]
            nkb = 4 * (qb + 1)
            sl_, q0 = res
            st, sp_ = (kb == 0), (kb == nkb - 1)
            for c in range(2):
                p.op("pe", lambda e, c=c: e.matmul(bank(4 + c)[:, q0:512], lhsT=v_tok[:, kb, h * 128:(h + 1) * 128], rhs=p_sb[:, sl_, c, q0:512], start=st, stop=sp_),
                     reads=[("v_tok", kb), ("p_sb", sl_, c)], writes=[PS(4 + c)])
                p.op("pe", lambda e, c=c: e.matmul(bank(6 + c)[:, q0:512], lhsT=ones_bf, rhs=p_sb[:, sl_, c, q0:512], start=st, stop=sp_),
                     reads=[("ones",), ("p_sb", sl_, c)], writes=[PS(6 + c)])

        def make_epilogue(bi):
            h, qb = blocks[bi]
            tsl = slice(qb * 512, qb * 512 + 512)

            def s2():
                p.op("act", lambda e: e.activation(out=a_r, in_=a_r, func=AF.Exp, scale=-1.0), reads=[("a_r",)], writes=[("a_r",)])

            def s3():
                p.op("dve", lambda e: e.tensor_tensor(out=o_s, in0=o_s, in1=a_r, op=ALU.mult), reads=[("o_s",), ("a_r",)], writes=[("o_s",)])

            def s4():
                p.op("dve", lambda e: e.scalar_tensor_tensor(out=a_o, in0=o_s[:, 1, :], scalar=neg_lam, in1=o_s[:, 0, :], op0=ALU.mult, op1=ALU.add),
                     reads=[("o_s",), ("misc", "nl")], writes=[("a_o",)])
                p.op("dve", lambda e: e.tensor_tensor(out=e_sq, in0=a_o, in1=a_o, op=ALU.mult), reads=[("a_o",)], writes=[("e_sq",)])

            def s5():
                b_ = 2 * ((item[0] + 1) % 2)
                p.op("pe", lambda e: e.matmul(bank(b_), lhsT=ones_bf, rhs=e_sq, start=True, stop=True), reads=[("e_sq",), ("ones",)], writes=[PS(b_)])
                rstd_from_ss(bank(b_), e_rs, 128, [PS(b_)], [("e_rs",)])

            def s7():
                p.op("dve", lambda e: e.scalar_tensor_tensor(out=cc[:, h, tsl], in0=a_o, scalar=w8, in1=e_rs, op0=ALU.mult, op1=ALU.mult),
                     reads=[("a_o",), ("e_rs",), ("misc", "w8")], writes=[("cc", h, qb)])
            return [s2, s3, s4, s5, s7]

        prev = emit_S(steps[0])
        for si, step in enumerate(steps):
            bi, kb = step
            nkb = 4 * (blocks[bi][1] + 1)
            nxt = emit_S(steps[si + 1]) if si + 1 < len(steps) else None
            emit_AV(step, prev)
            prev = nxt
            for _ in range(2 if nkb == 4 else 1):
                flush_one()
            if kb == nkb - 1:
                while pending:
                    flush_one()
                for c in range(2):
                    p.op("dve", lambda e, c=c: e.tensor_copy(out=o_s[:, c, :], in_=bank(4 + c)), reads=[PS(4 + c)], writes=[("o_s", c)])
                    p.op("act", lambda e, c=c: e.activation(out=a_r[:, c, :], in_=bank(6 + c), func=AF.Ln), reads=[PS(6 + c)], writes=[("a_r", c)])
                pending = make_epilogue(bi)
        for tb_ in range(2):
            p.op("sp", lambda e, tb_=tb_: e.dma_start(out=xsB[tb_], in_=xT[:, :, tb_ * 512:(tb_ + 1) * 512]), writes=[("xs", tb_)], dma=f"xs{tb_}")
        p.barrier(exempt=("dma:w0", "dma:w1", "dma:w2", "dma:xs0", "dma:xs1"))
        A.release("qT", "kT", "v_tok", "p_sb", "e_t", "e_rs1", "hT")
        if stop_after == "A5":
            return finish(nc, p, es, (cc, "cc"), dbg_out)

        dbg("w_o", w_o, [("w",)])
        dbg("cc", cc, [("cc",)])
        x1T = A.view("x1T", F32, [KC, S])
        rstd2 = A.view("rstd2", F32, [S])
        sqB = A.view("sq1", BF16, [KC, 512])
        h2 = A.view("h2", BF16, [KC, 1024])
        wu = [wview(16384, [KC, 2, 128]), wview(20480, [KC, 2, 128]), wview(0, [KC, 2, 128])]
        wd = [wview(4096 + i * 5632, [NFF, 128]) for i in range(2)]

        def load_wu(ffc, slot):
            for g in range(2):
                c0 = g * DFF + ffc * 128
                wdma(wu[slot][:, :, g, :], w_up[:, c0:c0 + 128].rearrange("(kc p) n -> p kc n", p=128), ("wu", slot, g), f"wu{slot}{g}")

        def load_wd(dmc, slot):
            wdma(wd[slot], w_down[:, dmc * 128:(dmc + 1) * 128].rearrange("(f p) n -> p f n", p=128), ("wd", slot), f"wd{slot}")

        load_wu(0, 0)
        load_wu(1, 1)

        def h2_one(th, kc, tb2):
            t0 = th * 1024
            tsl = slice(t0 + tb2 * 512, t0 + (tb2 + 1) * 512)
            p.op("dve", lambda e: e.scalar_tensor_tensor(
                out=h2[:, kc, tb2 * 512:(tb2 + 1) * 512], in0=x1T[:, kc, tsl], scalar=cst[:, C_LN2 + kc:C_LN2 + kc + 1], in1=rstd2[:, tsl],
                op0=ALU.mult, op1=ALU.mult),
                reads=[("x1T", 2 * th + tb2, kc), ("rstd2", 2 * th + tb2), ("cst",)], writes=[("h2", kc, tb2)])

        def h2_compute(th):
            for kc in range(KC):
                for tb2 in range(2):
                    h2_one(th, kc, tb2)

        def b_proj(tb, extra=None):
            sl = tb % 2
            tsl = slice(tb * 512, (tb + 1) * 512)
            if tb >= 2:
                p.op("sp", lambda e: e.dma_start(out=xsB[sl], in_=xT[:, :, tsl]), writes=[("xs", sl)], dma=f"xs{sl}")
            for dmc in range(KC):
                b = psrot[0] % 4
                psrot[0] += 1
                for kc in range(KC):
                    p.op("pe", lambda e, b=b, kc=kc, dmc=dmc: e.matmul(bank(b), lhsT=w_o[:, kc, dmc * 128:(dmc + 1) * 128], rhs=cc[:, kc, tsl],
                                                                     start=(kc == 0), stop=(kc == KC - 1)),
                         reads=[("w", dmc // 4), ("cc", kc, tb)], writes=[PS(b)], nobar=True)
                p.op("dve", lambda e, b=b, dmc=dmc: e.tensor_tensor(out=x1T[:, dmc, tsl], in0=bank(b), in1=xsB[sl][:, dmc, :], op=ALU.add),
                     reads=[PS(b), ("xs", sl)], writes=[("x1T", tb, dmc)])
                if extra is not None:
                    extra(dmc)

        def b_stats_sq(tb):
            tsl = slice(tb * 512, (tb + 1) * 512)
            p.op("act", lambda e: e.activation(out=sqB, in_=x1T[:, :, tsl], func=AF.Square), reads=[("x1T", tb)], writes=[("sq1",)])

        def b_stats_mm(tb, bs=None):
            tsl = slice(tb * 512, (tb + 1) * 512)
            if bs is None:
                bs = 4 + tb % 2
            for kc in range(KC):
                p.op("pe", lambda e, kc=kc: e.matmul(bank(bs), lhsT=ones_bf, rhs=sqB[:, kc, :], start=(kc == 0), stop=(kc == KC - 1)),
                     reads=[("ones",), ("sq1",)], writes=[PS(bs)])
            rstd_from_ss(bank(bs), rstd2[:, tsl], D, [PS(bs)], [("rstd2", tb)])

        def b_stats(tb):
            b_stats_sq(tb)
            b_stats_mm(tb)

        b_proj(0, extra=lambda dmc: flush_one())
        while pending:
            flush_one()
        b_proj(1)
        b_stats(0)
        b_proj(2, extra=lambda dmc: h2_one(0, dmc, 0))
        b_stats(1)
        b_proj(3, extra=lambda dmc: h2_one(0, dmc, 1))
        p.barrier()
        A.release("cc", "xs0", "xs1", "e_sq", "e_rs", "a_r", "a_o", "o_s")
        if stop_after == "B":
            return finish(nc, p, es, (x1T, "x1T"), dbg_out)

        actT = A.view("actT", BF16, [NFF, 1024])
        uh = A.view("uh", F32, [2 * NFF, 2])
        NCB = 2
        u_sb = [[A.view(f"u_sb{i}{g}", F32, [1026]) for g in range(2)] for i in range(NCB)]
        y_sb = [[A.view(f"y_sb{i}{g}", F32, [1024]) for g in range(2)] for i in range(NCB)]
        for i_ in range(NCB):
            for g_ in range(2):
                p.op("pool", lambda e, us=u_sb[i_][g_]: e.memset(us[:, 0:2], 0.0), writes=[("u_sb", i_, g_, "h")])

        rsf = A.view("rsf", F32, [512])
        fin_pending = []

        def final_stages(tb):
            tsl = slice(tb * 512, (tb + 1) * 512)

            def f1():
                p.op("act", lambda e: e.activation(out=sqB, in_=x1T[:, :, tsl], func=AF.Square), reads=[("x1T", tb)], writes=[("sq1",)])

            def f2():
                bs = psrot[0] % 8
                psrot[0] += 1
                for kc in range(KC):
                    p.op("pe", lambda e, kc=kc: e.matmul(bank(bs), lhsT=ones_bf, rhs=sqB[:, kc, :], start=(kc == 0), stop=(kc == KC - 1)),
                         reads=[("ones",), ("sq1",)], writes=[PS(bs)])
                rstd_from_ss(bank(bs), rsf, D, [PS(bs)], [("rsf",)])

            def f3():
                for kc in range(KC):
                    p.op("dve", lambda e, kc=kc: e.scalar_tensor_tensor(out=x1T[:, kc, tsl], in0=x1T[:, kc, tsl], scalar=cst[:, C_LNF + kc:C_LNF + kc + 1],
                                                                        in1=rsf, op0=ALU.mult, op1=ALU.mult),
                         reads=[("x1T", tb, kc), ("rsf",), ("cst",)], writes=[("x1T", tb, kc)])
                    if kc % 4 == 3:
                        k0 = kc - 3
                        p.op("sp", lambda e, k0=k0: e.dma_start(out=outT[:, k0:k0 + 4, tsl], in_=x1T[:, k0:k0 + 4, tsl]),
                             reads=[("x1T", tb, k0), ("x1T", tb, k0 + 1), ("x1T", tb, k0 + 2), ("x1T", tb, k0 + 3)], writes=[("out", tb, k0)],
                             dma=f"out{tb % 2}{k0 // 4}")
            return [f1, f2, f3]

        pair_it = 0
        for th in range(2):
            t0 = th * 1024
            for ffc in range(NFF):
                slot = pair_it % 3
                cb = pair_it % NCB
                pair_it += 1
                if ffc + 2 < NFF:
                    load_wu(ffc + 2, (slot + 2) % 3)
                if ffc == NFF - 4:
                    load_wd(0, 0)
                    load_wd(1, 1)
                pb = 4 * (ffc % 2)
                for g in range(2):
                    for tb2 in range(2):
                        b = pb + 2 * g + tb2
                        for kc in range(KC):
                            p.op("pe", lambda e, b=b, kc=kc, g=g, tb2=tb2, slot=slot: e.matmul(bank(b), lhsT=wu[slot][:, kc, g, :], rhs=h2[:, kc, tb2 * 512:(tb2 + 1) * 512],
                                                                                             start=(kc == 0), stop=(kc == KC - 1)),
                                 reads=[("wu", slot, g), ("h2",)], writes=[PS(b)], nobar=True)
                for g in range(2):
                    ch = g * NFF + ffc
                    b0 = pb + 2 * g
                    pss = ps[:, b0:b0 + 2, :]
                    rd = [PS(b0), PS(b0 + 1)]
                    us = u_sb[cb][g]
                    ys = y_sb[cb][g]
                    cw = lambda i, ch=ch: cst[:, C_CW + ch * 3 + i:C_CW + ch * 3 + i + 1]
                    p.op("act", lambda e, us=us, pss=pss: e.activation(out=us[:, 2:1026].rearrange("p (a b) -> p a b", a=2), in_=pss, func=AF.Copy),
                         reads=rd, writes=[("u_sb", cb, g, "m")])
                    if th == 0:
                        p.op("act", lambda e, ch=ch, b0=b0: e.activation(out=uh[:, ch, :], in_=bank(b0 + 1)[:, 510:512], func=AF.Copy),
                             reads=[PS(b0 + 1)], writes=[("uh", ch)])
                    else:
                        p.op("dve", lambda e, us=us, ch=ch: e.tensor_copy(out=us[:, 0:2], in_=uh[:, ch, :]), reads=[("uh", ch)], writes=[("u_sb", cb, g, "h")])
                    p.op("act", lambda e, ys=ys, pss=pss, ch=ch, cw=cw: e.activation(out=ys.rearrange("p (a b) -> p a b", a=2), in_=pss, func=AF.Identity,
                                                                                    scale=cw(2), bias=cst[:, C_CB + ch:C_CB + ch + 1]),
                         reads=rd + [("cst",)], writes=[("y_sb", cb, g)])
                    p.op("dve", lambda e, ys=ys, us=us, cw=cw: e.scalar_tensor_tensor(out=ys, in0=us[:, 1:1025], scalar=cw(1), in1=ys, op0=ALU.mult, op1=ALU.add),
                         reads=[("u_sb", cb, g), ("y_sb", cb, g), ("cst",)], writes=[("y_sb", cb, g)])
                    p.op("dve", lambda e, ys=ys, us=us, cw=cw: e.scalar_tensor_tensor(out=ys, in0=us[:, 0:1024], scalar=cw(0), in1=ys, op0=ALU.mult, op1=ALU.add),
                         reads=[("u_sb", cb, g), ("y_sb", cb, g), ("cst",)], writes=[("y_sb", cb, g)])
                yg, yv = y_sb[cb][0], y_sb[cb][1]
                p.op("act", lambda e, yg=yg: e.activation(out=yg, in_=yg, func=AF.Silu), reads=[("y_sb", cb, 0)], writes=[("y_sb", cb, 0)])
                p.op("dve", lambda e, yg=yg, yv=yv, ffc=ffc: e.tensor_tensor(out=actT[:, ffc, :], in0=yg, in1=yv, op=ALU.mult),
                     reads=[("y_sb", cb, 0), ("y_sb", cb, 1)], writes=[("actT", ffc)])
            if th == 0:
                load_wu(0, (pair_it) % 3)
                load_wu(1, (pair_it + 1) % 3)
                b_stats_sq(2)
            def dn_mm(b, f, slot, tb2):
                p.op("pe", lambda e: e.matmul(bank(b), lhsT=wd[slot][:, f, :], rhs=actT[:, f, tb2 * 512:(tb2 + 1) * 512],
                                              start=(f == 0), stop=(f == NFF - 1)),
                     reads=[("wd", slot), ("actT", f)], writes=[PS(b)])

            def dn_evac(b, dmc, tb2):
                tsl = slice(t0 + tb2 * 512, t0 + (tb2 + 1) * 512)
                p.op("dve", lambda e: e.tensor_tensor(out=x1T[:, dmc, tsl], in0=bank(b), in1=x1T[:, dmc, tsl], op=ALU.add),
                     reads=[PS(b), ("x1T", 2 * th + tb2, dmc)], writes=[("x1T", 2 * th + tb2, dmc)])

            NL = 4
            first = []
            for dmc in (0, 1):
                for tb2 in (0, 1):
                    b = psrot[0] % 8
                    psrot[0] += 1
                    first.append((b, dmc, tb2))
                    for f in range(NFF - NL):
                        dn_mm(b, f, dmc % 2, tb2)
            if th == 0:
                bs_ = psrot[0] % 8
                psrot[0] += 1
                b_stats_mm(2, bs_)
                b_stats_sq(3)
            for (b, dmc, tb2) in first:
                for f in range(NFF - NL, NFF):
                    dn_mm(b, f, dmc % 2, tb2)
                dn_evac(b, dmc, tb2)
            if th == 0:
                bs_ = psrot[0] % 8
                psrot[0] += 1
                b_stats_mm(3, bs_)
                h2_compute(1)
            load_wd(2, 0)
            load_wd(3, 1)
            for dmc in range(2, KC):
                slot = dmc % 2
                for tb2 in (0, 1):
                    b = psrot[0] % 8
                    psrot[0] += 1
                    for f in range(NFF):
                        dn_mm(b, f, slot, tb2)
                    dn_evac(b, dmc, tb2)
                if dmc + 2 < KC:
                    load_wd(dmc + 2, slot)
                if fin_pending:
                    fin_pending.pop(0)()
            if th == 0:
                fin_pending.extend(final_stages(0) + final_stages(1))
            else:
                a2, a3 = final_stages(2), final_stages(3)
                for f in (a2[0], a2[1], a3[0], a2[2], a3[1], a3[2]):
                    f()
        return finish(nc, p, es, None, dbg_out)


def finish(nc, p, es, dump, dbg_out=None):
    if dump is not None and dbg_out:
        ap, name = dump
        if name in dbg_out:
            p.op("sp", lambda e: e.dma_start(out=dbg_out[name], in_=ap), reads=[(name,)], writes=[("dbg", name)], dma="dbg_" + name)
    fin = p.op("sp", lambda e: e.nop())
    for s, i in p.last.items():
        if s.startswith("dma:") and i != fin.id:
            fin.deps.add(i)
    p.finalize()
    sems = {}
    for s in p.streams:
        sems[s] = es.enter_context(nc.semaphore("s_" + s.replace(":", "_")))
    block = es.enter_context(nc.Block())
    p.emit(nc, block, sems)
    return nc


def make_inputs(x, ln1_w, w_in, diff_lq1, diff_lk1, diff_lq2, diff_lk2, diff_subln_w, gla_wg2, gla_bg,
                gla_norm_w, w_out, ln2_w, w_up, conv_w, conv_b, w_down, lnf_w):
    f = lambda a: np.ascontiguousarray(np.asarray(a, dtype=np.float32))
    cst = np.zeros((128, NCST), np.float32)
    cst[:, C_LN1:C_LN1 + 8] = f(ln1_w)[0].reshape(8, 128).T
    cst[:, C_LN2:C_LN2 + 8] = f(ln2_w)[0].reshape(8, 128).T
    cst[:, C_LNF:C_LNF + 8] = f(lnf_w).reshape(8, 128).T
    cst[:, C_BG:C_BG + 2] = f(gla_bg)[0].reshape(2, 128).T
    cst[:, C_SUB] = f(diff_subln_w)[0]
    cst[:, C_GNW] = f(gla_norm_w)[0]
    cst[:, C_CB:C_CB + 44] = f(conv_b)[0].reshape(44, 128).T
    cst[:, C_CW:C_CW + 132] = f(conv_w)[0].T.reshape(44, 128, 3).transpose(1, 0, 2).reshape(128, 132)
    cst[:, C_LQ1:C_LQ1 + 64] = f(diff_lq1)[0][None, :]
    cst[:, C_LK1:C_LK1 + 64] = f(diff_lk1)[0][None, :]
    cst[:, C_LQ2:C_LQ2 + 64] = f(diff_lq2)[0][None, :]
    cst[:, C_LK2:C_LK2 + 64] = f(diff_lk2)[0][None, :]
    shared = {"w_in": f(w_in)[0], "w_out": f(w_out)[0], "w_up": f(w_up)[0], "w_down": f(w_down)[0],
              "wg2": f(gla_wg2)[0], "cst": cst}
    xf = f(x)
    maps = []
    for b in range(xf.shape[0]):
        xt = np.ascontiguousarray(xf[b].T.reshape(KC, 128, S).transpose(1, 0, 2))
        m = dict(shared)
        m["xT"] = xt
        maps.append(m)
    return maps


_NC_CACHE = {}


def kernel(**inputs):
    maps = make_inputs(**inputs)
    if "nc" not in _NC_CACHE:
        _NC_CACHE["nc"] = build_nc()
    nc = _NC_CACHE["nc"]
    n = len(maps)
    res = run_bass_kernel_spmd(nc, maps, core_ids=list(range(n)))
    out = np.empty((n, S, D), np.float32)
    for b in range(n):
        o = res.results[b]["outT"]
        out[b] = o.transpose(1, 0, 2).reshape(D, S).T
    return out
```

```python
import numpy as np
import concourse.bass as bass
import concourse.mybir as mybir
from concourse.bass_utils import run_bass_kernel_spmd

F32 = mybir.dt.float32
BF16 = mybir.dt.bfloat16
AF = mybir.ActivationFunctionType
ALU = mybir.AluOpType

S = 2048
D = 1024
KC = 8
DFF = 2816
NFF = 22
EPS = 1e-6
SAME_ENGINE_SYNC = True

C_LN1, C_LN2, C_LNF, C_BG, C_SUB, C_GNW, C_CB, C_CW = 0, 8, 16, 24, 26, 27, 28, 72
C_LQ1, C_LK1, C_LQ2, C_LK2 = 204, 268, 332, 396
NCST = 460


class Op:
    __slots__ = ("id", "eng", "fn", "deps", "dma", "stream", "signal", "tick", "waits")

    def __init__(self, id, eng, fn, dma):
        self.id, self.eng, self.fn, self.dma = id, eng, fn, dma
        self.stream = ("dma:" + dma) if dma else eng
        self.deps = set()
        self.signal = False
        self.tick = 0
        self.waits = {}


class Prog:
    ENGS = ("pe", "act", "dve", "pool", "sp")

    def __init__(self):
        self.ops = []
        self.state = {}
        self.desc = {}
        self.last = {}
        self.barrier_op = None

    def _related(self, t):
        out = []
        for i in range(1, len(t) + 1):
            pre = t[:i]
            if pre in self.state:
                out.append(pre)
        for d in self.desc.get(t, ()):
            if d in self.state:
                out.append(d)
        return out

    def _ensure(self, t):
        if t not in self.state:
            self.state[t] = [None, {}]
            for i in range(1, len(t)):
                self.desc.setdefault(t[:i], set()).add(t)

    def op(self, eng, fn, reads=(), writes=(), dma=None, nobar=False):
        o = Op(len(self.ops), eng, fn, dma)
        deps = o.deps
        for t in reads:
            for r in self._related(t):
                w = self.state[r][0]
                if w is not None:
                    deps.add(w)
        for t in writes:
            for r in self._related(t):
                st = self.state[r]
                if st[0] is not None:
                    deps.add(st[0])
                deps.update(st[1].values())
        if self.barrier_op is not None and not nobar:
            deps.add(self.barrier_op)
        for t in reads:
            self._ensure(t)
            self.state[t][1][o.stream] = o.id
        for t in writes:
            for d in list(self.desc.get(t, ())):
                self.state.pop(d, None)
            self.desc.pop(t, None)
            self._ensure(t)
            self.state[t] = [o.id, {}]
        deps.discard(o.id)
        self.ops.append(o)
        self.last[o.stream] = o.id
        return o

    def barrier(self, exempt=()):
        o = self.op("pool", lambda e: e.nop())
        for s, i in self.last.items():
            if s not in exempt and i != o.id:
                o.deps.add(i)
        self.barrier_op = o.id
        return o

    def finalize(self):
        ops = self.ops
        for o in ops:
            for d in list(o.deps):
                do = ops[d]
                if do.stream == o.stream and not do.dma:
                    if o.eng == "pe" or o.eng == "sp" or not SAME_ENGINE_SYNC:
                        o.deps.discard(d)
                        continue
                do.signal = True
        cnt = {}
        for o in ops:
            if o.dma:
                o.signal = True
            if o.signal:
                inc = 16 if o.dma else 1
                cnt[o.stream] = cnt.get(o.stream, 0) + inc
                o.tick = cnt[o.stream]
        for s, v in cnt.items():
            assert v < 30000, (s, v)
        self.streams = sorted(cnt.keys())
        clock = {e: {} for e in self.ENGS}
        snap = {}
        for o in ops:
            need = {}
            for d in o.deps:
                do = ops[d]
                if need.get(do.stream, 0) < do.tick:
                    need[do.stream] = do.tick
            ck = clock[o.eng]
            for d in o.deps:
                do = ops[d]
                if ck.get(do.stream, 0) >= do.tick:
                    for k, v in snap[d].items():
                        if ck.get(k, 0) < v:
                            ck[k] = v
            waits = {}
            for k, v in need.items():
                if ck.get(k, 0) < v:
                    waits[k] = v
            for k, v in waits.items():
                ck[k] = v
            for d in o.deps:
                for k, v in snap[d].items():
                    if ck.get(k, 0) < v:
                        ck[k] = v
            o.waits = waits
            if o.signal:
                sn = dict(ck)
                sn[o.stream] = o.tick
                snap[o.id] = sn

    def emit(self, nc, block, sems):
        by_eng = {e: [o for o in self.ops if o.eng == e] for e in self.ENGS}

        def run(eng_name, e):
            for o in by_eng[eng_name]:
                for k, v in o.waits.items():
                    e.wait_ge(sems[k], v)
                inst = o.fn(e)
                if o.signal:
                    inst.then_inc(sems[o.stream], 16 if o.dma else 1)

        @block.tensor
        def _(e):
            run("pe", e)

        @block.scalar
        def _(e):
            run("act", e)

        @block.vector
        def _(e):
            run("dve", e)

        @block.gpsimd
        def _(e):
            run("pool", e)

        @block.sync
        def _(e):
            run("sp", e)


class Arena:
    def __init__(self, big, nbytes):
        self.big = big
        self.free = [(0, nbytes)]
        self.used = {}

    def alloc(self, name, nbytes, top=False):
        nbytes = (nbytes + 63) // 64 * 64
        if top:
            for i in range(len(self.free) - 1, -1, -1):
                o, n = self.free[i]
                if n >= nbytes:
                    self.free[i] = (o, n - nbytes)
                    self.used[name] = (o + n - nbytes, nbytes)
                    return o + n - nbytes
        for i, (o, n) in enumerate(self.free):
            if n >= nbytes:
                self.free[i] = (o + nbytes, n - nbytes)
                self.used[name] = (o, nbytes)
                return o
        raise RuntimeError(f"arena out of space for {name} {nbytes}: {self.free}")

    def release(self, *names):
        for name in names:
            o, n = self.used.pop(name)
            self.free.append((o, n))
        self.free.sort()
        m = []
        for o, n in self.free:
            if n == 0:
                continue
            if m and m[-1][0] + m[-1][1] == o:
                m[-1] = (m[-1][0], m[-1][1] + n)
            else:
                m.append((o, n))
        self.free = m

    def view(self, name, dtype, shape, parts=128, top=False):
        esz = 4 if dtype == F32 else 2
        n = int(np.prod(shape))
        o = self.alloc(name, n * esz, top=top)
        ap = self.big[0:parts, o // 4:(o + n * esz) // 4]
        if dtype != F32:
            ap = ap.bitcast(dtype)
        if len(shape) == 2:
            ap = ap.rearrange("p (a b) -> p a b", a=shape[0])
        elif len(shape) == 3:
            ap = ap.rearrange("p (a b c) -> p a b c", a=shape[0], b=shape[1])
        return ap


def build_nc(stop_after=None, debug=()):
    nc = bass.Bass("TRN2", target_bir_lowering=False)
    xT = nc.dram_tensor("xT", [128, KC, S], F32, kind="ExternalInput").ap()
    w_in = nc.dram_tensor("w_in", [D, 3088], F32, kind="ExternalInput").ap()
    w_out = nc.dram_tensor("w_out", [D, D], F32, kind="ExternalInput").ap()
    w_up = nc.dram_tensor("w_up", [D, 2 * DFF], F32, kind="ExternalInput").ap()
    w_down = nc.dram_tensor("w_down", [DFF, D], F32, kind="ExternalInput").ap()
    wg2 = nc.dram_tensor("wg2", [16, 256], F32, kind="ExternalInput").ap()
    cst_d = nc.dram_tensor("cst", [128, NCST], F32, kind="ExternalInput").ap()
    outT = nc.dram_tensor("outT", [128, KC, S], F32, kind="ExternalOutput").ap()
    dbg_out = {}
    for name, shape, dt in debug:
        dbg_out[name] = nc.dram_tensor("dbg_" + name, list(shape), dt, kind="ExternalOutput").ap()

    p = Prog()
    ARENA_BYTES = 207 * 1024
    import contextlib
    es = contextlib.ExitStack()
    with es:
        big = es.enter_context(nc.sbuf_tensor("big", [128, ARENA_BYTES // 4], F32))
        ps = es.enter_context(nc.psum_tensor("ps", [128, 8, 512], F32))
        A = Arena(big, ARENA_BYTES)

        def bank(b):
            return ps[:, b, :]

        PS = lambda b: ("ps", b)

        cst = A.view("cst", F32, [NCST])
        ones_bf = A.view("ones", BF16, [128])
        ident = A.view("ident", BF16, [128])
        maskT = A.view("maskT", BF16, [128])
        mask2 = A.view("mask2", BF16, [128])
        misc = A.view("misc", F32, [16])
        wg2_sb = A.view("wg2", F32, [256])
        wslot_bytes = 8192
        wsl = [A.alloc(f"wslot{i}", wslot_bytes) for i in range(3)]
        wbase = wsl[0]
        assert wsl[1] == wbase + wslot_bytes and wsl[2] == wbase + 2 * wslot_bytes

        def wview(byte_off, shape, dtype=BF16):
            n = int(np.prod(shape))
            esz = 2
            o = wbase + byte_off
            ap = big[:, o // 4:(o + n * esz) // 4].bitcast(dtype)
            if len(shape) == 2:
                ap = ap.rearrange("p (a b) -> p a b", a=shape[0])
            elif len(shape) == 3:
                ap = ap.rearrange("p (a b c) -> p a b c", a=shape[0], b=shape[1])
            return ap

        p.op("sp", lambda e: e.dma_start(out=cst, in_=cst_d), writes=[("cst",)], dma="cst")
        p.op("sp", lambda e: e.dma_start(out=wg2_sb[0:16, :], in_=wg2), writes=[("wg2",)], dma="wg2")
        p.op("pool", lambda e: e.memset(ones_bf, 1.0), writes=[("ones",)])
        p.op("pool", lambda e: e.affine_select(out=maskT, in_=ones_bf, pattern=[[1, 128]], compare_op=ALU.is_ge,
                                               fill=0.0, base=0, channel_multiplier=-1),
             reads=[("ones",)], writes=[("maskT",)])
        p.op("pool", lambda e: e.affine_select(out=ident, in_=ones_bf, pattern=[[1, 128]], compare_op=ALU.is_equal,
                                               fill=0.0, base=0, channel_multiplier=-1),
             reads=[("ones",)], writes=[("ident",)])
        p.op("pool", lambda e: e.tensor_copy(out=mask2, in_=maskT), reads=[("maskT",)], writes=[("mask2",)])
        p.op("pool", lambda e: e.memset(mask2[0:64, 64:128], 0.0), writes=[("mask2",)])
        neg_lam = misc[:, 0:1]
        w8 = misc[:, 1:2]
        lt = A.view("lamtmp", F32, [2, 64])
        p.op("dve", lambda e: e.tensor_tensor(out=lt[:, 0, :], in0=cst[:, C_LQ1:C_LQ1 + 64], in1=cst[:, C_LK1:C_LK1 + 64], op=ALU.mult),
             reads=[("cst",)], writes=[("lt", 0)])
        p.op("dve", lambda e: e.tensor_tensor(out=lt[:, 1, :], in0=cst[:, C_LQ2:C_LQ2 + 64], in1=cst[:, C_LK2:C_LK2 + 64], op=ALU.mult),
             reads=[("cst",)], writes=[("lt", 1)])
        p.op("dve", lambda e: e.reduce_sum(out=misc[:, 2:4], in_=lt, axis=mybir.AxisListType.X), reads=[("lt",)], writes=[("misc", "s")])
        p.op("act", lambda e: e.activation(out=misc[:, 4:6], in_=misc[:, 2:4], func=AF.Exp), reads=[("misc", "s")], writes=[("misc", "e")])
        p.op("dve", lambda e: e.tensor_tensor(out=misc[:, 6:7], in0=misc[:, 5:6], in1=misc[:, 4:5], op=ALU.subtract),
             reads=[("misc", "e")], writes=[("misc", "d")])
        p.op("dve", lambda e: e.tensor_scalar(out=neg_lam, in0=misc[:, 6:7], scalar1=-0.2, scalar2=None, op0=ALU.add),
             reads=[("misc", "d")], writes=[("misc", "nl")])
        p.op("dve", lambda e: e.tensor_scalar(out=w8, in0=cst[:, C_SUB:C_SUB + 1], scalar1=0.8, scalar2=None, op0=ALU.mult),
             reads=[("cst",)], writes=[("misc", "w8")])

        def wdma(dst, src, tok, key):
            toks = [("w", 0), ("w", 1)] if tok is None else [tok]
            p.op("pool", lambda e: e.dma_start(out=dst, in_=src), writes=toks, dma=key)

        def w_in_cols(c0, c1):
            return w_in[:, c0:c1].rearrange("(kc p) n -> p kc n", p=128)

        def rstd_from_ss(ss_ps_ap, out_ap, n, reads, writes):
            p.op("act", lambda e: e.activation(out=out_ap, in_=ss_ps_ap, func=AF.Ln, scale=1.0 / n, bias=eps_ap),
                 reads=reads + [("misc", "eps")], writes=writes)
            p.op("act", lambda e: e.activation(out=out_ap, in_=out_ap, func=AF.Exp, scale=-0.5),
                 reads=writes, writes=writes)

        eps_ap = misc[:, 8:9]
        one_ap = misc[:, 9:10]
        p.op("pool", lambda e: e.memset(eps_ap, EPS), writes=[("misc", "eps")])
        p.op("pool", lambda e: e.memset(one_ap, 1.0), writes=[("misc", "one")])

        def dbg(name, ap, reads):
            if name in dbg_out:
                p.op("sp", lambda e: e.dma_start(out=dbg_out[name], in_=ap), reads=reads, writes=[("dbg", name)], dma="dbg_" + name)

        wA = [wview(i * wslot_bytes, [KC, 512]) for i in range(3)]
        w_r = A.view("w_r", BF16, [KC, 16])
        wdma(w_r, w_in_cols(2560, 2576), ("w_r",), "wr")
        wdma(wA[1], w_in_cols(2048, 2560), ("w", 1), "w1")
        gv_tok = A.view("gv_tok", BF16, [16, 512])
        grT = A.view("grT", F32, [S])
        v_tok = A.view("v_tok", BF16, [16, 512], top=True)
        psrot = [0]

        def tm_group(tkb, wtile, tok_w, dst, dst_name, evac, bank_base):
            b = bank_base + psrot[0] % 4
            psrot[0] += 1
            tsl = slice(tkb * 128, (tkb + 1) * 128)
            for kc in range(KC):
                p.op("pe", lambda e, kc=kc: e.matmul(bank(b), lhsT=hT[:, kc, tsl], rhs=wtile[:, kc, :], start=(kc == 0), stop=(kc == KC - 1)),
                     reads=[tok_w, ("hT", tkb // 4)], writes=[PS(b)])
            if evac == "act":
                p.op("act", lambda e: e.activation(out=dst[:, tkb, :], in_=bank(b), func=AF.Copy), reads=[PS(b)], writes=[(dst_name, tkb)])
            else:
                p.op("dve", lambda e: e.tensor_copy(out=dst[:, tkb, :], in_=bank(b)), reads=[PS(b)], writes=[(dst_name, tkb)])
        hT = A.view("hT", BF16, [KC, S])
        xs = [A.view(f"xs{i}", F32, [KC, 512]) for i in range(4)]
        sq1 = A.view("sq1", BF16, [KC, 512])
        rs1 = A.view("rs1", F32, [512])
        def a1_proj(tb):
            tsl = slice(tb * 512, (tb + 1) * 512)
            for kc in range(KC):
                p.op("pe", lambda e, kc=kc: e.matmul(bank(2)[0:16, :], lhsT=w_r[:, kc, 0:16], rhs=hT[:, kc, tsl], start=(kc == 0), stop=(kc == KC - 1)),
                     reads=[("w_r",), ("hT", tb)], writes=[PS(2)])
            p.op("act", lambda e: e.activation(out=grT[0:16, tb * 512:(tb + 1) * 512], in_=bank(2)[0:16, :], func=AF.Copy),
                 reads=[PS(2)], writes=[("grT", tb)])
            for i in range(4):
                tm_group(4 * tb + i, wA[1], ("w", 1), gv_tok, "gv_tok", "dve", 4)

        for tb in range(4):
            p.op("sp", lambda e, tb=tb: e.dma_start(out=xs[tb], in_=xT[:, :, tb * 512:(tb + 1) * 512]), writes=[("xs", tb)], dma=f"xs{tb}")
        for tb in range(4):
            sl = tb
            tsl = slice(tb * 512, (tb + 1) * 512)
            p.op("act", lambda e, sl=sl: e.activation(out=sq1, in_=xs[sl], func=AF.Square), reads=[("xs", sl)], writes=[("sq1",)])
            b = tb % 2
            for kc in range(KC):
                p.op("pe", lambda e, kc=kc, b=b: e.matmul(bank(b), lhsT=ones_bf, rhs=sq1[:, kc, :], start=(kc == 0), stop=(kc == KC - 1)),
                     reads=[("ones",), ("sq1",)], writes=[PS(b)])
            rstd_from_ss(bank(b), rs1, D, [PS(b)], [("rs1",)])
            for kc in range(KC):
                eng = "dve" if kc % 2 == 0 else "dve"
                p.op(eng, lambda e, kc=kc, sl=sl, tsl=tsl: e.scalar_tensor_tensor(
                    out=hT[:, kc, tsl], in0=xs[sl][:, kc, :], scalar=cst[:, C_LN1 + kc:C_LN1 + kc + 1], in1=rs1,
                    op0=ALU.mult, op1=ALU.mult),
                    reads=[("xs", sl), ("rs1",), ("cst",)], writes=[("hT", tb, kc)])
            if tb >= 1:
                a1_proj(tb - 1)
            if tb == 1:
                p.op("pool", lambda e: e.dma_start(out=wA[2], in_=w_in_cols(1024, 1536)), reads=[("xs", 1)], writes=[("w", 2)], dma="w2")
                p.op("pool", lambda e: e.dma_start(out=wA[0], in_=w_in_cols(1536, 2048)), reads=[("xs", 1)], writes=[("w", 0)], dma="w0")
        a1_proj(3)
        dbg("hT", hT, [("hT",)])
        p.barrier()
        A.release("xs0", "xs1", "xs2", "xs3", "sq1", "rs1", "lamtmp")
        if stop_after == "A1":
            return finish(nc, p, es, None)


        def proj_fm(wtile, c0, ncols, tok_w, consume):
            for tb in range(4):
                b = psrot[0] % 4
                psrot[0] += 1
                tsl = slice(tb * 512, (tb + 1) * 512)
                for kc in range(KC):
                    p.op("pe", lambda e, kc=kc, b=b, tsl=tsl: e.matmul(bank(b)[0:ncols, :], lhsT=wtile[:, kc, c0:c0 + ncols], rhs=hT[:, kc, tsl],
                                                                      start=(kc == 0), stop=(kc == KC - 1)),
                         reads=[tok_w, ("hT", tb)], writes=[PS(b)], nobar=True)
                consume(tb, bank(b)[0:ncols, :], b)

        def proj_tm(wtile, tok_w, consume):
            for tkb in range(16):
                b = psrot[0] % 4
                psrot[0] += 1
                tsl = slice(tkb * 128, (tkb + 1) * 128)
                for kc in range(KC):
                    p.op("pe", lambda e, kc=kc, b=b, tsl=tsl: e.matmul(bank(b), lhsT=hT[:, kc, tsl], rhs=wtile[:, kc, :],
                                                                      start=(kc == 0), stop=(kc == KC - 1)),
                         reads=[tok_w, ("hT", tkb // 4)], writes=[PS(b)])
                consume(tkb, bank(b), b)


        q_decT = A.view("q_decT", BF16, [2, S])
        k_decT = A.view("k_decT", BF16, [2, S])
        kteT = A.view("kteT", BF16, [2, S])
        kte_tok = A.view("kte_tok", BF16, [16, 256])
        sgo = A.view("sgo", BF16, [4, S])
        resetm = A.view("resetm", F32, [S])
        T0 = A.view("T0", F32, [S])
        T1 = A.view("T1", F32, [S])
        T2 = A.view("T2", F32, [S])
        ebT = A.view("ebT", F32, [S])
        enbT = A.view("enbT", F32, [S])
        ek2T = A.view("ek2T", F32, [S])
        ebl = A.view("ebl", F32, [2, 32])

        p.op("pool", lambda e: e.memset(resetm, 1.0), writes=[("resetm",)])
        p.op("pool", lambda e: e.memset(resetm[:, 0:S:64], 0.0), writes=[("resetm",)])


        def go_groups(c):
            out = []
            for c4 in (2 * c, 2 * c + 1):
                for tb in range(4):
                    def g(c4=c4, tb=tb):
                        b = psrot[0] % 4
                        psrot[0] += 1
                        tsl = slice(tb * 512, (tb + 1) * 512)
                        for kc in range(KC):
                            p.op("pe", lambda e, kc=kc: e.matmul(bank(b), lhsT=wA[2][:, kc, c4 * 128:(c4 + 1) * 128], rhs=hT[:, kc, tsl],
                                                                 start=(kc == 0), stop=(kc == KC - 1)),
                                 reads=[("w", 2), ("hT", tb)], writes=[PS(b)])
                        p.op("act", lambda e: e.activation(out=sgo[:, c4, tsl], in_=bank(b), func=AF.Copy), reads=[PS(b)], writes=[("sgo", c4, tb)])
                    out.append(g)
            return out

        fill = []

        def FILL(n=2):
            for _ in range(n):
                if fill:
                    fill.pop(0)()

        for c in range(2):
            if c == 0:
                fill.extend([(lambda tkb=tkb: tm_group(tkb, wA[2], ("w", 2), v_tok, "v_tok", "act", 0)) for tkb in range(16)])
            else:
                fill.extend(go_groups(0) + go_groups(1))
            for tb in range(4):
                p.op("pe", lambda e, tb=tb, c=c: e.matmul(bank(4 + tb), lhsT=wg2_sb[0:16, c * 128:(c + 1) * 128],
                                                         rhs=grT[0:16, tb * 512:(tb + 1) * 512], start=True, stop=True),
                     reads=[("wg2",), ("grT", tb)], writes=[PS(4 + tb)], nobar=True)
            zps = ps[:, 4:8, :]
            T0v = T0.rearrange("p (a b) -> p a b", a=4)
            rd4 = [PS(4), PS(5), PS(6), PS(7)]
            p.op("act", lambda e, c=c: e.activation(out=T0v, in_=zps, func=AF.Identity, bias=cst[:, C_BG + c:C_BG + c + 1], scale=1.0),
                 reads=rd4 + [("cst",)], writes=[("T0",)])
            FILL()
            p.op("act", lambda e: e.activation(out=T1, in_=T0, func=AF.Abs), reads=[("T0",)], writes=[("T1",)])
            FILL()
            p.op("act", lambda e: e.activation(out=T1, in_=T1, func=AF.Exp, scale=-1.0), reads=[("T1",)], writes=[("T1",)])
            FILL()
            p.op("act", lambda e: e.activation(out=T1, in_=T1, func=AF.Ln, bias=one_ap, scale=1.0), reads=[("T1",), ("misc", "one")], writes=[("T1",)])
            FILL()
            p.op("dve", lambda e: e.tensor_scalar(out=T2, in0=T0, scalar1=0.0, scalar2=1.0 / 16.0, op0=ALU.min, op1=ALU.mult),
                 reads=[("T0",)], writes=[("T2",)])
            p.op("dve", lambda e: e.scalar_tensor_tensor(out=T1, in0=T1, scalar=-1.0 / 16.0, in1=T2, op0=ALU.mult, op1=ALU.add),
                 reads=[("T1",), ("T2",)], writes=[("T1",)])
            FILL()
            p.op("dve", lambda e: e.tensor_tensor_scan(out=T0, data0=resetm, data1=T1, initial=0.0, op0=ALU.mult, op1=ALU.add),
                 reads=[("T1",), ("resetm",)], writes=[("T0",)])
            FILL()
            p.op("act", lambda e: e.activation(out=ebT, in_=T0, func=AF.Exp), reads=[("T0",)], writes=[("ebT",)])
            p.op("act", lambda e: e.activation(out=enbT, in_=T0, func=AF.Exp, scale=-1.0), reads=[("T0",)], writes=[("enbT",)])
            FILL()
            bl = T0[:, 63:S:64].unsqueeze(2).broadcast_to([128, 32, 64])
            p.op("dve", lambda e, bl=bl: e.tensor_tensor(out=T2.rearrange("p (a b) -> p a b", a=32), in0=bl,
                                                        in1=T0.rearrange("p (a b) -> p a b", a=32), op=ALU.subtract),
                 reads=[("T0",)], writes=[("T2",)])
            p.op("act", lambda e: e.activation(out=ek2T, in_=T2, func=AF.Exp), reads=[("T2",)], writes=[("ek2T",)])
            p.op("dve", lambda e, c=c: e.tensor_copy(out=ebl[:, c, :], in_=ebT[:, 63:S:64]), reads=[("ebT",)], writes=[("ebl", c)])
            while fill:
                FILL()
            if c == 0:
                wdma(wA[2], w_in_cols(2576, 3088), ("w", 2), "w2")
            if c == 0:
                dbg("bT", T0, [("T0",)])
                dbg("ebT", ebT, [("ebT",)])

            def cons_q(tb, pap, b, c=c):
                tsl = slice(tb * 512, (tb + 1) * 512)
                p.op("dve", lambda e: e.scalar_tensor_tensor(out=q_decT[:, c, tsl], in0=pap, scalar=0.125, in1=ebT[:, tsl],
                                                             op0=ALU.mult, op1=ALU.mult),
                     reads=[PS(b), ("ebT",)], writes=[("q_decT", c, tb)])
            proj_fm(wA[0], c * 128, 128, ("w", 0), cons_q)

            def cons_k(tb, pap, b, c=c):
                tsl = slice(tb * 512, (tb + 1) * 512)
                p.op("dve", lambda e: e.tensor_tensor(out=k_decT[:, c, tsl], in0=pap, in1=enbT[:, tsl], op=ALU.mult),
                     reads=[PS(b), ("enbT",)], writes=[("k_decT", c, tb)])
                p.op("dve", lambda e: e.tensor_tensor(out=kteT[:, c, tsl], in0=pap, in1=ek2T[:, tsl], op=ALU.mult),
                     reads=[PS(b), ("ek2T",)], writes=[("kteT", c, tb)])
            proj_fm(wA[0], 256 + c * 128, 128, ("w", 0), cons_k)

        for c in range(2):
            for g in range(4):
                b = psrot[0] % 4
                psrot[0] += 1
                pst = bank(b)[:, 0:256].bitcast(BF16).rearrange("p (a b) -> p a b", a=4)
                for i in range(4):
                    tkb = g * 4 + i
                    p.op("pe", lambda e, i=i, tkb=tkb, c=c, pst=pst: e.transpose(out=pst[:, i, :], in_=kteT[:, c, tkb * 128:(tkb + 1) * 128], identity=ident),
                         reads=[("kteT", c, tkb // 4), ("ident",)], writes=[PS(b)])
                p.op("act", lambda e, g=g, c=c, pst=pst: e.activation(out=kte_tok[:, g * 4:(g + 1) * 4, c * 128:(c + 1) * 128], in_=pst, func=AF.Copy),
                     reads=[PS(b)], writes=[("kte_tok", c, g)])


        wdma(wA[0], w_in_cols(0, 512), ("w", 0), "w0")
        wdma(wA[1], w_in_cols(512, 1024), ("w", 1), "w1")

        dbg("q_decT", q_decT, [("q_decT",)])
        dbg("k_decT", k_decT, [("k_decT",)])
        dbg("kte_tok", kte_tok, [("kte_tok",)])
        dbg("gv_tok", gv_tok, [("gv_tok",)])
        dbg("sgo", sgo, [("sgo",)])
        dbg("ebl", ebl, [("ebl",)])
        p.barrier(exempt=("dma:w0", "dma:w1", "dma:w2"))
        A.release("kteT", "grT", "resetm", "T0", "T1", "T2", "ebT", "enbT", "ek2T")
        if stop_after == "A2":
            return finish(nc, p, es, None)

        cc = A.view("cc", BF16, [KC, S], top=True)
        Sbf = A.view("Sbf", BF16, [33, 2, 128])
        aT_halves = [A.view("aT_lo", BF16, [8, 4, 128]), A.view("aT_hi", BF16, [8, 4, 128])]

        def aTv(tkb):
            return aT_halves[tkb // 8][:, tkb % 8]
        Sst2 = A.view("Sst", F32, [2, 2, 128])
        e_sq = A.view("e_sq", BF16, [512], top=True)
        e_rs = A.view("e_rs", F32, [512], top=True)
        e_sq1 = A.view("e_t", BF16, [512], top=True)
        e_rs1 = A.view("e_rs1", F32, [512], top=True)

        for tkb in range(16):
            pair = (tkb % 2) * 2
            tsl = slice(tkb * 128, (tkb + 1) * 128)
            for h in range(4):
                c, hh = h // 2, h % 2
                rows = slice(hh * 64, (hh + 1) * 64)
                b = pair + hh
                p.op("pe", lambda e, b=b, h=h, c=c, rows=rows, tsl=tsl: e.matmul(bank(b)[:, c * 128:(c + 1) * 128], lhsT=k_decT[rows, c, tsl],
                                                                                  rhs=q_decT[rows, c, tsl], start=True, stop=True),
                     reads=[("k_decT", c, tkb // 4), ("q_decT", c, tkb // 4)], writes=[PS(b)], nobar=True)
            for hh in range(2):
                b = pair + hh
                p.op("dve", lambda e, b=b, tkb=tkb, hh=hh: e.tensor_tensor(out=aTv(tkb)[:, hh:4:2, :], in0=bank(b)[:, 0:256].rearrange("p (a b) -> p a b", a=2),
                                                                       in1=mask2.unsqueeze(1).broadcast_to([128, 2, 128]), op=ALU.mult),
                     reads=[PS(b), ("mask2",)], writes=[("aT", tkb, hh)])

        for c4 in range(4):
            p.op("act", lambda e, c4=c4: e.activation(out=sgo[:, c4, :], in_=sgo[:, c4, :], func=AF.Silu), reads=[("sgo", c4)], writes=[("sgo", c4)])

        p.op("pool", lambda e: e.memset(Sbf[:, 0, :, :], 0.0), writes=[("Sbf", 0)])
        p.op("pool", lambda e: e.memset(Sst2[:, 0, :, :], 0.0), writes=[("Sst", 0)])
        for tkb in range(16):
            bp = 4 + 2 * (tkb % 2)
            for c in range(2):
                for hh in range(2):
                    h = 2 * c + hh
                    orow = slice(hh * 64, (hh + 1) * 64)
                    for s in range(2):
                        trow = slice(s * 64, (s + 1) * 64)
                        p.op("pe", lambda e, c=c, hh=hh, h=h, trow=trow, orow=orow, s=s, bp=bp, tkb=tkb: e.matmul(
                            bank(bp + s)[orow, c * 128:(c + 1) * 128], lhsT=kte_tok[trow, tkb, c * 128 + hh * 64:c * 128 + hh * 64 + 64],
                            rhs=gv_tok[trow, tkb, h * 128:(h + 1) * 128], start=True, stop=True),
                            reads=[("kte_tok", c, tkb // 4), ("gv_tok", tkb)], writes=[PS(bp + s)])
            for s in range(2):
                n = 2 * tkb + s
                so, sn = n % 2, (n + 1) % 2
                for c in range(2):
                    p.op("dve", lambda e, c=c, n=n, so=so, sn=sn, s=s, bp=bp: e.scalar_tensor_tensor(out=Sst2[:, sn, c, :], in0=Sst2[:, so, c, :], scalar=ebl[:, c, n:n + 1],
                                                                                            in1=bank(bp + s)[:, c * 128:(c + 1) * 128], op0=ALU.mult, op1=ALU.add),
                         reads=[PS(bp + s), ("Sst", so, c), ("ebl", c)], writes=[("Sst", sn, c)])
                p.op("act", lambda e, n=n, sn=sn: e.activation(out=Sbf[:, n + 1, :, :], in_=Sst2[:, sn, :, :], func=AF.Copy), reads=[("Sst", sn)], writes=[("Sbf", n + 1)])

        it = 0
        epi_prev = []

        def a3_epi(tb, h):
            b = h
            tsl = slice(tb * 512, (tb + 1) * 512)
            k = it_box[0] % 2
            it_box[0] += 1
            sq_, rs_ = (e_sq, e_rs) if k == 0 else (e_sq1, e_rs1)
            tq, tr = ("e_sqA", k), ("e_rsA", k)
            bs = 6 + k

            def part1():
                p.op("act", lambda e: e.activation(out=sq_, in_=bank(b), func=AF.Square), reads=[PS(b)], writes=[tq])

            def part2():
                p.op("pe", lambda e: e.matmul(bank(bs), lhsT=ones_bf, rhs=sq_, start=True, stop=True), reads=[tq, ("ones",)], writes=[PS(bs)])
                rstd_from_ss(bank(bs), rs_, 128, [PS(bs)], [tr])
                p.op("dve", lambda e: e.scalar_tensor_tensor(out=rs_, in0=bank(b), scalar=cst[:, C_GNW:C_GNW + 1], in1=rs_, op0=ALU.mult, op1=ALU.mult),
                     reads=[PS(b), tr, ("cst",)], writes=[tr])
                p.op("dve", lambda e: e.tensor_tensor(out=cc[:, 4 + h, tsl], in0=rs_, in1=sgo[:, h, tsl], op=ALU.mult),
                     reads=[tr, ("sgo", h, tb)], writes=[("cc", 4 + h, tb)])
            return part1, part2

        it_box = [0]
        carry = None
        for tb in range(4):
            for c in range(2):
                hs = (2 * c, 2 * c + 1)
                for i in range(4):
                    tkb = tb * 4 + i
                    osl = slice(i * 128, (i + 1) * 128)
                    for h in hs:
                        p.op("pe", lambda e, h=h, osl=osl, tkb=tkb: e.matmul(bank(h)[:, osl], lhsT=gv_tok[:, tkb, h * 128:(h + 1) * 128], rhs=aTv(tkb)[:, h, :], start=True, stop=False),
                             reads=[("gv_tok", tkb), ("aT", tkb)], writes=[PS(h)])
                    for s in range(2):
                        n = 2 * tkb + s
                        o2 = slice(i * 128 + s * 64, i * 128 + s * 64 + 64)
                        t2 = slice(tkb * 128 + s * 64, tkb * 128 + s * 64 + 64)
                        for h in hs:
                            rows = slice((h % 2) * 64, (h % 2) * 64 + 64)
                            p.op("pe", lambda e, h=h, rows=rows, s=s, n=n, o2=o2, t2=t2, c=c: e.matmul(bank(h)[:, o2], lhsT=Sbf[rows, n, c, :], rhs=q_decT[rows, c, t2],
                                                                                              start=False, stop=(s == 1)),
                                 reads=[("Sbf", n), ("q_decT", c, tkb // 4)], writes=[PS(h)])
                for h in hs:
                    p1, p2 = a3_epi(tb, h)
                    p1()
                    if carry is not None:
                        carry()
                    carry = p2
        if carry is not None:
            carry()
        p.barrier(exempt=("dma:w0", "dma:w1", "dma:w2"))
        A.release("aT_lo", "aT_hi", "Sbf", "Sst", "q_decT", "k_decT", "kte_tok", "gv_tok", "sgo", "ebl")
        if stop_after == "A3":
            return finish(nc, p, es, (cc, "cc"), dbg_out)

        qT = A.view("qT", BF16, [4, S])
        kT = A.view("kT", BF16, [4, S])
        flip = [0]

        def cons_qk(dst, name, h):
            def cons(tb, pap, b):
                flip[0] ^= 1
                if flip[0]:
                    p.op("act", lambda e: e.activation(out=dst[:, h, tb * 512:(tb + 1) * 512], in_=pap, func=AF.Copy), reads=[PS(b)], writes=[(name, h, tb)])
                else:
                    p.op("dve", lambda e: e.tensor_copy(out=dst[:, h, tb * 512:(tb + 1) * 512], in_=pap), reads=[PS(b)], writes=[(name, h, tb)])
            return cons

        def cons_dv(tkb, pap, b):
            flip[0] ^= 1
            if flip[0]:
                p.op("act", lambda e: e.activation(out=v_tok[:, tkb, :], in_=pap, func=AF.Copy), reads=[PS(b)], writes=[("v_tok", tkb)])
            else:
                p.op("dve", lambda e: e.tensor_copy(out=v_tok[:, tkb, :], in_=pap), reads=[PS(b)], writes=[("v_tok", tkb)])
        for h in range(4):
            proj_fm(wA[0], h * 128, 128, ("w", 0), cons_qk(qT, "qT", h))
            proj_fm(wA[1], h * 128, 128, ("w", 1), cons_qk(kT, "kT", h))
        w_o = wview(0, [KC, D])
        wdma(w_o, w_out.rearrange("(kc p) n -> p kc n", p=128), None, "w0")

        def filler_ops(hn):
            out = []
            for wi, dst, name in ((0, qT, "qT"), (1, kT, "kT")):
                for tb in range(4):
                    tsl = slice(tb * 512, (tb + 1) * 512)
                    for kc in range(KC):
                        def mo(wi=wi, dst=dst, name=name, tb=tb, tsl=tsl, kc=kc):
                            p.op("pe", lambda e: e.matmul(bank(7), lhsT=wA[wi][:, kc, hn * 128:(hn + 1) * 128], rhs=hT[:, kc, tsl],
                                                          start=(kc == 0), stop=(kc == KC - 1)),
                                 reads=[("w", wi), ("hT", tb)], writes=[PS(7)])
                            if kc == KC - 1:
                                p.op("dve", lambda e: e.tensor_copy(out=dst[:, hn, tsl], in_=bank(7)), reads=[PS(7)], writes=[(name, hn, tb)])
                        out.append(mo)
            return out

        NP = 4
        p_sb = A.view("p_sb", BF16, [NP, 2, 512])
        a_r = A.view("a_r", F32, [2, 512])
        a_o = A.view("a_o", F32, [512])
        xsB = [A.view(f"xs{i}", F32, [KC, 512], top=True) for i in range(2)]
        o_s = A.view("o_s", F32, [2, 512])
        item = [0]
        pending = []

        def flush_one():
            if pending:
                pending.pop(0)()

        blocks = [(h, qb) for h in range(4) for qb in range(4)]
        steps = [(bi, kb) for bi, (h, qb) in enumerate(blocks) for kb in range(4 * (qb + 1))]

        def emit_S(step):
            bi, kb = step
            h, qb = blocks[bi]
            qs = qb * 512
            j = kb - 4 * qb
            q0 = 128 * j if j > 0 else 0
            sl_ = item[0] % NP
            sbk = 2 * (item[0] % 2)
            item[0] += 1
            for c in range(2):
                rows = slice(c * 64, (c + 1) * 64)
                p.op("pe", lambda e, c=c, rows=rows: e.matmul(bank(sbk + c)[:, q0:512], lhsT=kT[rows, h, kb * 128:(kb + 1) * 128],
                                                              rhs=qT[rows, h, qs + q0:qs + 512], start=True, stop=True),
                     reads=[("kT", h, kb // 4), ("qT", h, qb)], writes=[PS(sbk + c)])
            for c in range(2):
                p.op("act", lambda e, c=c: e.activation(out=p_sb[:, sl_, c, q0:512], in_=bank(sbk + c)[:, q0:512], func=AF.Exp, scale=0.125),
                     reads=[PS(sbk + c)], writes=[("p_sb", sl_, c)])
                if j >= 0:
                    d0 = 128 * j
                    p.op("dve", lambda e, c=c, d0=d0: e.tensor_tensor(out=p_sb[:, sl_, c, d0:d0 + 128], in0=p_sb[:, sl_, c, d0:d0 + 128], in1=maskT, op=ALU.mult),
                         reads=[("p_sb", sl_, c), ("maskT",)], writes=[("p_sb", sl_, c)])
            return (sl_, q0)

        def emit_AV(step, res):
            bi, kb = step
            h, qb = blocks[bi]
            nkb = 4 * (qb + 1)
            sl_, q0 = res
            st, sp_ = (kb == 0), (kb == nkb - 1)
            for c in range(2):
                p.op("pe", lambda e, c=c: e.matmul(bank(4 + c)[:, q0:512], lhsT=v_tok[:, kb, h * 128:(h + 1) * 128], rhs=p_sb[:, sl_, c, q0:512], start=st, stop=sp_),
                     reads=[("v_tok", kb), ("p_sb", sl_, c)], writes=[PS(4 + c)])
                p.op("pe", lambda e, c=c: e.matmul(bank(6 + c)[:, q0:512], lhsT=ones_bf, rhs=p_sb[:, sl_, c, q0:512], start=st, stop=sp_),
                     reads=[("ones",), ("p_sb", sl_, c)], writes=[PS(6 + c)])

        def make_epilogue(bi):
            h, qb = blocks[bi]
            tsl = slice(qb * 512, qb * 512 + 512)

            def s2():
                p.op("act", lambda e: e.activation(out=a_r, in_=a_r, func=AF.Exp, scale=-1.0), reads=[("a_r",)], writes=[("a_r",)])

            def s3():
                p.op("dve", lambda e: e.tensor_tensor(out=o_s, in0=o_s, in1=a_r, op=ALU.mult), reads=[("o_s",), ("a_r",)], writes=[("o_s",)])

            def s4():
                p.op("dve", lambda e: e.scalar_tensor_tensor(out=a_o, in0=o_s[:, 1, :], scalar=neg_lam, in1=o_s[:, 0, :], op0=ALU.mult, op1=ALU.add),
                     reads=[("o_s",), ("misc", "nl")], writes=[("a_o",)])
                p.op("dve", lambda e: e.tensor_tensor(out=e_sq, in0=a_o, in1=a_o, op=ALU.mult), reads=[("a_o",)], writes=[("e_sq",)])

            def s5():
                b_ = 2 * ((item[0] + 1) % 2)
                p.op("pe", lambda e: e.matmul(bank(b_), lhsT=ones_bf, rhs=e_sq, start=True, stop=True), reads=[("e_sq",), ("ones",)], writes=[PS(b_)])
                rstd_from_ss(bank(b_), e_rs, 128, [PS(b_)], [("e_rs",)])

            def s7():
                p.op("dve", lambda e: e.scalar_tensor_tensor(out=cc[:, h, tsl], in0=a_o, scalar=w8, in1=e_rs, op0=ALU.mult, op1=ALU.mult),
                     reads=[("a_o",), ("e_rs",), ("misc", "w8")], writes=[("cc", h, qb)])
            return [s2, s3, s4, s5, s7]

        prev = emit_S(steps[0])
        for si, step in enumerate(steps):
            bi, kb = step
            nkb = 4 * (blocks[bi][1] + 1)
            nxt = emit_S(steps[si + 1]) if si + 1 < len(steps) else None
            emit_AV(step, prev)
            prev = nxt
            for _ in range(2 if nkb == 4 else 1):
                flush_one()
            if kb == nkb - 1:
                while pending:
                    flush_one()
                for c in range(2):
                    p.op("dve", lambda e, c=c: e.tensor_copy(out=o_s[:, c, :], in_=bank(4 + c)), reads=[PS(4 + c)], writes=[("o_s", c)])
                    p.op("act", lambda e, c=c: e.activation(out=a_r[:, c, :], in_=bank(6 + c), func=AF.Ln), reads=[PS(6 + c)], writes=[("a_r", c)])
                pending = make_epilogue(bi)
        for tb_ in range(2):
            p.op("sp", lambda e, tb_=tb_: e.dma_start(out=xsB[tb_], in_=xT[:, :, tb_ * 512:(tb_ + 1) * 512]), writes=[("xs", tb_)], dma=f"xs{tb_}")
        p.barrier(exempt=("dma:w0", "dma:w1", "dma:w2", "dma:xs0", "dma:xs1"))
        A.release("qT", "kT", "v_tok", "p_sb", "e_t", "e_rs1", "hT")
        if stop_after == "A5":
            return finish(nc, p, es, (cc, "cc"), dbg_out)

        dbg("w_o", w_o, [("w",)])
        dbg("cc", cc, [("cc",)])
        x1T = A.view("x1T", F32, [KC, S])
        rstd2 = A.view("rstd2", F32, [S])
        sqB = A.view("sq1", BF16, [KC, 512])
        h2 = A.view("h2", BF16, [KC, 1024])
        wu = [wview(16384, [KC, 2, 128]), wview(20480, [KC, 2, 128]), wview(0, [KC, 2, 128])]
        wd = [wview(4096 + i * 5632, [NFF, 128]) for i in range(2)]

        def load_wu(ffc, slot):
            for g in range(2):
                c0 = g * DFF + ffc * 128
                wdma(wu[slot][:, :, g, :], w_up[:, c0:c0 + 128].rearrange("(kc p) n -> p kc n", p=128), ("wu", slot, g), f"wu{slot}{g}")

        def load_wd(dmc, slot):
            wdma(wd[slot], w_down[:, dmc * 128:(dmc + 1) * 128].rearrange("(f p) n -> p f n", p=128), ("wd", slot), f"wd{slot}")

        load_wu(0, 0)
        load_wu(1, 1)

        def h2_one(th, kc, tb2):
            t0 = th * 1024
            tsl = slice(t0 + tb2 * 512, t0 + (tb2 + 1) * 512)
            p.op("dve", lambda e: e.scalar_tensor_tensor(
                out=h2[:, kc, tb2 * 512:(tb2 + 1) * 512], in0=x1T[:, kc, tsl], scalar=cst[:, C_LN2 + kc:C_LN2 + kc + 1], in1=rstd2[:, tsl],
                op0=ALU.mult, op1=ALU.mult),
                reads=[("x1T", 2 * th + tb2, kc), ("rstd2", 2 * th + tb2), ("cst",)], writes=[("h2", kc, tb2)])

        def h2_compute(th):
            for kc in range(KC):
                for tb2 in range(2):
                    h2_one(th, kc, tb2)

        def b_proj(tb, extra=None):
            sl = tb % 2
            tsl = slice(tb * 512, (tb + 1) * 512)
            if tb >= 2:
                p.op("sp", lambda e: e.dma_start(out=xsB[sl], in_=xT[:, :, tsl]), writes=[("xs", sl)], dma=f"xs{sl}")
            for dmc in range(KC):
                b = psrot[0] % 4
                psrot[0] += 1
                for kc in range(KC):
                    p.op("pe", lambda e, b=b, kc=kc, dmc=dmc: e.matmul(bank(b), lhsT=w_o[:, kc, dmc * 128:(dmc + 1) * 128], rhs=cc[:, kc, tsl],
                                                                     start=(kc == 0), stop=(kc == KC - 1)),
                         reads=[("w", dmc // 4), ("cc", kc, tb)], writes=[PS(b)], nobar=True)
                p.op("dve", lambda e, b=b, dmc=dmc: e.tensor_tensor(out=x1T[:, dmc, tsl], in0=bank(b), in1=xsB[sl][:, dmc, :], op=ALU.add),
                     reads=[PS(b), ("xs", sl)], writes=[("x1T", tb, dmc)])
                if extra is not None:
                    extra(dmc)

        def b_stats_sq(tb):
            tsl = slice(tb * 512, (tb + 1) * 512)
            p.op("act", lambda e: e.activation(out=sqB, in_=x1T[:, :, tsl], func=AF.Square), reads=[("x1T", tb)], writes=[("sq1",)])

        def b_stats_mm(tb, bs=None):
            tsl = slice(tb * 512, (tb + 1) * 512)
            if bs is None:
                bs = 4 + tb % 2
            for kc in range(KC):
                p.op("pe", lambda e, kc=kc: e.matmul(bank(bs), lhsT=ones_bf, rhs=sqB[:, kc, :], start=(kc == 0), stop=(kc == KC - 1)),
                     reads=[("ones",), ("sq1",)], writes=[PS(bs)])
            rstd_from_ss(bank(bs), rstd2[:, tsl], D, [PS(bs)], [("rstd2", tb)])

        def b_stats(tb):
            b_stats_sq(tb)
            b_stats_mm(tb)

        b_proj(0, extra=lambda dmc: flush_one())
        while pending:
            flush_one()
        b_proj(1)
        b_stats(0)
        b_proj(2, extra=lambda dmc: h2_one(0, dmc, 0))
        b_stats(1)
        b_proj(3, extra=lambda dmc: h2_one(0, dmc, 1))
        p.barrier()
        A.release("cc", "xs0", "xs1", "e_sq", "e_rs", "a_r", "a_o", "o_s")
        if stop_after == "B":
            return finish(nc, p, es, (x1T, "x1T"), dbg_out)

        actT = A.view("actT", BF16, [NFF, 1024])
        uh = A.view("uh", F32, [2 * NFF, 2])
        NCB = 2
        u_sb = [[A.view(f"u_sb{i}{g}", F32, [1026]) for g in range(2)] for i in range(NCB)]
        y_sb = [[A.view(f"y_sb{i}{g}", F32, [1024]) for g in range(2)] for i in range(NCB)]
        for i_ in range(NCB):
            for g_ in range(2):
                p.op("pool", lambda e, us=u_sb[i_][g_]: e.memset(us[:, 0:2], 0.0), writes=[("u_sb", i_, g_, "h")])

        rsf = A.view("rsf", F32, [512])
        fin_pending = []

        def final_stages(tb):
            tsl = slice(tb * 512, (tb + 1) * 512)

            def f1():
                p.op("act", lambda e: e.activation(out=sqB, in_=x1T[:, :, tsl], func=AF.Square), reads=[("x1T", tb)], writes=[("sq1",)])

            def f2():
                bs = psrot[0] % 8
                psrot[0] += 1
                for kc in range(KC):
                    p.op("pe", lambda e, kc=kc: e.matmul(bank(bs), lhsT=ones_bf, rhs=sqB[:, kc, :], start=(kc == 0), stop=(kc == KC - 1)),
                         reads=[("ones",), ("sq1",)], writes=[PS(bs)])
                rstd_from_ss(bank(bs), rsf, D, [PS(bs)], [("rsf",)])

            def f3():
                for kc in range(KC):
                    p.op("dve", lambda e, kc=kc: e.scalar_tensor_tensor(out=x1T[:, kc, tsl], in0=x1T[:, kc, tsl], scalar=cst[:, C_LNF + kc:C_LNF + kc + 1],
                                                                        in1=rsf, op0=ALU.mult, op1=ALU.mult),
                         reads=[("x1T", tb, kc), ("rsf",), ("cst",)], writes=[("x1T", tb, kc)])
                    if kc % 4 == 3:
                        k0 = kc - 3
                        p.op("sp", lambda e, k0=k0: e.dma_start(out=outT[:, k0:k0 + 4, tsl], in_=x1T[:, k0:k0 + 4, tsl]),
                             reads=[("x1T", tb, k0), ("x1T", tb, k0 + 1), ("x1T", tb, k0 + 2), ("x1T", tb, k0 + 3)], writes=[("out", tb, k0)],
                             dma=f"out{tb % 2}{k0 // 4}")
            return [f1, f2, f3]

        pair_it = 0
        for th in range(2):
            t0 = th * 1024
            for ffc in range(NFF):
                slot = pair_it % 3
                cb = pair_it % NCB
                pair_it += 1
                if ffc + 2 < NFF:
                    load_wu(ffc + 2, (slot + 2) % 3)
                if ffc == NFF - 4:
                    load_wd(0, 0)
                    load_wd(1, 1)
                pb = 4 * (ffc % 2)
                for g in range(2):
                    for tb2 in range(2):
                        b = pb + 2 * g + tb2
                        for kc in range(KC):
                            p.op("pe", lambda e, b=b, kc=kc, g=g, tb2=tb2, slot=slot: e.matmul(bank(b), lhsT=wu[slot][:, kc, g, :], rhs=h2[:, kc, tb2 * 512:(tb2 + 1) * 512],
                                                                                             start=(kc == 0), stop=(kc == KC - 1)),
                                 reads=[("wu", slot, g), ("h2",)], writes=[PS(b)], nobar=True)
                for g in range(2):
                    ch = g * NFF + ffc
                    b0 = pb + 2 * g
                    pss = ps[:, b0:b0 + 2, :]
                    rd = [PS(b0), PS(b0 + 1)]
                    us = u_sb[cb][g]
                    ys = y_sb[cb][g]
                    cw = lambda i, ch=ch: cst[:, C_CW + ch * 3 + i:C_CW + ch * 3 + i + 1]
                    p.op("act", lambda e, us=us, pss=pss: e.activation(out=us[:, 2:1026].rearrange("p (a b) -> p a b", a=2), in_=pss, func=AF.Copy),
                         reads=rd, writes=[("u_sb", cb, g, "m")])
                    if th == 0:
                        p.op("act", lambda e, ch=ch, b0=b0: e.activation(out=uh[:, ch, :], in_=bank(b0 + 1)[:, 510:512], func=AF.Copy),
                             reads=[PS(b0 + 1)], writes=[("uh", ch)])
                    else:
                        p.op("dve", lambda e, us=us, ch=ch: e.tensor_copy(out=us[:, 0:2], in_=uh[:, ch, :]), reads=[("uh", ch)], writes=[("u_sb", cb, g, "h")])
                    p.op("act", lambda e, ys=ys, pss=pss, ch=ch, cw=cw: e.activation(out=ys.rearrange("p (a b) -> p a b", a=2), in_=pss, func=AF.Identity,
                                                                                    scale=cw(2), bias=cst[:, C_CB + ch:C_CB + ch + 1]),
                         reads=rd + [("cst",)], writes=[("y_sb", cb, g)])
                    p.op("dve", lambda e, ys=ys, us=us, cw=cw: e.scalar_tensor_tensor(out=ys, in0=us[:, 1:1025], scalar=cw(1), in1=ys, op0=ALU.mult, op1=ALU.add),
                         reads=[("u_sb", cb, g), ("y_sb", cb, g), ("cst",)], writes=[("y_sb", cb, g)])
                    p.op("dve", lambda e, ys=ys, us=us, cw=cw: e.scalar_tensor_tensor(out=ys, in0=us[:, 0:1024], scalar=cw(0), in1=ys, op0=ALU.mult, op1=ALU.add),
                         reads=[("u_sb", cb, g), ("y_sb", cb, g), ("cst",)], writes=[("y_sb", cb, g)])
                yg, yv = y_sb[cb][0], y_sb[cb][1]
                p.op("act", lambda e, yg=yg: e.activation(out=yg, in_=yg, func=AF.Silu), reads=[("y_sb", cb, 0)], writes=[("y_sb", cb, 0)])
                p.op("dve", lambda e, yg=yg, yv=yv, ffc=ffc: e.tensor_tensor(out=actT[:, ffc, :], in0=yg, in1=yv, op=ALU.mult),
                     reads=[("y_sb", cb, 0), ("y_sb", cb, 1)], writes=[("actT", ffc)])
            if th == 0:
                load_wu(0, (pair_it) % 3)
                load_wu(1, (pair_it + 1) % 3)
                b_stats_sq(2)
            def dn_mm(b, f, slot, tb2):
                p.op("pe", lambda e: e.matmul(bank(b), lhsT=wd[slot][:, f, :], rhs=actT[:, f, tb2 * 512:(tb2 + 1) * 512],
                                              start=(f == 0), stop=(f == NFF - 1)),
                     reads=[("wd", slot), ("actT", f)], writes=[PS(b)])

            def dn_evac(b, dmc, tb2):
                tsl = slice(t0 + tb2 * 512, t0 + (tb2 + 1) * 512)
                p.op("dve", lambda e: e.tensor_tensor(out=x1T[:, dmc, tsl], in0=bank(b), in1=x1T[:, dmc, tsl], op=ALU.add),
                     reads=[PS(b), ("x1T", 2 * th + tb2, dmc)], writes=[("x1T", 2 * th + tb2, dmc)])

            NL = 4
            first = []
            for dmc in (0, 1):
                for tb2 in (0, 1):
                    b = psrot[0] % 8
                    psrot[0] += 1
                    first.append((b, dmc, tb2))
                    for f in range(NFF - NL):
                        dn_mm(b, f, dmc % 2, tb2)
            if th == 0:
                bs_ = psrot[0] % 8
                psrot[0] += 1
                b_stats_mm(2, bs_)
                b_stats_sq(3)
            for (b, dmc, tb2) in first:
                for f in range(NFF - NL, NFF):
                    dn_mm(b, f, dmc % 2, tb2)
                dn_evac(b, dmc, tb2)
            if th == 0:
                bs_ = psrot[0] % 8
                psrot[0] += 1
                b_stats_mm(3, bs_)
                h2_compute(1)
            load_wd(2, 0)
            load_wd(3, 1)
            for dmc in range(2, KC):
                slot = dmc % 2
                for tb2 in (0, 1):
                    b = psrot[0] % 8
                    psrot[0] += 1
                    for f in range(NFF):
                        dn_mm(b, f, slot, tb2)
                    dn_evac(b, dmc, tb2)
                if dmc + 2 < KC:
                    load_wd(dmc + 2, slot)
                if fin_pending:
                    fin_pending.pop(0)()
            if th == 0:
                fin_pending.extend(final_stages(0) + final_stages(1))
            else:
                a2, a3 = final_stages(2), final_stages(3)
                for f in (a2[0], a2[1], a3[0], a2[2], a3[1], a3[2]):
                    f()
        return finish(nc, p, es, None, dbg_out)


def finish(nc, p, es, dump, dbg_out=None):
    if dump is not None and dbg_out:
        ap, name = dump
        if name in dbg_out:
            p.op("sp", lambda e: e.dma_start(out=dbg_out[name], in_=ap), reads=[(name,)], writes=[("dbg", name)], dma="dbg_" + name)
    fin = p.op("sp", lambda e: e.nop())
    for s, i in p.last.items():
        if s.startswith("dma:") and i != fin.id:
            fin.deps.add(i)
    p.finalize()
    sems = {}
    for s in p.streams:
        sems[s] = es.enter_context(nc.semaphore("s_" + s.replace(":", "_")))
    block = es.enter_context(nc.Block())
    p.emit(nc, block, sems)
    return nc


def make_inputs(x, ln1_w, w_in, diff_lq1, diff_lk1, diff_lq2, diff_lk2, diff_subln_w, gla_wg2, gla_bg,
                gla_norm_w, w_out, ln2_w, w_up, conv_w, conv_b, w_down, lnf_w):
    f = lambda a: np.ascontiguousarray(np.asarray(a, dtype=np.float32))
    cst = np.zeros((128, NCST), np.float32)
    cst[:, C_LN1:C_LN1 + 8] = f(ln1_w)[0].reshape(8, 128).T
    cst[:, C_LN2:C_LN2 + 8] = f(ln2_w)[0].reshape(8, 128).T
    cst[:, C_LNF:C_LNF + 8] = f(lnf_w).reshape(8, 128).T
    cst[:, C_BG:C_BG + 2] = f(gla_bg)[0].reshape(2, 128).T
    cst[:, C_SUB] = f(diff_subln_w)[0]
    cst[:, C_GNW] = f(gla_norm_w)[0]
    cst[:, C_CB:C_CB + 44] = f(conv_b)[0].reshape(44, 128).T
    cst[:, C_CW:C_CW + 132] = f(conv_w)[0].T.reshape(44, 128, 3).transpose(1, 0, 2).reshape(128, 132)
    cst[:, C_LQ1:C_LQ1 + 64] = f(diff_lq1)[0][None, :]
    cst[:, C_LK1:C_LK1 + 64] = f(diff_lk1)[0][None, :]
    cst[:, C_LQ2:C_LQ2 + 64] = f(diff_lq2)[0][None, :]
    cst[:, C_LK2:C_LK2 + 64] = f(diff_lk2)[0][None, :]
    shared = {"w_in": f(w_in)[0], "w_out": f(w_out)[0], "w_up": f(w_up)[0], "w_down": f(w_down)[0],
              "wg2": f(gla_wg2)[0], "cst": cst}
    xf = f(x)
    maps = []
    for b in range(xf.shape[0]):
        xt = np.ascontiguousarray(xf[b].T.reshape(KC, 128, S).transpose(1, 0, 2))
        m = dict(shared)
        m["xT"] = xt
        maps.append(m)
    return maps


_NC_CACHE = {}


def kernel(**inputs):
    maps = make_inputs(**inputs)
    if "nc" not in _NC_CACHE:
        _NC_CACHE["nc"] = build_nc()
    nc = _NC_CACHE["nc"]
    n = len(maps)
    res = run_bass_kernel_spmd(nc, maps, core_ids=list(range(n)))
    out = np.empty((n, S, D), np.float32)
    for b in range(n):
        o = res.results[b]["outT"]
        out[b] = o.transpose(1, 0, 2).reshape(D, S).T
    return out
```
